# Optimizing a Trainium2 kernel written in Bass

```python
import jax, jax.numpy as jnp
from jax import lax
import numpy as np

D_MODEL = 1024
BATCH = 8
SEQ = 8192
DEPTH = 1

HEAD_DIM = 64
ROT_DIM = HEAD_DIM // 4
ROPE_THETA = 500000.0
BLOCK = 128
A_Q_HEADS = 16
A_KV_HEADS = 4
A_WINDOW = 128
B_GROUPS = ((128, 1), (512, 4), (2048, 16))
B_NG = len(B_GROUPS)
B_HEADS_PER_GROUP = 4
B_HEADS = B_NG * B_HEADS_PER_GROUP
A_Q_W = A_Q_HEADS * HEAD_DIM
A_KV_W = A_KV_HEADS * HEAD_DIM
B_W = B_HEADS * HEAD_DIM
SPLITS = (A_Q_W, A_Q_W + A_KV_W, A_Q_W + 2 * A_KV_W, A_Q_W + 2 * A_KV_W + B_W, A_Q_W + 2 * A_KV_W + 2 * B_W)
IN_W = A_Q_W + 2 * A_KV_W + 3 * B_W
A_OUT_W = A_Q_W
B_OUT_W = B_HEADS_PER_GROUP * HEAD_DIM
N_BRANCH = 2
D_FF = ((8 * D_MODEL // 3 + 255) // 256) * 256
DN_ALPHA = (2 * DEPTH) ** 0.25
DN_BETA = (8 * DEPTH) ** -0.25
LN_EPS = 1e-5
NEG = -1e30

kernel_name = "hybrid_gated_window_dilated_encoder_layer"


def layer_norm(x, g, b):
    xf = x.astype(jnp.float32)
    mu = xf.mean(-1, keepdims=True)
    var = jnp.square(xf - mu).mean(-1, keepdims=True)
    return ((xf - mu) * lax.rsqrt(var + LN_EPS) * g.astype(jnp.float32) + b.astype(jnp.float32)).astype(x.dtype)


def rope_partial(t, pos):
    half = ROT_DIM // 2
    inv = ROPE_THETA ** (-jnp.arange(half, dtype=jnp.float32) / half)
    ang = pos.astype(jnp.float32)[:, None] * inv[None, :]
    cos = jnp.cos(ang)[None, :, None, :].astype(t.dtype)
    sin = jnp.sin(ang)[None, :, None, :].astype(t.dtype)
    t1, t2, tp = t[..., :half], t[..., half:ROT_DIM], t[..., ROT_DIM:]
    return jnp.concatenate([t1 * cos - t2 * sin, t1 * sin + t2 * cos, tp], axis=-1)


def windowed_gqa_sink(q, k, v, sink):
    b, s, hq, dh = q.shape
    hkv = k.shape[2]
    g = hq // hkv
    nb = s // BLOCK
    pad = ((0, 0), (BLOCK, BLOCK), (0, 0), (0, 0))
    kp = jnp.pad(k, pad)
    vp = jnp.pad(v, pad)
    qg = q.reshape(b, s, hkv, g, dh)
    sk = sink.astype(jnp.float32).reshape(1, hkv, g, 1)
    scale = dh ** -0.5

    def block(i):
        start = i * BLOCK
        qb = lax.dynamic_slice_in_dim(qg, start, BLOCK, axis=1)
        kb = lax.dynamic_slice_in_dim(kp, start, 3 * BLOCK, axis=1)
        vb = lax.dynamic_slice_in_dim(vp, start, 3 * BLOCK, axis=1)
        qpos = start + jnp.arange(BLOCK)
        kpos = start - BLOCK + jnp.arange(3 * BLOCK)
        valid = (jnp.abs(kpos[None, :] - qpos[:, None]) <= A_WINDOW) & ((kpos >= 0) & (kpos < s))[None, :]
        sc = jnp.einsum('bqkgd,bskd->bkgqs', qb, kb).astype(jnp.float32) * scale
        sc = jnp.where(valid, sc, NEG)
        m = jnp.maximum(sc.max(-1), sk)
        p = jnp.exp(sc - m[..., None])
        den = p.sum(-1) + jnp.exp(sk - m)
        o = jnp.einsum('bkgqs,bskd->bqkgd', p, vb.astype(jnp.float32))
        o = o / den.transpose(0, 3, 1, 2)[..., None]
        return o.reshape(b, BLOCK, hq * dh)

    out = lax.map(block, jnp.arange(nb))
    return out.transpose(1, 0, 2, 3).reshape(b, s, hq * dh)


def dilated_mixture(q, k, v):
    b, s, ng, hg, dh = q.shape
    nb = s // BLOCK
    scale = dh ** -0.5
    qs = [q[:, :, gi] for gi in range(ng)]
    ks = [k[:, :, gi] for gi in range(ng)]
    vs = [v[:, :, gi] for gi in range(ng)]

    def block(i):
        start = i * BLOCK
        qpos = start + jnp.arange(BLOCK)
        outs, maxs, dens = [], [], []
        for gi, (w, d) in enumerate(B_GROUPS):
            n_side = (w // 2) // d
            offs = jnp.arange(-n_side, n_side + 1) * d
            idx = qpos[:, None] + offs[None, :]
            valid = (idx >= 0) & (idx < s)
            idx_c = jnp.clip(idx, 0, s - 1)
            qb = lax.dynamic_slice_in_dim(qs[gi], start, BLOCK, axis=1)
            kb = jnp.take(ks[gi], idx_c, axis=1)
            vb = jnp.take(vs[gi], idx_c, axis=1)
            sc = jnp.einsum('bqhd,bqjhd->bhqj', qb, kb).astype(jnp.float32) * scale
            sc = jnp.where(valid[None, None], sc, NEG)
            m = sc.max(-1)
            p = jnp.exp(sc - m[..., None])
            l = p.sum(-1)
            o = jnp.einsum('bhqj,bqjhd->bqhd', p, vb.astype(jnp.float32)) / l.transpose(0, 2, 1)[..., None]
            outs.append(o)
            maxs.append(m)
            dens.append(l)
        m_all = jnp.stack(maxs)
        l_all = jnp.stack(dens)
        wts = l_all * jnp.exp(m_all - m_all.max(0, keepdims=True))
        wts = wts / wts.sum(0, keepdims=True)
        o = (wts.transpose(0, 1, 3, 2)[..., None] * jnp.stack(outs)).sum(0)
        return o.reshape(b, BLOCK, hg * dh)

    out = lax.map(block, jnp.arange(nb))
    return out.transpose(1, 0, 2, 3).reshape(b, s, hg * dh)


def setup_inputs(seed: int = 0) -> dict:
    key = jax.random.key(seed)
    ks = jax.random.split(key, 16)
    f32 = jnp.float32
    n = lambda k, shape, sc: jax.random.normal(k, shape, f32) * sc
    x = jax.random.normal(ks[0], (BATCH, SEQ, D_MODEL), f32)
    w_in = n(ks[1], (DEPTH, D_MODEL, IN_W), D_MODEL ** -0.5)
    col_scale = np.ones((IN_W,), np.float32)
    col_scale[SPLITS[1]:SPLITS[2]] = DN_BETA
    col_scale[SPLITS[4]:] = DN_BETA
    w_in = w_in * jnp.asarray(col_scale)
    a_sink = n(ks[2], (DEPTH, A_Q_HEADS), 0.5)
    w_gate = n(ks[3], (DEPTH, D_MODEL, N_BRANCH * D_MODEL), D_MODEL ** -0.5)
    b_gate = n(ks[4], (DEPTH, N_BRANCH * D_MODEL), 0.1)
    w_br_a = n(ks[5], (DEPTH, A_OUT_W, D_MODEL), A_OUT_W ** -0.5)
    w_br_b = n(ks[6], (DEPTH, B_OUT_W, D_MODEL), B_OUT_W ** -0.5)
    w_out = n(ks[7], (DEPTH, D_MODEL, D_MODEL), DN_BETA * D_MODEL ** -0.5)
    ln1_g = 1.0 + n(ks[8], (DEPTH, D_MODEL), 0.02)
    ln1_b = n(ks[9], (DEPTH, D_MODEL), 0.02)
    w_ff_gate = n(ks[10], (DEPTH, D_MODEL, D_FF), D_MODEL ** -0.5)
    w_ff_up = n(ks[11], (DEPTH, D_MODEL, D_FF), D_MODEL ** -0.5)
    w_ff_down = n(ks[12], (DEPTH, D_FF, D_MODEL), DN_BETA * D_FF ** -0.5)
    ln2_g = 1.0 + n(ks[13], (DEPTH, D_MODEL), 0.02)
    ln2_b = n(ks[14], (DEPTH, D_MODEL), 0.02)
    return {"x": x, "w_in": w_in, "a_sink": a_sink, "w_gate": w_gate, "b_gate": b_gate,
            "w_br_a": w_br_a, "w_br_b": w_br_b, "w_out": w_out, "ln1_g": ln1_g, "ln1_b": ln1_b,
            "w_ff_gate": w_ff_gate, "w_ff_up": w_ff_up, "w_ff_down": w_ff_down,
            "ln2_g": ln2_g, "ln2_b": ln2_b}


def reference(x, w_in, a_sink, w_gate, b_gate, w_br_a, w_br_b, w_out, ln1_g, ln1_b,
              w_ff_gate, w_ff_up, w_ff_down, ln2_g, ln2_b):
    b, s, _ = x.shape
    pos = jnp.arange(s)
    h = x
    for l in range(DEPTH):
        proj = h @ w_in[l]
        qa, ka, va, qb, kb, vb = jnp.split(proj, SPLITS, axis=-1)
        qa = rope_partial(qa.reshape(b, s, A_Q_HEADS, HEAD_DIM), pos)
        ka = rope_partial(ka.reshape(b, s, A_KV_HEADS, HEAD_DIM), pos)
        va = va.reshape(b, s, A_KV_HEADS, HEAD_DIM)
        qb = rope_partial(qb.reshape(b, s, B_HEADS, HEAD_DIM), pos).reshape(b, s, B_NG, B_HEADS_PER_GROUP, HEAD_DIM)
        kb = rope_partial(kb.reshape(b, s, B_HEADS, HEAD_DIM), pos).reshape(b, s, B_NG, B_HEADS_PER_GROUP, HEAD_DIM)
        vb = vb.reshape(b, s, B_NG, B_HEADS_PER_GROUP, HEAD_DIM)
        o_a = windowed_gqa_sink(qa, ka, va, a_sink[l]).astype(h.dtype)
        o_b = dilated_mixture(qb, kb, vb).astype(h.dtype)
        gates = jax.nn.sigmoid(h @ w_gate[l] + b_gate[l])
        g_a, g_b = jnp.split(gates, N_BRANCH, axis=-1)
        mixed = g_a * (o_a @ w_br_a[l]) + g_b * (o_b @ w_br_b[l])
        y = mixed @ w_out[l]
        h = layer_norm(DN_ALPHA * h + y, ln1_g[l], ln1_b[l])
        f = (jax.nn.silu(h @ w_ff_gate[l]) * (h @ w_ff_up[l])) @ w_ff_down[l]
        h = layer_norm(DN_ALPHA * h + f, ln2_g[l], ln2_b[l])
    return h
```

```python
import contextlib
import numpy as np
import concourse.bass as bass
import concourse.mybir as mybir
from concourse.bass_utils import run_bass_kernel_spmd

F32 = mybir.dt.float32
BF16 = mybir.dt.bfloat16
AF = mybir.ActivationFunctionType
ALU = mybir.AluOpType

D = 1024
HD = 64
IN_W = 3840
ROW = 4864
C_QA, C_KA, C_QB, C_KB, C_VA, C_VB = 0, 1024, 1280, 2048, 2816, 3328
DFF = 2816
NFC = DFF // 128
ALPHA = 2.0 ** 0.25
EPS = 1e-5
SB = 2048
ENG = ("pe", "act", "dve", "pool", "sp")
import os as _os0
STQ = _os0.environ.get("STQ", "pool")


class Tk:
    __slots__ = ("w", "r", "sem", "name")

    def __init__(self, name="", sem=None):
        self.w = None
        self.r = {}
        self.sem = sem
        self.name = name


class Sched:
    def __init__(self, nc):
        self.nc = nc
        self.ops = {e: [] for e in ENG}
        self.cnt = {e: 0 for e in ENG}
        self.waited = {e: {} for e in ENG}
        self.dma_tot = {}
        self.sems = {}
        for e in ("pe", "act", "dve", "pool"):
            self.sems[e] = nc.alloc_semaphore(name="sem_" + e)
        self.ndma = 0

    def dma_sem(self):
        k = "dma%d" % self.ndma
        self.ndma += 1
        self.sems[k] = self.nc.alloc_semaphore(name=k)
        self.dma_tot[k] = 0
        return k

    def _deps(self, eng, reads, writes, pe_accum, skip_same_w=False):
        deps = {}

        def add(ev):
            if ev is None:
                return
            s, v = ev
            if deps.get(s, 0) < v:
                deps[s] = v

        for t in reads:
            add(t.w)
        for t in writes:
            if not ((pe_accum and t.w is not None and t.w[0] == "pe") or
                    (skip_same_w and t.w is not None and t.w[0] == eng)):
                add(t.w)
            for s, v in t.r.items():
                add((s, v))
        waits = []
        wd = self.waited[eng]
        for s, v in deps.items():
            if s == eng and eng in ("pe", "sp"):
                continue
            if wd.get(s, 0) < v:
                wd[s] = v
                waits.append((s, v))
        return waits

    def _mark(self, ev, reads, writes):
        s, v = ev
        for t in reads:
            if t.r.get(s, 0) < v:
                t.r[s] = v
        for t in writes:
            t.w = ev
            t.r = {}

    def op(self, eng, fns, reads=(), writes=(), pe_accum=False, ssw=False):
        if not isinstance(fns, (list, tuple)):
            fns = [fns]
        waits = self._deps(eng, reads, writes, pe_accum, ssw)
        self.cnt[eng] += 1
        ev = (eng, self.cnt[eng])
        self.ops[eng].append((waits, list(fns), (eng, 1)))
        self._mark(ev, reads, writes)
        return ev

    def dma(self, q, out, in_, tile, reads=(), writes=()):
        if tile.sem is None:
            tile.sem = self.dma_sem()
        waits = self._deps(q, reads, writes, False)
        self.dma_tot[tile.sem] += 16
        ev = (tile.sem, self.dma_tot[tile.sem])
        self.ops[q].append((waits, [lambda e, o=out, i=in_: e.dma_start(out=o, in_=i)], (tile.sem, 16)))
        self._mark(ev, reads, writes)
        return ev

    def barrier(self):
        for e in ENG:
            waits = []
            wd = self.waited[e]
            for f in ("pe", "act", "dve", "pool"):
                if f != e and self.cnt[f] > wd.get(f, 0):
                    wd[f] = self.cnt[f]
                    waits.append((f, self.cnt[f]))
            for k, v in self.dma_tot.items():
                if v > wd.get(k, 0):
                    wd[k] = v
                    waits.append((k, v))
            self.ops[e].append((waits, [], None))

    def emit(self):
        nc = self.nc
        with nc.Block() as block:
            def run(name, eng):
                for waits, fns, inc in self.ops[name]:
                    for s, v in waits:
                        eng.wait_ge(self.sems[s], v)
                    ins = None
                    for f in fns:
                        ins = f(eng)
                    if inc is not None and ins is not None:
                        ins.then_inc(self.sems[inc[0]], inc[1])

            @block.tensor
            def _(e):
                run("pe", e)

            @block.scalar
            def _(e):
                run("act", e)

            @block.vector
            def _(e):
                run("dve", e)

            @block.gpsimd
            def _(e):
                run("pool", e)

            @block.sync
            def _(e):
                run("sp", e)


def sl(start, n, step):
    return slice(start, start + step * (n - 1) + 1, step)


class Ring:
    def __init__(self, items):
        self.items = items
        self.i = 0

    def next(self):
        it = self.items[self.i % len(self.items)]
        self.i += 1
        return it


def build(S=8192, dbg=False, phases=(1, 2, 3)):
    nc = bass.Bass("TRN2", target_bir_lowering=False)
    sc = Sched(nc)
    NB = S // 128

    def din(name, shape, dt=F32):
        return nc.dram_tensor(name, list(shape), dt, kind="ExternalInput").ap()

    x_d = din("x", [S, D])
    win_d = din("w_in", [D, IN_W])
    wgate_d = din("w_gate", [D, 2048])
    bg_d = din("bg", [128, 16])
    sink_d = din("a_sink", [16])
    wbra_d = din("w_br_a", [1024, D])
    wbrb_d = din("w_br_b", [256, D])
    wout_d = din("w_out", [D, D])
    ln1g_d = din("ln1_g", [D])
    ln1b_d = din("ln1_b", [D])
    wffg_d = din("w_ff_gate", [D, DFF])
    wffu_d = din("w_ff_up", [D, DFF])
    wffd_d = din("w_ff_down", [DFF, D])
    ln2g_d = din("ln2_g", [D])
    ln2b_d = din("ln2_b", [D])
    csn_d = din("csn", [S, 32])
    masks_d = din("masks", [128, 4, 128])
    ident_d = din("ident", [128, 128])

    skind = "ExternalOutput" if dbg else "Internal"
    qkv_d = nc.dram_tensor("qkv_s", [S, ROW], BF16, kind=skind).ap()
    gT_d = nc.dram_tensor("gT_s", [16, 128, S], BF16, kind=skind).ap()
    h1_d = nc.dram_tensor("h1_s", [S, D], F32, kind=skind).ap()
    out_d = nc.dram_tensor("out", [S, D], F32, kind="ExternalOutput").ap()

    psb = [nc.alloc_psum_tensor("ps%d" % i, [128, 512], F32) for i in range(6)]
    ptb = [nc.alloc_psum_tensor("pt%d" % i, [128, 1024], BF16) for i in range(2)]
    ps_ring = Ring([(psb[i], Tk("ps%d" % i)) for i in range(6)])
    pt_ring = Ring([(ptb[i], Tk("pt%d" % i)) for i in range(2)])

    uid = [0]

    def sb(st, name, shape, dt):
        uid[0] += 1
        return st.enter_context(nc.sbuf_tensor("sb%d_%s" % (uid[0], name), list(shape), dt))

    rr = {"cast": 0}

    def cast_any(out, in_, reads, writes, engs=("dve", "pool", "act")):
        e = engs[rr["cast"] % len(engs)]
        rr["cast"] += 1
        if e == "act":
            sc.op("act", lambda g: g.activation(out=out, in_=in_, func=AF.Copy), reads, writes)
        else:
            sc.op(e, lambda g: g.tensor_copy(out=out, in_=in_), reads, writes)

    def load_weight(st, name, src, KC, N, wst):
        w = sb(st, name, [128, KC, N], BF16)
        tk = Tk(name)
        for kc in range(KC):
            for n0 in range(0, N, 1024):
                n1 = min(N, n0 + 1024)
                stg, stk = wst.next()
                sc.dma("sp", stg[:, 0:n1 - n0], src[kc * 128:(kc + 1) * 128, n0:n1], stk, writes=[stk])
                cast_any(w[:, kc, n0:n1], stg[:, 0:n1 - n0], [stk], [tk])
        return w, tk

    def consts(st, wst):
        ident = sb(st, "ident", [128, 128], BF16)
        masks = sb(st, "masks", [128, 4, 128], BF16)
        ones = sb(st, "ones", [128, 128], BF16)
        ctk = Tk("consts")
        stg, stk = wst.next()
        sc.dma("sp", stg[:, 0:128], ident_d, stk, writes=[stk])
        sc.op("dve", lambda g: g.tensor_copy(out=ident[:], in_=stg[:, 0:128]), [stk], [ctk])
        stg2, stk2 = wst.next()
        sc.dma("sp", stg2[:, 0:512], masks_d.rearrange("p a b -> p (a b)"), stk2, writes=[stk2])
        sc.op("dve", lambda g: g.tensor_copy(out=masks[:].rearrange("p a b -> p (a b)"), in_=stg2[:, 0:512]),
              [stk2], [ctk])
        sc.op("dve", lambda g: g.memset(ones[:], 1.0), [], [ctk])
        return ident, masks, ones, ctk

    def layer_norm(pre, ptk, g_t, b_t, outt, otk, small, smtk):
        stats = small[:, 0:12]
        mv = small[:, 12:14]
        rstd = small[:, 14:15]
        sc.op("dve", lambda g: g.bn_stats(out=small[:, 0:6], in_=pre[:, 0:512]), [ptk], [smtk])
        sc.op("dve", lambda g: g.bn_stats(out=small[:, 6:12], in_=pre[:, 512:1024]), [ptk], [smtk])
        sc.op("dve", lambda g: g.bn_aggr(out=mv, in_=stats), [smtk], [smtk])
        sc.op("dve", lambda g: g.tensor_scalar_add(out=rstd, in0=small[:, 13:14], scalar1=EPS), [smtk], [smtk])
        sc.op("act", lambda g: g.activation(out=rstd, in_=rstd, func=AF.Sqrt), [smtk], [smtk])
        sc.op("dve", lambda g: g.reciprocal(out=rstd, in_=rstd), [smtk], [smtk])
        sc.op("dve", lambda g: g.tensor_scalar(out=pre, in0=pre, scalar1=small[:, 12:13], scalar2=rstd,
                                               op0=ALU.subtract, op1=ALU.mult), [ptk, smtk], [ptk])
        sc.op("pool", lambda g: g.tensor_mul(out=pre, in0=pre, in1=g_t), [ptk], [ptk])
        sc.op("pool", lambda g: g.tensor_add(out=outt, in0=pre, in1=b_t), [ptk], [otk])

    def bcast_load(st, name, src, wst_unused=None):
        t = sb(st, name, [128, D], F32)
        tk = Tk(name)
        sc.dma("sp", t[:], src.partition_broadcast(128), tk, writes=[tk])
        return t, tk

    def phase1(st):
      if True:
        wst = Ring([(sb(st, "wst%d" % i, [128, 1024], F32), Tk("wst%d" % i)) for i in range(3)])
        ident, masks, ones, ctk = consts(st, wst)
        w_bf, wtk = load_weight(st, "w_in_bf", win_d, 8, IN_W, wst)
        wg_bf, wgtk = load_weight(st, "w_g_bf", wgate_d, 8, 2048, wst)
        bg = sb(st, "bg", [128, 16], F32)
        bgtk = Tk("bg")
        sc.dma("sp", bg[:], bg_d, bgtk, writes=[bgtk])
        sc.barrier()

        xin = Ring([(sb(st, "xin%d" % i, [128, D], F32), Tk("xin%d" % i)) for i in range(3)])
        xbf = Ring([(sb(st, "xbf%d" % i, [128, D], BF16), Tk("xbf%d" % i)) for i in range(2)])
        xT = [(sb(st, "xT%d" % i, [128, 8, 512], BF16), [Tk("xT%d_%d" % (i, b)) for b in range(4)]) for i in range(2)]
        cs = [(sb(st, "cs%d" % i, [128, 4, 32], F32), Tk("cs%d" % i)) for i in range(2)]
        stq = Ring([(sb(st, "stq%d" % i, [128, ROW], BF16), [Tk("stq%d_%d" % (i, n)) for n in range(8)])
                    for i in range(2)])
        rot = Ring([(sb(st, "rot%d" % i, [128, 44, 16], F32), Tk("rot%d" % i)) for i in range(2)])
        gst = Ring([(sb(st, "gst%d" % i, [128, 4, 512], BF16), Tk("gst%d" % i)) for i in range(2)])
        rA = Ring([(sb(st, "rA%d" % i, [128, 44, 16], F32), Tk("rA%d" % i)) for i in range(2)])
        rB = Ring([(sb(st, "rB%d" % i, [128, 44, 16], F32), Tk("rB%d" % i)) for i in range(2)])
        NT = S // 512

        def prep(t):
            xt, xtk = xT[t % 2]
            c_t, c_tk = cs[t % 2]
            sc.dma("sp", c_t[:], csn_d[t * 512:(t + 1) * 512, :].rearrange("(b p) c -> p b c", p=128), c_tk,
                   writes=[c_tk])
            for b in range(4):
                xi, xitk = xin.next()
                sc.dma("sp", xi[:], x_d[t * 512 + b * 128: t * 512 + (b + 1) * 128, :], xitk, writes=[xitk])
                xb, xbtk = xbf.next()
                sc.op("dve", lambda g, o=xb, i=xi: g.tensor_copy(out=o[:], in_=i[:]), [xitk], [xbtk])
                pt, pttk = pt_ring.next()
                sc.op("pe", [lambda g, o=pt, i=xb, kc=kc: g.transpose(out=o[:, kc * 128:(kc + 1) * 128],
                                                                       in_=i[:, kc * 128:(kc + 1) * 128],
                                                                       identity=ident[:])
                             for kc in range(8)], [xbtk], [pttk])
                sc.op("act", lambda g, o=xt, i=pt, b=b: g.activation(
                    out=o[:, :, b * 128:(b + 1) * 128], in_=i[:].rearrange("p (k t) -> p k t", k=8), func=AF.Copy),
                    [pttk], [xtk[b]])

        def v_chunk(ps, pstk, pcol0, nh, sq, sqtk, col0):
            pv = ps[:, pcol0:pcol0 + nh * 64].rearrange("p (h d) -> p h d", d=64).unsqueeze(2).broadcast_to(
                [128, nh, 2, 64])
            ov = sq[:, col0:col0 + nh * 128].rearrange("p (h r d) -> p h r d", r=2, d=64)
            sc.op("act", lambda g: g.activation(out=ov, in_=pv, func=AF.Copy), [pstk], [sqtk])

        def compute(t):
            xt, xtk = xT[t % 2]
            c_t, c_tk = cs[t % 2]
            for b in range(4):
                sq, sqtks = stq.next()
                rt_, rtk = rot.next()
                for n in range(8):
                    wdt = 512 if n < 7 else 256
                    ps, pstk = ps_ring.next()
                    sc.op("pe", [lambda g, o=ps, kc=kc, n=n, wdt=wdt, b=b: g.matmul(
                        o[:, 0:wdt], lhsT=xt[:, kc, b * 128:(b + 1) * 128], rhs=w_bf[:, kc, n * 512:n * 512 + wdt],
                        start=(kc == 0), stop=(kc == 7)) for kc in range(8)], [xtk[b]], [pstk])
                    nh = 8 if n <= 4 else (4 if n == 5 else 0)
                    if nh:
                        sc.op("act", lambda g, ps=ps, sq=sq, n=n, nh=nh: g.activation(
                            out=sq[:, n * 512:n * 512 + nh * 64], in_=ps[:, 0:nh * 64], func=AF.Copy),
                            [pstk], [sqtks[n]])
                        sc.op("act", lambda g, ps=ps, rt_=rt_, n=n, nh=nh: g.activation(
                            out=rt_[:, 8 * n:8 * n + nh, :],
                            in_=ps[:, 0:nh * 64].rearrange("p (h d) -> p h d", d=64)[:, :, 0:16], func=AF.Copy),
                            [pstk], [rtk])
                    if n == 5:
                        v_chunk(ps, pstk, 256, 4, sq, sqtks[n], C_VA)
                    elif n == 6:
                        v_chunk(ps, pstk, 0, 8, sq, sqtks[n], C_VB)
                    elif n == 7:
                        v_chunk(ps, pstk, 0, 4, sq, sqtks[n], C_VB + 1024)
                a_t, atk = rA.next()
                b_t, btk = rB.next()
                cc = c_t[:, b, 0:16].unsqueeze(1).broadcast_to([128, 44, 16])
                ns = c_t[:, b, 16:24].unsqueeze(1).broadcast_to([128, 44, 8])
                ps_ = c_t[:, b, 24:32].unsqueeze(1).broadcast_to([128, 44, 8])
                sc.op("dve", lambda g, a_t=a_t, rt_=rt_, cc=cc: g.tensor_tensor(
                    out=a_t[:], in0=rt_[:], in1=cc, op=ALU.mult), [rtk, c_tk], [atk])
                sc.op("pool", lambda g, b_t=b_t, rt_=rt_, ns=ns: g.tensor_tensor(
                    out=b_t[:, :, 0:8], in0=rt_[:, :, 8:16], in1=ns, op=ALU.mult), [rtk, c_tk], [btk])
                sc.op("pool", lambda g, b_t=b_t, rt_=rt_, ps_=ps_: g.tensor_tensor(
                    out=b_t[:, :, 8:16], in0=rt_[:, :, 0:8], in1=ps_, op=ALU.mult), [rtk, c_tk], [btk])
                ov = sq[:, 0:2816].rearrange("p (h d) -> p h d", d=64)[:, :, 0:16]
                sc.op("dve", lambda g, ov=ov, a_t=a_t, b_t=b_t: g.tensor_tensor(
                    out=ov, in0=a_t[:], in1=b_t[:], op=ALU.add), [atk, btk], sqtks[0:6])
                r0 = t * 512 + b * 128
                sc.dma(STQ, qkv_d[r0:r0 + 128, :], sq[:], sqtks[0], reads=sqtks)
            for gc in range(16):
                ps, pstk = ps_ring.next()
                sc.op("pe", [lambda g, o=ps, kc=kc, gc=gc: g.matmul(
                    o[:, 0:512], lhsT=wg_bf[:, kc, gc * 128:(gc + 1) * 128], rhs=xt[:, kc, :],
                    start=(kc == 0), stop=(kc == 7)) for kc in range(8)], xtk, [pstk])
                if gc % 4 == 0:
                    gs, gstk = gst.next()
                sc.op("act", lambda g, o=gs, i=ps, gc=gc: g.activation(
                    out=o[:, gc % 4, :], in_=i[:, 0:512], func=AF.Sigmoid, bias=bg[:, gc:gc + 1]), [pstk, bgtk], [gstk])
                if gc % 4 == 3:
                    c0 = gc - 3
                    sc.dma(STQ, gT_d[c0:c0 + 4, :, t * 512:(t + 1) * 512].rearrange("c p t -> p c t"), gs[:],
                           gstk, reads=[gstk])

        import os
        cut = int(os.environ.get("P1CUT", "9"))
        if cut >= 1:
            prep(0)
        for t in range(NT if cut >= 2 else 0):
            if t + 1 < NT:
                prep(t + 1)
            compute(t)
        sc.barrier()

    if 1 in phases:
        with contextlib.ExitStack() as st:
            phase1(st)

    def phase2(st):
      if True:
        wst = Ring([(sb(st, "wst%d" % i, [128, 1024], F32), Tk("wst%d" % i)) for i in range(2)])
        ident, masks, ones, ctk = consts(st, wst)
        wbra, _ = load_weight(st, "wbra", wbra_d, 8, D, wst)
        wbrb, _ = load_weight(st, "wbrb", wbrb_d, 2, D, wst)
        wout, _ = load_weight(st, "wout", wout_d, 8, D, wst)
        ln_g, lgtk = bcast_load(st, "ln1g", ln1g_d)
        ln_b, lbtk = bcast_load(st, "ln1b", ln1b_d)
        es = sb(st, "es", [128, 16], F32)
        estk = Tk("es")
        sc.dma("sp", es[:], sink_d.partition_broadcast(128), estk, writes=[estk])
        sc.op("act", lambda g: g.activation(out=es[:], in_=es[:], func=AF.Exp), [estk], [estk])
        mp = sb(st, "mp", [128, 4, 2, 128], BF16)
        for f in range(2):
            for l in range(2):
                i = f * 2 + l
                sc.op("dve", lambda g, i=i, f=f: g.tensor_copy(out=mp[:, i, 0, :], in_=masks[:, 2 if f else 0, :]),
                      [ctk], [ctk])
                sc.op("dve", lambda g, i=i, l=l: g.tensor_copy(out=mp[:, i, 1, :], in_=masks[:, 3 if l else 1, :]),
                      [ctk], [ctk])
        accU = sb(st, "accU", [128, 2, SB], F32)
        accL = sb(st, "accL", [128, 2, SB], F32)
        acctk = Tk("acc")
        obT = sb(st, "obT", [128, 2, SB], BF16)
        obtk = Tk("obT")
        qb = [(sb(st, "qb%d" % i, [128, 256], BF16), Tk("qb%d" % i)) for i in range(3)]
        kbt = [(sb(st, "kb%d" % i, [128, 2, 256], BF16), Tk("kb%d" % i)) for i in range(3)]
        vbt = [(sb(st, "vb%d" % i, [128, 2, 512], BF16), Tk("vb%d" % i)) for i in range(3)]
        for i in range(3):
            sc.op("pool", lambda g, i=i: g.memset(qb[i][0][:], 0.0), [], [qb[i][1]])
            sc.op("pool", lambda g, i=i: g.memset(kbt[i][0][:], 0.0), [], [kbt[i][1]])
            sc.op("pool", lambda g, i=i: g.memset(vbt[i][0][:], 0.0), [], [vbt[i][1]])
        qkT = Ring([(sb(st, "qkT%d" % i, [128, 6, 128], BF16), Tk("qkT%d" % i)) for i in range(3)])
        pTb = Ring([(sb(st, "pTb%d" % i, [128, 2, 2, 2, 128], BF16), Tk("pTb%d" % i)) for i in range(2)])
        qa = Ring([(sb(st, "qa%d" % i, [128, 1024], BF16), Tk("qa%d" % i)) for i in range(2)])
        ka = [(sb(st, "ka%d" % i, [128, 256], BF16), Tk("ka%d" % i)) for i in range(4)]
        va = [(sb(st, "va%d" % i, [128, 512], BF16), Tk("va%d" % i)) for i in range(4)]
        kaT = [(sb(st, "kaT%d" % i, [128, 2, 128], BF16), Tk("kaT%d" % i)) for i in range(4)]
        qaT = Ring([(sb(st, "qaT%d" % i, [128, 8, 128], BF16), Tk("qaT%d" % i)) for i in range(2)])
        pA = Ring([(sb(st, "pA%d" % i, [128, 512], BF16), Tk("pA%d" % i)) for i in range(6)])
        rtmp = Ring([(sb(st, "rtmp%d" % i, [128, 4, 128], F32), Tk("rtmp%d" % i)) for i in range(2)])
        T2 = 256
        oaT = Ring([(sb(st, "oaT%d" % i, [128, 8, T2], BF16), Tk("oaT%d" % i)) for i in range(2)])
        gt = Ring([(sb(st, "gt%d" % i, [128, 16, T2], BF16), Tk("gt%d" % i)) for i in range(2)])
        xr = Ring([(sb(st, "xr%d" % i, [128, D], F32), Tk("xr%d" % i)) for i in range(3)])
        t1r = Ring([(sb(st, "t1_%d" % i, [128, T2], F32), Tk("t1_%d" % i)) for i in range(2)])
        t2r = Ring([(sb(st, "t2_%d" % i, [128, T2], F32), Tk("t2_%d" % i)) for i in range(2)])
        mixT = Ring([(sb(st, "mixT%d" % i, [128, 8, T2], BF16), Tk("mixT%d" % i)) for i in range(2)])
        pre = Ring([(sb(st, "pre%d" % i, [128, D], F32), Tk("pre%d" % i)) for i in range(2)])
        hst = Ring([(sb(st, "hst%d" % i, [128, D], F32), Tk("hst%d" % i)) for i in range(2)])
        small = Ring([(sb(st, "sm%d" % i, [128, 16], F32), Tk("sm%d" % i)) for i in range(2)])
        sc.barrier()

        bcount = [0]

        def b_T(u):
            sbi, g, d, r, J, j0 = u["args"]
            slot = bcount[0] % 3
            bcount[0] += 1
            q_t, qtk = qb[slot]
            k_t, ktk = kbt[slot]
            v_t, vtk = vbt[slot]
            cq = C_QB + g * 256
            ck = C_KB + g * 256
            cv = C_VB + g * 512
            t0 = r + d * j0
            sc.dma("sp", q_t[:], qkv_d[sl(t0, 128, d), cq:cq + 256], qtk, writes=[qtk])
            first = last = 0
            for kb in range(2):
                jk0 = j0 - 64 + kb * 128
                lo = max(0, -jk0)
                hi = min(128, J - jk0)
                if kb == 0 and lo > 0:
                    first = 1
                if kb == 1 and hi < 128:
                    last = 1
                ts = r + d * (jk0 + lo)
                sc.dma("sp", k_t[lo:hi, kb, :], qkv_d[sl(ts, hi - lo, d), ck:ck + 256], ktk, writes=[ktk])
                sc.dma("sp", v_t[lo:hi, kb, :], qkv_d[sl(ts, hi - lo, d), cv:cv + 512], vtk, writes=[vtk])
            pt, pttk = pt_ring.next()
            fns = []
            for c in range(2):
                fns.append(lambda e, c=c: e.transpose(out=pt[:, c * 128:(c + 1) * 128],
                                                      in_=q_t[:, c * 128:(c + 1) * 128], identity=ident[:]))
            for kb in range(2):
                for c in range(2):
                    i = 2 + kb * 2 + c
                    fns.append(lambda e, c=c, kb=kb, i=i: e.transpose(out=pt[:, i * 128:(i + 1) * 128],
                                                                      in_=k_t[:, kb, c * 128:(c + 1) * 128],
                                                                      identity=ident[:]))
            sc.op("pe", fns, [qtk, ktk], [pttk])
            qk, qktk = qkT.next()
            sc.op("act", lambda e: e.activation(out=qk[:].rearrange("p a b -> p (a b)"), in_=pt[:, 0:768],
                                                func=AF.Copy), [pttk], [qktk])
            u.update(qk=qk, qktk=qktk, v_t=v_t, vtk=vtk, first=first, last=last, t0=t0)

        def b_S(u):
            qk, qktk = u["qk"], u["qktk"]
            sbank = [ps_ring.next(), ps_ring.next()]
            for half in range(2):
                ps, pstk = sbank[half]
                fns = []
                for c in range(2):
                    for kb in range(2):
                        fns.append(lambda e, c=c, kb=kb, half=half, ps=ps: e.matmul(
                            ps[:, (c * 2 + kb) * 128:(c * 2 + kb + 1) * 128],
                            lhsT=qk[half * 64:(half + 1) * 64, 2 + kb * 2 + c, :],
                            rhs=qk[half * 64:(half + 1) * 64, c, :], start=True, stop=True))
                sc.op("pe", fns, [qktk], [pstk])
            p_t, ptk_ = pTb.next()
            for half in range(2):
                ps, pstk = sbank[half]
                sc.op("act", lambda e, half=half, ps=ps: e.activation(
                    out=p_t[:, half].rearrange("p c k q -> p (c k q)"), in_=ps[:, 0:512], func=AF.Exp, scale=0.125),
                    [pstk], [ptk_], ssw=True)
            pv4 = p_t[:].rearrange("p h c k q -> p (h c) k q")
            mview = mp[:, u["first"] * 2 + u["last"]].unsqueeze(1).broadcast_to([128, 4, 2, 128])
            sc.op("dve", lambda e: e.tensor_tensor(out=pv4, in0=pv4, in1=mview, op=ALU.mult), [ptk_, ctk], [ptk_])
            u.update(p_t=p_t, ptk_=ptk_)

        def b_PV(u):
            sbi, g, d, r, J, j0 = u["args"]
            p_t, ptk_, v_t, vtk = u["p_t"], u["ptk_"], u["v_t"], u["vtk"]
            (po, potk), (pl, pltk) = ps_ring.next(), ps_ring.next()
            fo, fl = [], []
            for h in range(4):
                c, half = h // 2, h % 2
                for kb in range(2):
                    fo.append(lambda e, h=h, c=c, half=half, kb=kb: e.matmul(
                        po[:, h * 128:(h + 1) * 128], lhsT=v_t[:, kb, h * 128:(h + 1) * 128],
                        rhs=p_t[:, half, c, kb, :], start=(kb == 0), stop=(kb == 1)))
                    fl.append(lambda e, h=h, c=c, half=half, kb=kb: e.matmul(
                        pl[:, h * 128:(h + 1) * 128], lhsT=ones[:],
                        rhs=p_t[:, half, c, kb, :], start=(kb == 0), stop=(kb == 1)))
            sc.op("pe", fo, [ptk_, vtk], [potk])
            sc.op("pe", fl, [ptk_, ctk], [pltk])
            a0 = u["t0"] - sbi * SB
            for half in range(2):
                hs = slice(half * 64, (half + 1) * 64)
                for acc, ps, pstk in ((accU, po, potk), (accL, pl, pltk)):
                    av = acc[hs, :, sl(a0, 128, d)]
                    pv = ps[hs, :].rearrange("p (c x q) -> p c x q", c=2, x=2)[:, :, half, :]
                    sc.op("dve", lambda e, av=av, pv=pv: e.tensor_tensor(out=av, in0=av, in1=pv, op=ALU.add),
                          [acctk, pstk], [acctk], ssw=True)

        def mixer_b(sbi):
            sc.op("pool", lambda e: e.memset(accU[:], 0.0), [], [acctk])
            sc.op("pool", lambda e: e.memset(accL[:], 0.0), [], [acctk])
            units = []
            for g, d in enumerate((1, 4, 16)):
                J = S // d
                nj = SB // d // 128
                for r in range(d):
                    for jb in range(nj):
                        units.append({"args": (sbi, g, d, r, J, sbi * (SB // d) + jb * 128)})
            n = len(units)
            b_T(units[0])
            b_T(units[1])
            b_S(units[0])
            for i in range(n):
                if i + 2 < n:
                    b_T(units[i + 2])
                if i + 1 < n:
                    b_S(units[i + 1])
                b_PV(units[i])
            sc.op("dve", lambda e: e.reciprocal(out=accL[:], in_=accL[:]), [acctk], [acctk])
            sc.op("dve", lambda e: e.tensor_tensor(out=obT[:], in0=accU[:], in1=accL[:], op=ALU.mult),
                  [acctk], [obtk])

        def a_load_kv(blk):
            k_t, ktk = ka[blk % 4]
            v_t, vtk = va[blk % 4]
            sc.dma("sp", k_t[:], qkv_d[blk * 128:(blk + 1) * 128, C_KA:C_KA + 256], ktk, writes=[ktk])
            sc.dma("sp", v_t[:], qkv_d[blk * 128:(blk + 1) * 128, C_VA:C_VA + 512], vtk, writes=[vtk])
            pt, pttk = pt_ring.next()
            sc.op("pe", [lambda e, c=c: e.transpose(out=pt[:, c * 128:(c + 1) * 128], in_=k_t[:, c * 128:(c + 1) * 128],
                                                    identity=ident[:]) for c in range(2)], [ktk], [pttk])
            kt_t, kttk = kaT[blk % 4]
            sc.op("act", lambda e: e.activation(out=kt_t[:].rearrange("p a b -> p (a b)"), in_=pt[:, 0:256],
                                                func=AF.Copy), [pttk], [kttk])

        blkctx = {}

        def a_prep(i):
            if i == 0:
                a_load_kv(0)
            if i + 1 < NB:
                a_load_kv(i + 1)
            q_t, qtk = qa.next()
            sc.dma("sp", q_t[:], qkv_d[i * 128:(i + 1) * 128, 0:1024], qtk, writes=[qtk])
            pt, pttk = pt_ring.next()
            sc.op("pe", [lambda e, c=c: e.transpose(out=pt[:, c * 128:(c + 1) * 128], in_=q_t[:, c * 128:(c + 1) * 128],
                                                    identity=ident[:]) for c in range(8)], [qtk], [pttk])
            qT, qTtk = qaT.next()
            sc.op("act", lambda e: e.activation(out=qT[:].rearrange("p a b -> p (a b)"), in_=pt[:, 0:1024],
                                                func=AF.Copy), [pttk], [qTtk])
            blkctx[i] = (qT, qTtk)

        def a_S(u):
            i, k = u["i"], u["k"]
            qT, qTtk = blkctx[i]
            kbs = [kb for kb in range(3) if 0 <= i - 1 + kb < NB]
            p, half = k // 2, k % 2
            hs = slice(half * 64, (half + 1) * 64)
            plist = []
            for kb in kbs:
                blk = i - 1 + kb
                ps, pstk = ps_ring.next()
                kt_t, kttk = kaT[blk % 4]
                sc.op("pe", lambda e, ps=ps, kt_t=kt_t: e.matmul(
                    ps[:, 0:512], lhsT=kt_t[hs, p, :], rhs=qT[hs, p * 4:(p + 1) * 4, :], start=True, stop=True),
                    [kttk, qTtk], [pstk])
                pp, pptk = pA.next()
                sc.op("act", lambda e, ps=ps, pp=pp: e.activation(out=pp[:], in_=ps[:, 0:512], func=AF.Exp,
                                                                  scale=0.125), [pstk], [pptk])
                if kb != 1:
                    mv = masks[:, 0 if kb == 0 else 1, :].unsqueeze(1).broadcast_to([128, 4, 128])
                    ppv = pp[:].rearrange("p (m q) -> p m q", m=4)
                    sc.op("pool" if kb == 0 else "dve",
                          lambda e, ppv=ppv, mv=mv: e.tensor_tensor(out=ppv, in0=ppv, in1=mv, op=ALU.mult),
                          [pptk, ctk], [pptk])
                plist.append((pp, pptk, blk))
            u["plist"] = plist

        def a_PV(u):
            i, k, oa, oatk, bi = u["i"], u["k"], u["oa"], u["oatk"], u["bi"]
            plist = u["plist"]
            (po, potk), (pl, pltk) = ps_ring.next(), ps_ring.next()
            n = len(plist)
            sc.op("pe", [lambda e, j=j, pp=pp, blk=blk: e.matmul(
                po[:, 0:512], lhsT=va[blk % 4][0][:, k * 128:(k + 1) * 128], rhs=pp[:],
                start=(j == 0), stop=(j == n - 1)) for j, (pp, pptk, blk) in enumerate(plist)],
                [x[1] for x in plist] + [va[x[2] % 4][1] for x in plist], [potk])
            sc.op("pe", [lambda e, j=j, pp=pp: e.matmul(
                pl[:, 0:512], lhsT=ones[:], rhs=pp[:],
                start=(j == 0), stop=(j == n - 1)) for j, (pp, pptk, blk) in enumerate(plist)],
                [x[1] for x in plist] + [ctk], [pltk])
            rt, rttk = rtmp.next()
            esv = es[:, 4 * k:4 * k + 4].unsqueeze(2).broadcast_to([128, 4, 128])
            plv = pl[:, 0:512].rearrange("p (m q) -> p m q", m=4)
            pov = po[:, 0:512].rearrange("p (m q) -> p m q", m=4)
            sc.op("dve", lambda e: e.tensor_tensor(out=rt[:], in0=plv, in1=esv, op=ALU.add), [pltk, estk], [rttk])
            sc.op("dve", lambda e: e.reciprocal(out=rt[:], in_=rt[:]), [rttk], [rttk])
            for hf in range(2):
                h2 = slice(hf * 64, (hf + 1) * 64)
                sc.op("dve", lambda e, h2=h2, hf=hf: e.tensor_tensor(
                    out=oa[h2, 2 * k:2 * k + 2, bi * 128:(bi + 1) * 128], in0=pov[h2, hf::2, :],
                    in1=rt[h2, hf::2, :], op=ALU.mult), [potk, rttk], [oatk], ssw=True)

        def dense2(tt, oa, oatk, sbi):
            tok0 = tt * T2
            g_t, gtk = gt.next()
            sc.dma("sp", g_t[:], gT_d[:, :, tok0:tok0 + T2].rearrange("c p t -> p c t"), gtk, writes=[gtk])
            mx, mxtk = mixT.next()
            so = tok0 - sbi * SB
            for c in range(8):
                (pa, patk), (pb, pbtk) = ps_ring.next(), ps_ring.next()
                sc.op("pe", [lambda e, kc=kc, c=c, pa=pa: e.matmul(
                    pa[:, 0:T2], lhsT=wbra[:, kc, c * 128:(c + 1) * 128], rhs=oa[:, kc, :],
                    start=(kc == 0), stop=(kc == 7)) for kc in range(8)], [oatk], [patk])
                sc.op("pe", [lambda e, kc=kc, c=c, pb=pb: e.matmul(
                    pb[:, 0:T2], lhsT=wbrb[:, kc, c * 128:(c + 1) * 128], rhs=obT[:, kc, so:so + T2],
                    start=(kc == 0), stop=(kc == 1)) for kc in range(2)], [obtk], [pbtk])
                t1, t1tk = t1r.next()
                t2, t2tk = t2r.next()
                sc.op("dve", lambda e, t1=t1, pa=pa, c=c: e.tensor_tensor(out=t1[:], in0=pa[:, 0:T2],
                                                                          in1=g_t[:, c, :], op=ALU.mult),
                      [patk, gtk], [t1tk])
                sc.op("dve", lambda e, t2=t2, pb=pb, c=c: e.tensor_tensor(out=t2[:], in0=pb[:, 0:T2],
                                                                          in1=g_t[:, 8 + c, :], op=ALU.mult),
                      [pbtk, gtk], [t2tk])
                sc.op("pool", lambda e, t1=t1, t2=t2, c=c: e.tensor_tensor(out=mx[:, c, :], in0=t1[:], in1=t2[:],
                                                                            op=ALU.add), [t1tk, t2tk], [mxtk])
            for bi in range(T2 // 128):
                r0 = tok0 + bi * 128
                x_t, xtk_ = xr.next()
                sc.dma("sp", x_t[:], x_d[r0:r0 + 128, :], xtk_, writes=[xtk_])
                pr, prtk = pre.next()
                for n in range(2):
                    ps, pstk = ps_ring.next()
                    sc.op("pe", [lambda e, kc=kc, n=n, ps=ps, bi=bi: e.matmul(
                        ps[:, 0:512], lhsT=mx[:, kc, bi * 128:(bi + 1) * 128], rhs=wout[:, kc, n * 512:(n + 1) * 512],
                        start=(kc == 0), stop=(kc == 7)) for kc in range(8)], [mxtk], [pstk])
                    sc.op("dve", lambda e, n=n, ps=ps, pr=pr, x_t=x_t: e.scalar_tensor_tensor(
                        out=pr[:, n * 512:(n + 1) * 512], in0=x_t[:, n * 512:(n + 1) * 512], scalar=ALPHA,
                        in1=ps[:, 0:512], op0=ALU.mult, op1=ALU.add), [pstk, xtk_], [prtk])
                hs_, hstk = hst.next()
                sm, smtk = small.next()
                layer_norm(pr[:], prtk, ln_g[:], ln_b[:], hs_[:], hstk, sm, smtk)
                sc.dma(STQ, h1_d[r0:r0 + 128, :], hs_[:], hstk, reads=[hstk])

        NBT = T2 // 128
        for sbi in range(S // SB):
            mixer_b(sbi)
            units = []
            for tt in range(sbi * (SB // T2), (sbi + 1) * (SB // T2)):
                oa, oatk = oaT.next()
                for bi in range(NBT):
                    for k in range(4):
                        units.append({"i": tt * NBT + bi, "k": k, "bi": bi, "tt": tt, "oa": oa, "oatk": oatk})
            n = len(units)
            a_prep(units[0]["i"])
            a_S(units[0])
            for j in range(n):
                u = units[j]
                if u["k"] == 1 and u["i"] + 1 < (sbi + 1) * (SB // 128):
                    a_prep(u["i"] + 1)
                if j + 1 < n:
                    a_S(units[j + 1])
                a_PV(u)
                if u["k"] == 3 and u["bi"] == NBT - 1:
                    dense2(u["tt"], u["oa"], u["oatk"], sbi)
        sc.barrier()

    if 2 in phases:
        with contextlib.ExitStack() as st:
            phase2(st)

    def phase3(st):
      if True:
        wst = Ring([(sb(st, "wst%d" % i, [128, 1024], F32), Tk("wst%d" % i)) for i in range(2)])
        ident, masks, ones, ctk = consts(st, wst)
        wfg, _ = load_weight(st, "wfg", wffg_d, 8, DFF, wst)
        wfu, _ = load_weight(st, "wfu", wffu_d, 8, DFF, wst)
        wfd, _ = load_weight(st, "wfd", wffd_d, NFC, D, wst)
        ln_g, lgtk = bcast_load(st, "ln2g", ln2g_d)
        ln_b, lbtk = bcast_load(st, "ln2b", ln2b_d)
        T3 = 256
        hr = Ring([(sb(st, "hr%d" % i, [128, D], F32), Tk("hr%d" % i)) for i in range(4)])
        hbf = Ring([(sb(st, "hbf%d" % i, [128, D], BF16), Tk("hbf%d" % i)) for i in range(2)])
        hT = [(sb(st, "hT%d" % i, [128, 8, T3], BF16), [Tk("hT%d_%d" % (i, b)) for b in range(2)]) for i in range(2)]
        sg = Ring([(sb(st, "sg%d" % i, [128, T3], F32), Tk("sg%d" % i)) for i in range(2)])
        guT = Ring([(sb(st, "guT%d" % i, [128, NFC, T3], BF16), Tk("guT%d" % i)) for i in range(1)])
        pre = Ring([(sb(st, "pre3_%d" % i, [128, D], F32), Tk("pre3_%d" % i)) for i in range(2)])
        ost = Ring([(sb(st, "ost%d" % i, [128, D], F32), Tk("ost%d" % i)) for i in range(2)])
        small = Ring([(sb(st, "sm3_%d" % i, [128, 16], F32), Tk("sm3_%d" % i)) for i in range(2)])
        sc.barrier()
        NT3 = S // T3
        hrs = {}

        def prep3(t):
            ht, httk = hT[t % 2]
            for b in range(2):
                h_t, htk = hr.next()
                hrs[(t, b)] = (h_t, htk)
                r0 = t * T3 + b * 128
                sc.dma("sp", h_t[:], h1_d[r0:r0 + 128, :], htk, writes=[htk])
                hb, hbtk = hbf.next()
                sc.op("pool", lambda e, hb=hb, h_t=h_t: e.tensor_copy(out=hb[:], in_=h_t[:]), [htk], [hbtk])
                pt, pttk = pt_ring.next()
                sc.op("pe", [lambda e, kc=kc, pt=pt, hb=hb: e.transpose(out=pt[:, kc * 128:(kc + 1) * 128],
                                                                        in_=hb[:, kc * 128:(kc + 1) * 128],
                                                                        identity=ident[:]) for kc in range(8)],
                      [hbtk], [pttk])
                sc.op("act", lambda e, pt=pt, b=b: e.activation(
                    out=ht[:, :, b * 128:(b + 1) * 128], in_=pt[:].rearrange("p (k t) -> p k t", k=8), func=AF.Copy),
                    [pttk], [httk[b]])

        def compute3(t):
            ht, httk = hT[t % 2]
            gu, gutk = guT.next()
            for c in range(NFC):
                (pg, pgtk), (pu, putk) = ps_ring.next(), ps_ring.next()
                sc.op("pe", [lambda e, kc=kc, c=c, pg=pg: e.matmul(
                    pg[:, 0:T3], lhsT=wfg[:, kc, c * 128:(c + 1) * 128], rhs=ht[:, kc, :],
                    start=(kc == 0), stop=(kc == 7)) for kc in range(8)], httk, [pgtk])
                sc.op("pe", [lambda e, kc=kc, c=c, pu=pu: e.matmul(
                    pu[:, 0:T3], lhsT=wfu[:, kc, c * 128:(c + 1) * 128], rhs=ht[:, kc, :],
                    start=(kc == 0), stop=(kc == 7)) for kc in range(8)], httk, [putk])
                s_t, stk_ = sg.next()
                sc.op("act", lambda e, s_t=s_t, pg=pg: e.activation(out=s_t[:], in_=pg[:, 0:T3], func=AF.Silu),
                      [pgtk], [stk_])
                sc.op("dve", lambda e, s_t=s_t, pu=pu, c=c: e.tensor_tensor(out=gu[:, c, :], in0=pu[:, 0:T3],
                                                                            in1=s_t[:], op=ALU.mult),
                      [putk, stk_], [gutk])
            for b in range(2):
                h_t, htk = hrs.pop((t, b))
                r0 = t * T3 + b * 128
                pr, prtk = pre.next()
                for n in range(2):
                    ps, pstk = ps_ring.next()
                    sc.op("pe", [lambda e, c=c, n=n, ps=ps, b=b: e.matmul(
                        ps[:, 0:512], lhsT=gu[:, c, b * 128:(b + 1) * 128], rhs=wfd[:, c, n * 512:(n + 1) * 512],
                        start=(c == 0), stop=(c == NFC - 1)) for c in range(NFC)], [gutk], [pstk])
                    sc.op("dve", lambda e, n=n, ps=ps, pr=pr, h_t=h_t: e.scalar_tensor_tensor(
                        out=pr[:, n * 512:(n + 1) * 512], in0=h_t[:, n * 512:(n + 1) * 512], scalar=ALPHA,
                        in1=ps[:, 0:512], op0=ALU.mult, op1=ALU.add), [pstk, htk], [prtk])
                o_t, otk = ost.next()
                sm, smtk = small.next()
                layer_norm(pr[:], prtk, ln_g[:], ln_b[:], o_t[:], otk, sm, smtk)
                sc.dma(STQ, out_d[r0:r0 + 128, :], o_t[:], otk, reads=[otk])

        prep3(0)
        for t in range(NT3):
            if t + 1 < NT3:
                prep3(t + 1)
            compute3(t)
        sc.barrier()

    if 3 in phases:
        with contextlib.ExitStack() as st:
            phase3(st)

    sc.emit()
    return nc


def host_consts(S):
    half = 8
    inv = (500000.0 ** (-(np.arange(half, dtype=np.float32)) / np.float32(half))).astype(np.float32)
    ang = np.arange(S, dtype=np.float32)[:, None] * inv[None, :]
    cos = np.cos(ang).astype(np.float32)
    sin = np.sin(ang).astype(np.float32)
    csn = np.concatenate([cos, cos, -sin, sin], axis=1).astype(np.float32)
    kk = np.arange(128)[:, None]
    qq = np.arange(128)[None, :]
    m = np.zeros((128, 4, 128), np.float32)
    m[:, 0] = kk >= qq
    m[:, 1] = kk <= qq
    m[:, 2] = (kk >= qq) & (kk >= 64)
    m[:, 3] = (kk <= qq) & (kk < 64)
    ident = np.eye(128, dtype=np.float32)
    return csn, m, ident


def win_perm():
    cols = []
    for p in range(2):
        for m_ in range(4):
            for half in range(2):
                h = 4 * (2 * p + half) + m_
                cols.extend(range(h * 64, (h + 1) * 64))
    cols.extend(range(1024, 1280))
    cols.extend(range(1536, 2304))
    cols.extend(range(2304, 3072))
    cols.extend(range(1280, 1536))
    cols.extend(range(3072, 3840))
    return np.asarray(cols)


def make_in_maps(inputs, S, ncores):
    f = lambda a: np.ascontiguousarray(np.asarray(a, dtype=np.float32))
    csn, m, ident = host_consts(S)
    shared = {
        "w_in": f(np.asarray(inputs["w_in"])[0][:, win_perm()]),
        "w_gate": f(inputs["w_gate"][0]),
        "bg": f(np.asarray(inputs["b_gate"])[0].reshape(16, 128).T),
        "a_sink": f(inputs["a_sink"][0]),
        "w_br_a": f(inputs["w_br_a"][0]),
        "w_br_b": f(inputs["w_br_b"][0]),
        "w_out": f(inputs["w_out"][0]),
        "ln1_g": f(inputs["ln1_g"][0]),
        "ln1_b": f(inputs["ln1_b"][0]),
        "w_ff_gate": f(inputs["w_ff_gate"][0]),
        "w_ff_up": f(inputs["w_ff_up"][0]),
        "w_ff_down": f(inputs["w_ff_down"][0]),
        "ln2_g": f(inputs["ln2_g"][0]),
        "ln2_b": f(inputs["ln2_b"][0]),
        "csn": csn, "masks": m, "ident": ident,
    }
    x = np.asarray(inputs["x"])
    maps = []
    for c in range(ncores):
        d = dict(shared)
        d["x"] = f(x[c, :S])
        maps.append(d)
    return maps


_NC_CACHE = {}


def kernel(**inputs):
    S = 8192
    n = 8
    if S not in _NC_CACHE:
        _NC_CACHE[S] = build(S)
    nc = _NC_CACHE[S]
    in_maps = make_in_maps(inputs, S, n)
    res = run_bass_kernel_spmd(nc, in_maps, core_ids=list(range(n)))
    out = np.stack([np.asarray(r["out"], dtype=np.float32).reshape(S, D) for r in res.results], axis=0)
    return out
```

```python
import contextlib
import numpy as np
import concourse.bass as bass
import concourse.mybir as mybir
from concourse.bass_utils import run_bass_kernel_spmd

F32 = mybir.dt.float32
BF16 = mybir.dt.bfloat16
AF = mybir.ActivationFunctionType
ALU = mybir.AluOpType

D = 1024
HD = 64
IN_W = 3840
ROW = 4864
C_QA, C_KA, C_QB, C_KB, C_VA, C_VB = 0, 1024, 1280, 2048, 2816, 3328
DFF = 2816
NFC = DFF // 128
ALPHA = 2.0 ** 0.25
EPS = 1e-5
SB = 2048
ENG = ("pe", "act", "dve", "pool", "sp")
import os as _os0
STQ = _os0.environ.get("STQ", "pool")


class Tk:
    __slots__ = ("w", "r", "sem", "name")

    def __init__(self, name="", sem=None):
        self.w = None
        self.r = {}
        self.sem = sem
        self.name = name


class Sched:
    def __init__(self, nc):
        self.nc = nc
        self.ops = {e: [] for e in ENG}
        self.cnt = {e: 0 for e in ENG}
        self.waited = {e: {} for e in ENG}
        self.dma_tot = {}
        self.sems = {}
        for e in ("pe", "act", "dve", "pool"):
            self.sems[e] = nc.alloc_semaphore(name="sem_" + e)
        self.ndma = 0

    def dma_sem(self):
        k = "dma%d" % self.ndma
        self.ndma += 1
        self.sems[k] = self.nc.alloc_semaphore(name=k)
        self.dma_tot[k] = 0
        return k

    def _deps(self, eng, reads, writes, pe_accum, skip_same_w=False):
        deps = {}

        def add(ev):
            if ev is None:
                return
            s, v = ev
            if deps.get(s, 0) < v:
                deps[s] = v

        for t in reads:
            add(t.w)
        for t in writes:
            if not ((pe_accum and t.w is not None and t.w[0] == "pe") or
                    (skip_same_w and t.w is not None and t.w[0] == eng)):
                add(t.w)
            for s, v in t.r.items():
                add((s, v))
        waits = []
        wd = self.waited[eng]
        for s, v in deps.items():
            if s == eng and eng in ("pe", "sp"):
                continue
            if wd.get(s, 0) < v:
                wd[s] = v
                waits.append((s, v))
        return waits

    def _mark(self, ev, reads, writes):
        s, v = ev
        for t in reads:
            if t.r.get(s, 0) < v:
                t.r[s] = v
        for t in writes:
            t.w = ev
            t.r = {}

    def op(self, eng, fns, reads=(), writes=(), pe_accum=False, ssw=False):
        if not isinstance(fns, (list, tuple)):
            fns = [fns]
        waits = self._deps(eng, reads, writes, pe_accum, ssw)
        self.cnt[eng] += 1
        ev = (eng, self.cnt[eng])
        self.ops[eng].append((waits, list(fns), (eng, 1)))
        self._mark(ev, reads, writes)
        return ev

    def dma(self, q, out, in_, tile, reads=(), writes=()):
        if tile.sem is None:
            tile.sem = self.dma_sem()
        waits = self._deps(q, reads, writes, False)
        self.dma_tot[tile.sem] += 16
        ev = (tile.sem, self.dma_tot[tile.sem])
        self.ops[q].append((waits, [lambda e, o=out, i=in_: e.dma_start(out=o, in_=i)], (tile.sem, 16)))
        self._mark(ev, reads, writes)
        return ev

    def barrier(self):
        for e in ENG:
            waits = []
            wd = self.waited[e]
            for f in ("pe", "act", "dve", "pool"):
                if f != e and self.cnt[f] > wd.get(f, 0):
                    wd[f] = self.cnt[f]
                    waits.append((f, self.cnt[f]))
            for k, v in self.dma_tot.items():
                if v > wd.get(k, 0):
                    wd[k] = v
                    waits.append((k, v))
            self.ops[e].append((waits, [], None))

    def emit(self):
        nc = self.nc
        with nc.Block() as block:
            def run(name, eng):
                for waits, fns, inc in self.ops[name]:
                    for s, v in waits:
                        eng.wait_ge(self.sems[s], v)
                    ins = None
                    for f in fns:
                        ins = f(eng)
                    if inc is not None and ins is not None:
                        ins.then_inc(self.sems[inc[0]], inc[1])

            @block.tensor
            def _(e):
                run("pe", e)

            @block.scalar
            def _(e):
                run("act", e)

            @block.vector
            def _(e):
                run("dve", e)

            @block.gpsimd
            def _(e):
                run("pool", e)

            @block.sync
            def _(e):
                run("sp", e)


def sl(start, n, step):
    return slice(start, start + step * (n - 1) + 1, step)


class Ring:
    def __init__(self, items):
        self.items = items
        self.i = 0

    def next(self):
        it = self.items[self.i % len(self.items)]
        self.i += 1
        return it


def build(S=8192, dbg=False, phases=(1, 2, 3)):
    nc = bass.Bass("TRN2", target_bir_lowering=False)
    sc = Sched(nc)
    NB = S // 128

    def din(name, shape, dt=F32):
        return nc.dram_tensor(name, list(shape), dt, kind="ExternalInput").ap()

    x_d = din("x", [S, D])
    win_d = din("w_in", [D, IN_W])
    wgate_d = din("w_gate", [D, 2048])
    bg_d = din("bg", [128, 16])
    sink_d = din("a_sink", [16])
    wbra_d = din("w_br_a", [1024, D])
    wbrb_d = din("w_br_b", [256, D])
    wout_d = din("w_out", [D, D])
    ln1g_d = din("ln1_g", [D])
    ln1b_d = din("ln1_b", [D])
    wffg_d = din("w_ff_gate", [D, DFF])
    wffu_d = din("w_ff_up", [D, DFF])
    wffd_d = din("w_ff_down", [DFF, D])
    ln2g_d = din("ln2_g", [D])
    ln2b_d = din("ln2_b", [D])
    csn_d = din("csn", [S, 32])
    masks_d = din("masks", [128, 4, 128])
    ident_d = din("ident", [128, 128])

    skind = "ExternalOutput" if dbg else "Internal"
    qkv_d = nc.dram_tensor("qkv_s", [S, ROW], BF16, kind=skind).ap()
    gT_d = nc.dram_tensor("gT_s", [16, 128, S], BF16, kind=skind).ap()
    h1_d = nc.dram_tensor("h1_s", [S, D], F32, kind=skind).ap()
    out_d = nc.dram_tensor("out", [S, D], F32, kind="ExternalOutput").ap()

    psb = [nc.alloc_psum_tensor("ps%d" % i, [128, 512], F32) for i in range(6)]
    ptb = [nc.alloc_psum_tensor("pt%d" % i, [128, 1024], BF16) for i in range(2)]
    ps_ring = Ring([(psb[i], Tk("ps%d" % i)) for i in range(6)])
    pt_ring = Ring([(ptb[i], Tk("pt%d" % i)) for i in range(2)])

    uid = [0]

    def sb(st, name, shape, dt):
        uid[0] += 1
        return st.enter_context(nc.sbuf_tensor("sb%d_%s" % (uid[0], name), list(shape), dt))

    rr = {"cast": 0}

    def cast_any(out, in_, reads, writes, engs=("dve", "pool", "act")):
        e = engs[rr["cast"] % len(engs)]
        rr["cast"] += 1
        if e == "act":
            sc.op("act", lambda g: g.activation(out=out, in_=in_, func=AF.Copy), reads, writes)
        else:
            sc.op(e, lambda g: g.tensor_copy(out=out, in_=in_), reads, writes)

    def load_weight(st, name, src, KC, N, wst):
        w = sb(st, name, [128, KC, N], BF16)
        tk = Tk(name)
        for kc in range(KC):
            for n0 in range(0, N, 1024):
                n1 = min(N, n0 + 1024)
                stg, stk = wst.next()
                sc.dma("sp", stg[:, 0:n1 - n0], src[kc * 128:(kc + 1) * 128, n0:n1], stk, writes=[stk])
                cast_any(w[:, kc, n0:n1], stg[:, 0:n1 - n0], [stk], [tk])
        return w, tk

    def consts(st, wst):
        ident = sb(st, "ident", [128, 128], BF16)
        masks = sb(st, "masks", [128, 4, 128], BF16)
        ones = sb(st, "ones", [128, 128], BF16)
        ctk = Tk("consts")
        stg, stk = wst.next()
        sc.dma("sp", stg[:, 0:128], ident_d, stk, writes=[stk])
        sc.op("dve", lambda g: g.tensor_copy(out=ident[:], in_=stg[:, 0:128]), [stk], [ctk])
        stg2, stk2 = wst.next()
        sc.dma("sp", stg2[:, 0:512], masks_d.rearrange("p a b -> p (a b)"), stk2, writes=[stk2])
        sc.op("dve", lambda g: g.tensor_copy(out=masks[:].rearrange("p a b -> p (a b)"), in_=stg2[:, 0:512]),
              [stk2], [ctk])
        sc.op("dve", lambda g: g.memset(ones[:], 1.0), [], [ctk])
        return ident, masks, ones, ctk

    def layer_norm(pre, ptk, g_t, b_t, outt, otk, small, smtk):
        stats = small[:, 0:12]
        mv = small[:, 12:14]
        rstd = small[:, 14:15]
        sc.op("dve", lambda g: g.bn_stats(out=small[:, 0:6], in_=pre[:, 0:512]), [ptk], [smtk])
        sc.op("dve", lambda g: g.bn_stats(out=small[:, 6:12], in_=pre[:, 512:1024]), [ptk], [smtk])
        sc.op("dve", lambda g: g.bn_aggr(out=mv, in_=stats), [smtk], [smtk])
        sc.op("dve", lambda g: g.tensor_scalar_add(out=rstd, in0=small[:, 13:14], scalar1=EPS), [smtk], [smtk])
        sc.op("act", lambda g: g.activation(out=rstd, in_=rstd, func=AF.Ln), [smtk], [smtk])
        sc.op("act", lambda g: g.activation(out=rstd, in_=rstd, func=AF.Exp, scale=-0.5), [smtk], [smtk])
        sc.op("dve", lambda g: g.tensor_scalar(out=pre, in0=pre, scalar1=small[:, 12:13], scalar2=rstd,
                                               op0=ALU.subtract, op1=ALU.mult), [ptk, smtk], [ptk])
        sc.op("pool", lambda g: g.tensor_mul(out=pre, in0=pre, in1=g_t), [ptk], [ptk])
        sc.op("pool", lambda g: g.tensor_add(out=outt, in0=pre, in1=b_t), [ptk], [otk])

    def bcast_load(st, name, src, wst_unused=None):
        t = sb(st, name, [128, D], F32)
        tk = Tk(name)
        sc.dma("sp", t[:], src.partition_broadcast(128), tk, writes=[tk])
        return t, tk

    def phase1(st):
      if True:
        wst = Ring([(sb(st, "wst%d" % i, [128, 1024], F32), Tk("wst%d" % i)) for i in range(3)])
        ident, masks, ones, ctk = consts(st, wst)
        w_bf, wtk = load_weight(st, "w_in_bf", win_d, 8, IN_W, wst)
        wg_bf, wgtk = load_weight(st, "w_g_bf", wgate_d, 8, 2048, wst)
        bg = sb(st, "bg", [128, 16], F32)
        bgtk = Tk("bg")
        sc.dma("sp", bg[:], bg_d, bgtk, writes=[bgtk])
        sc.barrier()

        xin = Ring([(sb(st, "xin%d" % i, [128, D], F32), Tk("xin%d" % i)) for i in range(3)])
        xbf = Ring([(sb(st, "xbf%d" % i, [128, D], BF16), Tk("xbf%d" % i)) for i in range(2)])
        xT = [(sb(st, "xT%d" % i, [128, 8, 512], BF16), [Tk("xT%d_%d" % (i, b)) for b in range(4)]) for i in range(2)]
        cs = [(sb(st, "cs%d" % i, [128, 4, 32], F32), Tk("cs%d" % i)) for i in range(2)]
        stq = Ring([(sb(st, "stq%d" % i, [128, ROW], BF16), [Tk("stq%d_%d" % (i, n)) for n in range(8)])
                    for i in range(2)])
        rot = Ring([(sb(st, "rot%d" % i, [128, 44, 16], F32), Tk("rot%d" % i)) for i in range(2)])
        gst = Ring([(sb(st, "gst%d" % i, [128, 4, 512], BF16), Tk("gst%d" % i)) for i in range(2)])
        rA = Ring([(sb(st, "rA%d" % i, [128, 44, 16], F32), Tk("rA%d" % i)) for i in range(2)])
        rB = Ring([(sb(st, "rB%d" % i, [128, 44, 16], F32), Tk("rB%d" % i)) for i in range(2)])
        NT = S // 512

        def prep(t):
            xt, xtk = xT[t % 2]
            c_t, c_tk = cs[t % 2]
            sc.dma("sp", c_t[:], csn_d[t * 512:(t + 1) * 512, :].rearrange("(b p) c -> p b c", p=128), c_tk,
                   writes=[c_tk])
            for b in range(4):
                xi, xitk = xin.next()
                sc.dma("sp", xi[:], x_d[t * 512 + b * 128: t * 512 + (b + 1) * 128, :], xitk, writes=[xitk])
                xb, xbtk = xbf.next()
                sc.op("dve", lambda g, o=xb, i=xi: g.tensor_copy(out=o[:], in_=i[:]), [xitk], [xbtk])
                pt, pttk = pt_ring.next()
                sc.op("pe", [lambda g, o=pt, i=xb, kc=kc: g.transpose(out=o[:, kc * 128:(kc + 1) * 128],
                                                                       in_=i[:, kc * 128:(kc + 1) * 128],
                                                                       identity=ident[:])
                             for kc in range(8)], [xbtk], [pttk])
                sc.op("act", lambda g, o=xt, i=pt, b=b: g.activation(
                    out=o[:, :, b * 128:(b + 1) * 128], in_=i[:].rearrange("p (k t) -> p k t", k=8), func=AF.Copy),
                    [pttk], [xtk[b]])

        def v_chunk(ps, pstk, pcol0, nh, sq, sqtk, col0):
            pv = ps[:, pcol0:pcol0 + nh * 64].rearrange("p (h d) -> p h d", d=64).unsqueeze(2).broadcast_to(
                [128, nh, 2, 64])
            ov = sq[:, col0:col0 + nh * 128].rearrange("p (h r d) -> p h r d", r=2, d=64)
            sc.op("act", lambda g: g.activation(out=ov, in_=pv, func=AF.Copy), [pstk], [sqtk])

        def compute(t):
            xt, xtk = xT[t % 2]
            c_t, c_tk = cs[t % 2]
            for b in range(4):
                sq, sqtks = stq.next()
                rt_, rtk = rot.next()
                for n in range(8):
                    wdt = 512 if n < 7 else 256
                    ps, pstk = ps_ring.next()
                    sc.op("pe", [lambda g, o=ps, kc=kc, n=n, wdt=wdt, b=b: g.matmul(
                        o[:, 0:wdt], lhsT=xt[:, kc, b * 128:(b + 1) * 128], rhs=w_bf[:, kc, n * 512:n * 512 + wdt],
                        start=(kc == 0), stop=(kc == 7)) for kc in range(8)], [xtk[b]], [pstk])
                    nh = 8 if n <= 4 else (4 if n == 5 else 0)
                    if nh:
                        sc.op("act", lambda g, ps=ps, sq=sq, n=n, nh=nh: g.activation(
                            out=sq[:, n * 512:n * 512 + nh * 64], in_=ps[:, 0:nh * 64], func=AF.Copy),
                            [pstk], [sqtks[n]])
                        sc.op("act", lambda g, ps=ps, rt_=rt_, n=n, nh=nh: g.activation(
                            out=rt_[:, 8 * n:8 * n + nh, :],
                            in_=ps[:, 0:nh * 64].rearrange("p (h d) -> p h d", d=64)[:, :, 0:16], func=AF.Copy),
                            [pstk], [rtk])
                    if n == 5:
                        v_chunk(ps, pstk, 256, 4, sq, sqtks[n], C_VA)
                    elif n == 6:
                        v_chunk(ps, pstk, 0, 8, sq, sqtks[n], C_VB)
                    elif n == 7:
                        v_chunk(ps, pstk, 0, 4, sq, sqtks[n], C_VB + 1024)
                a_t, atk = rA.next()
                b_t, btk = rB.next()
                cc = c_t[:, b, 0:16].unsqueeze(1).broadcast_to([128, 44, 16])
                ns = c_t[:, b, 16:24].unsqueeze(1).broadcast_to([128, 44, 8])
                ps_ = c_t[:, b, 24:32].unsqueeze(1).broadcast_to([128, 44, 8])
                sc.op("dve", lambda g, a_t=a_t, rt_=rt_, cc=cc: g.tensor_tensor(
                    out=a_t[:], in0=rt_[:], in1=cc, op=ALU.mult), [rtk, c_tk], [atk])
                sc.op("pool", lambda g, b_t=b_t, rt_=rt_, ns=ns: g.tensor_tensor(
                    out=b_t[:, :, 0:8], in0=rt_[:, :, 8:16], in1=ns, op=ALU.mult), [rtk, c_tk], [btk])
                sc.op("pool", lambda g, b_t=b_t, rt_=rt_, ps_=ps_: g.tensor_tensor(
                    out=b_t[:, :, 8:16], in0=rt_[:, :, 0:8], in1=ps_, op=ALU.mult), [rtk, c_tk], [btk])
                ov = sq[:, 0:2816].rearrange("p (h d) -> p h d", d=64)[:, :, 0:16]
                sc.op("dve", lambda g, ov=ov, a_t=a_t, b_t=b_t: g.tensor_tensor(
                    out=ov, in0=a_t[:], in1=b_t[:], op=ALU.add), [atk, btk], sqtks[0:6])
                r0 = t * 512 + b * 128
                sc.dma(STQ, qkv_d[r0:r0 + 128, :], sq[:], sqtks[0], reads=sqtks)
            for gc in range(16):
                ps, pstk = ps_ring.next()
                sc.op("pe", [lambda g, o=ps, kc=kc, gc=gc: g.matmul(
                    o[:, 0:512], lhsT=wg_bf[:, kc, gc * 128:(gc + 1) * 128], rhs=xt[:, kc, :],
                    start=(kc == 0), stop=(kc == 7)) for kc in range(8)], xtk, [pstk])
                if gc % 4 == 0:
                    gs, gstk = gst.next()
                sc.op("act", lambda g, o=gs, i=ps, gc=gc: g.activation(
                    out=o[:, gc % 4, :], in_=i[:, 0:512], func=AF.Sigmoid, bias=bg[:, gc:gc + 1]), [pstk, bgtk], [gstk])
                if gc % 4 == 3:
                    c0 = gc - 3
                    sc.dma(STQ, gT_d[c0:c0 + 4, :, t * 512:(t + 1) * 512].rearrange("c p t -> p c t"), gs[:],
                           gstk, reads=[gstk])

        import os
        cut = int(os.environ.get("P1CUT", "9"))
        if cut >= 1:
            prep(0)
        for t in range(NT if cut >= 2 else 0):
            if t + 1 < NT:
                prep(t + 1)
            compute(t)
        sc.barrier()

    if 1 in phases:
        with contextlib.ExitStack() as st:
            phase1(st)

    def phase2(st):
      if True:
        wst = Ring([(sb(st, "wst%d" % i, [128, 1024], F32), Tk("wst%d" % i)) for i in range(2)])
        ident, masks, ones, ctk = consts(st, wst)
        wbra, _ = load_weight(st, "wbra", wbra_d, 8, D, wst)
        wbrb, _ = load_weight(st, "wbrb", wbrb_d, 2, D, wst)
        wout, _ = load_weight(st, "wout", wout_d, 8, D, wst)
        ln_g, lgtk = bcast_load(st, "ln1g", ln1g_d)
        ln_b, lbtk = bcast_load(st, "ln1b", ln1b_d)
        es = sb(st, "es", [128, 16], F32)
        estk = Tk("es")
        sc.dma("sp", es[:], sink_d.partition_broadcast(128), estk, writes=[estk])
        sc.op("act", lambda g: g.activation(out=es[:], in_=es[:], func=AF.Exp), [estk], [estk])
        mp = sb(st, "mp", [128, 4, 2, 128], BF16)
        for f in range(2):
            for l in range(2):
                i = f * 2 + l
                sc.op("dve", lambda g, i=i, f=f: g.tensor_copy(out=mp[:, i, 0, :], in_=masks[:, 2 if f else 0, :]),
                      [ctk], [ctk])
                sc.op("dve", lambda g, i=i, l=l: g.tensor_copy(out=mp[:, i, 1, :], in_=masks[:, 3 if l else 1, :]),
                      [ctk], [ctk])
        accU = sb(st, "accU", [128, 2, SB], F32)
        accL = sb(st, "accL", [128, 2, SB], F32)
        acctk = Tk("acc")
        obT = sb(st, "obT", [128, 2, SB], BF16)
        obtk = Tk("obT")
        qb = [(sb(st, "qb%d" % i, [128, 256], BF16), Tk("qb%d" % i)) for i in range(3)]
        kbt = [(sb(st, "kb%d" % i, [128, 2, 256], BF16), Tk("kb%d" % i)) for i in range(3)]
        vbt = [(sb(st, "vb%d" % i, [128, 2, 512], BF16), Tk("vb%d" % i)) for i in range(3)]
        for i in range(3):
            sc.op("pool", lambda g, i=i: g.memset(qb[i][0][:], 0.0), [], [qb[i][1]])
            sc.op("pool", lambda g, i=i: g.memset(kbt[i][0][:], 0.0), [], [kbt[i][1]])
            sc.op("pool", lambda g, i=i: g.memset(vbt[i][0][:], 0.0), [], [vbt[i][1]])
        qkT = Ring([(sb(st, "qkT%d" % i, [128, 6, 128], BF16), Tk("qkT%d" % i)) for i in range(3)])
        pTb = Ring([(sb(st, "pTb%d" % i, [128, 2, 2, 2, 128], BF16), Tk("pTb%d" % i)) for i in range(2)])
        qa = Ring([(sb(st, "qa%d" % i, [128, 1024], BF16), Tk("qa%d" % i)) for i in range(2)])
        ka = [(sb(st, "ka%d" % i, [128, 256], BF16), Tk("ka%d" % i)) for i in range(4)]
        va = [(sb(st, "va%d" % i, [128, 512], BF16), Tk("va%d" % i)) for i in range(4)]
        kaT = [(sb(st, "kaT%d" % i, [128, 2, 128], BF16), Tk("kaT%d" % i)) for i in range(4)]
        qaT = Ring([(sb(st, "qaT%d" % i, [128, 8, 128], BF16), Tk("qaT%d" % i)) for i in range(2)])
        pA = Ring([(sb(st, "pA%d" % i, [128, 512], BF16), Tk("pA%d" % i)) for i in range(6)])
        rtmp = Ring([(sb(st, "rtmp%d" % i, [128, 4, 128], F32), Tk("rtmp%d" % i)) for i in range(3)])
        T2 = 256
        oaT = Ring([(sb(st, "oaT%d" % i, [128, 8, T2], BF16), Tk("oaT%d" % i)) for i in range(2)])
        gt = Ring([(sb(st, "gt%d" % i, [128, 16, T2], BF16), Tk("gt%d" % i)) for i in range(2)])
        xr = Ring([(sb(st, "xr%d" % i, [128, D], F32), Tk("xr%d" % i)) for i in range(3)])
        t1r = Ring([(sb(st, "t1_%d" % i, [128, T2], F32), Tk("t1_%d" % i)) for i in range(2)])
        t2r = Ring([(sb(st, "t2_%d" % i, [128, T2], F32), Tk("t2_%d" % i)) for i in range(2)])
        mixT = Ring([(sb(st, "mixT%d" % i, [128, 8, T2], BF16), Tk("mixT%d" % i)) for i in range(2)])
        pre = Ring([(sb(st, "pre%d" % i, [128, D], F32), Tk("pre%d" % i)) for i in range(2)])
        hst = Ring([(sb(st, "hst%d" % i, [128, D], F32), Tk("hst%d" % i)) for i in range(2)])
        small = Ring([(sb(st, "sm%d" % i, [128, 16], F32), Tk("sm%d" % i)) for i in range(2)])
        sc.barrier()

        bcount = [0]

        def b_T(u):
            sbi, g, d, r, J, j0 = u["args"]
            slot = bcount[0] % 3
            bcount[0] += 1
            q_t, qtk = qb[slot]
            k_t, ktk = kbt[slot]
            v_t, vtk = vbt[slot]
            cq = C_QB + g * 256
            ck = C_KB + g * 256
            cv = C_VB + g * 512
            t0 = r + d * j0
            sc.dma("sp", q_t[:], qkv_d[sl(t0, 128, d), cq:cq + 256], qtk, writes=[qtk])
            first = last = 0
            for kb in range(2):
                jk0 = j0 - 64 + kb * 128
                lo = max(0, -jk0)
                hi = min(128, J - jk0)
                if kb == 0 and lo > 0:
                    first = 1
                if kb == 1 and hi < 128:
                    last = 1
                ts = r + d * (jk0 + lo)
                sc.dma("sp", k_t[lo:hi, kb, :], qkv_d[sl(ts, hi - lo, d), ck:ck + 256], ktk, writes=[ktk])
                sc.dma("sp", v_t[lo:hi, kb, :], qkv_d[sl(ts, hi - lo, d), cv:cv + 512], vtk, writes=[vtk])
            pt, pttk = pt_ring.next()
            fns = []
            for c in range(2):
                fns.append(lambda e, c=c: e.transpose(out=pt[:, c * 128:(c + 1) * 128],
                                                      in_=q_t[:, c * 128:(c + 1) * 128], identity=ident[:]))
            for kb in range(2):
                for c in range(2):
                    i = 2 + kb * 2 + c
                    fns.append(lambda e, c=c, kb=kb, i=i: e.transpose(out=pt[:, i * 128:(i + 1) * 128],
                                                                      in_=k_t[:, kb, c * 128:(c + 1) * 128],
                                                                      identity=ident[:]))
            sc.op("pe", fns, [qtk, ktk], [pttk])
            qk, qktk = qkT.next()
            sc.op("act", lambda e: e.activation(out=qk[:].rearrange("p a b -> p (a b)"), in_=pt[:, 0:768],
                                                func=AF.Copy), [pttk], [qktk])
            u.update(qk=qk, qktk=qktk, v_t=v_t, vtk=vtk, first=first, last=last, t0=t0)

        def b_S(u):
            qk, qktk = u["qk"], u["qktk"]
            sbank = [ps_ring.next(), ps_ring.next()]
            for half in range(2):
                ps, pstk = sbank[half]
                fns = []
                for c in range(2):
                    for kb in range(2):
                        fns.append(lambda e, c=c, kb=kb, half=half, ps=ps: e.matmul(
                            ps[:, (c * 2 + kb) * 128:(c * 2 + kb + 1) * 128],
                            lhsT=qk[half * 64:(half + 1) * 64, 2 + kb * 2 + c, :],
                            rhs=qk[half * 64:(half + 1) * 64, c, :], start=True, stop=True))
                sc.op("pe", fns, [qktk], [pstk])
            p_t, ptk_ = pTb.next()
            for half in range(2):
                ps, pstk = sbank[half]
                sc.op("act", lambda e, half=half, ps=ps: e.activation(
                    out=p_t[:, half].rearrange("p c k q -> p (c k q)"), in_=ps[:, 0:512], func=AF.Exp, scale=0.125),
                    [pstk], [ptk_], ssw=True)
            pv4 = p_t[:].rearrange("p h c k q -> p (h c) k q")
            mview = mp[:, u["first"] * 2 + u["last"]].unsqueeze(1).broadcast_to([128, 4, 2, 128])
            sc.op("dve", lambda e: e.tensor_tensor(out=pv4, in0=pv4, in1=mview, op=ALU.mult), [ptk_, ctk], [ptk_])
            u.update(p_t=p_t, ptk_=ptk_)

        def b_PV(u):
            sbi, g, d, r, J, j0 = u["args"]
            p_t, ptk_, v_t, vtk = u["p_t"], u["ptk_"], u["v_t"], u["vtk"]
            (po, potk), (pl, pltk) = ps_ring.next(), ps_ring.next()
            fo, fl = [], []
            for h in range(4):
                c, half = h // 2, h % 2
                for kb in range(2):
                    fo.append(lambda e, h=h, c=c, half=half, kb=kb: e.matmul(
                        po[:, h * 128:(h + 1) * 128], lhsT=v_t[:, kb, h * 128:(h + 1) * 128],
                        rhs=p_t[:, half, c, kb, :], start=(kb == 0), stop=(kb == 1)))
                    fl.append(lambda e, h=h, c=c, half=half, kb=kb: e.matmul(
                        pl[:, h * 128:(h + 1) * 128], lhsT=ones[:],
                        rhs=p_t[:, half, c, kb, :], start=(kb == 0), stop=(kb == 1)))
            sc.op("pe", fo, [ptk_, vtk], [potk])
            sc.op("pe", fl, [ptk_, ctk], [pltk])
            a0 = u["t0"] - sbi * SB
            for half in range(2):
                hs = slice(half * 64, (half + 1) * 64)
                for acc, ps, pstk in ((accU, po, potk), (accL, pl, pltk)):
                    av = acc[hs, :, sl(a0, 128, d)]
                    pv = ps[hs, :].rearrange("p (c x q) -> p c x q", c=2, x=2)[:, :, half, :]
                    sc.op("dve", lambda e, av=av, pv=pv: e.tensor_tensor(out=av, in0=av, in1=pv, op=ALU.add),
                          [acctk, pstk], [acctk], ssw=True)

        def mixer_b(sbi):
            sc.op("pool", lambda e: e.memset(accU[:], 0.0), [], [acctk])
            sc.op("pool", lambda e: e.memset(accL[:], 0.0), [], [acctk])
            units = []
            for g, d in enumerate((1, 4, 16)):
                J = S // d
                nj = SB // d // 128
                for r in range(d):
                    for jb in range(nj):
                        units.append({"args": (sbi, g, d, r, J, sbi * (SB // d) + jb * 128)})
            n = len(units)
            b_T(units[0])
            b_T(units[1])
            b_S(units[0])
            for i in range(n):
                if i + 2 < n:
                    b_T(units[i + 2])
                if i + 1 < n:
                    b_S(units[i + 1])
                b_PV(units[i])
            sc.op("act", lambda e: e.activation(out=accL[:], in_=accL[:], func=AF.Ln), [acctk], [acctk])
            sc.op("act", lambda e: e.activation(out=accL[:], in_=accL[:], func=AF.Exp, scale=-1.0), [acctk], [acctk])
            sc.op("dve", lambda e: e.tensor_tensor(out=obT[:], in0=accU[:], in1=accL[:], op=ALU.mult),
                  [acctk], [obtk])

        def a_load_kv(blk):
            k_t, ktk = ka[blk % 4]
            v_t, vtk = va[blk % 4]
            sc.dma("sp", k_t[:], qkv_d[blk * 128:(blk + 1) * 128, C_KA:C_KA + 256], ktk, writes=[ktk])
            sc.dma("sp", v_t[:], qkv_d[blk * 128:(blk + 1) * 128, C_VA:C_VA + 512], vtk, writes=[vtk])
            pt, pttk = pt_ring.next()
            sc.op("pe", [lambda e, c=c: e.transpose(out=pt[:, c * 128:(c + 1) * 128], in_=k_t[:, c * 128:(c + 1) * 128],
                                                    identity=ident[:]) for c in range(2)], [ktk], [pttk])
            kt_t, kttk = kaT[blk % 4]
            sc.op("act", lambda e: e.activation(out=kt_t[:].rearrange("p a b -> p (a b)"), in_=pt[:, 0:256],
                                                func=AF.Copy), [pttk], [kttk])

        blkctx = {}

        def a_prep(i):
            if i == 0:
                a_load_kv(0)
            if i + 1 < NB:
                a_load_kv(i + 1)
            q_t, qtk = qa.next()
            sc.dma("sp", q_t[:], qkv_d[i * 128:(i + 1) * 128, 0:1024], qtk, writes=[qtk])
            pt, pttk = pt_ring.next()
            sc.op("pe", [lambda e, c=c: e.transpose(out=pt[:, c * 128:(c + 1) * 128], in_=q_t[:, c * 128:(c + 1) * 128],
                                                    identity=ident[:]) for c in range(8)], [qtk], [pttk])
            qT, qTtk = qaT.next()
            sc.op("act", lambda e: e.activation(out=qT[:].rearrange("p a b -> p (a b)"), in_=pt[:, 0:1024],
                                                func=AF.Copy), [pttk], [qTtk])
            blkctx[i] = (qT, qTtk)

        def a_S(u):
            i, k = u["i"], u["k"]
            qT, qTtk = blkctx[i]
            kbs = [kb for kb in range(3) if 0 <= i - 1 + kb < NB]
            p, half = k // 2, k % 2
            hs = slice(half * 64, (half + 1) * 64)
            plist = []
            for kb in kbs:
                blk = i - 1 + kb
                ps, pstk = ps_ring.next()
                kt_t, kttk = kaT[blk % 4]
                sc.op("pe", lambda e, ps=ps, kt_t=kt_t: e.matmul(
                    ps[:, 0:512], lhsT=kt_t[hs, p, :], rhs=qT[hs, p * 4:(p + 1) * 4, :], start=True, stop=True),
                    [kttk, qTtk], [pstk])
                pp, pptk = pA.next()
                sc.op("act", lambda e, ps=ps, pp=pp: e.activation(out=pp[:], in_=ps[:, 0:512], func=AF.Exp,
                                                                  scale=0.125), [pstk], [pptk])
                if kb != 1:
                    mv = masks[:, 0 if kb == 0 else 1, :].unsqueeze(1).broadcast_to([128, 4, 128])
                    ppv = pp[:].rearrange("p (m q) -> p m q", m=4)
                    sc.op("pool" if kb == 0 else "dve",
                          lambda e, ppv=ppv, mv=mv: e.tensor_tensor(out=ppv, in0=ppv, in1=mv, op=ALU.mult),
                          [pptk, ctk], [pptk])
                plist.append((pp, pptk, blk))
            u["plist"] = plist

        def a_PV(u):
            i, k, oa, oatk, bi = u["i"], u["k"], u["oa"], u["oatk"], u["bi"]
            plist = u["plist"]
            (po, potk), (pl, pltk) = ps_ring.next(), ps_ring.next()
            n = len(plist)
            sc.op("pe", [lambda e, j=j, pp=pp, blk=blk: e.matmul(
                po[:, 0:512], lhsT=va[blk % 4][0][:, k * 128:(k + 1) * 128], rhs=pp[:],
                start=(j == 0), stop=(j == n - 1)) for j, (pp, pptk, blk) in enumerate(plist)],
                [x[1] for x in plist] + [va[x[2] % 4][1] for x in plist], [potk])
            sc.op("pe", [lambda e, j=j, pp=pp: e.matmul(
                pl[:, 0:512], lhsT=ones[:], rhs=pp[:],
                start=(j == 0), stop=(j == n - 1)) for j, (pp, pptk, blk) in enumerate(plist)],
                [x[1] for x in plist] + [ctk], [pltk])
            rt, rttk = rtmp.next()
            esv = es[:, 4 * k:4 * k + 4].unsqueeze(2).broadcast_to([128, 4, 128])
            plv = pl[:, 0:512].rearrange("p (m q) -> p m q", m=4)
            pov = po[:, 0:512].rearrange("p (m q) -> p m q", m=4)
            sc.op("dve", lambda e: e.tensor_tensor(out=rt[:], in0=plv, in1=esv, op=ALU.add), [pltk, estk], [rttk])
            sc.op("act", lambda e: e.activation(out=rt[:], in_=rt[:], func=AF.Ln), [rttk], [rttk])
            sc.op("act", lambda e: e.activation(out=rt[:], in_=rt[:], func=AF.Exp, scale=-1.0), [rttk], [rttk])
            for hf in range(2):
                h2 = slice(hf * 64, (hf + 1) * 64)
                sc.op("dve", lambda e, h2=h2, hf=hf: e.tensor_tensor(
                    out=oa[h2, 2 * k:2 * k + 2, bi * 128:(bi + 1) * 128], in0=pov[h2, hf::2, :],
                    in1=rt[h2, hf::2, :], op=ALU.mult), [potk, rttk], [oatk], ssw=True)

        def dense2(tt, oa, oatk, sbi):
            tok0 = tt * T2
            g_t, gtk = gt.next()
            sc.dma("sp", g_t[:], gT_d[:, :, tok0:tok0 + T2].rearrange("c p t -> p c t"), gtk, writes=[gtk])
            mx, mxtk = mixT.next()
            so = tok0 - sbi * SB
            for c in range(8):
                (pa, patk), (pb, pbtk) = ps_ring.next(), ps_ring.next()
                sc.op("pe", [lambda e, kc=kc, c=c, pa=pa: e.matmul(
                    pa[:, 0:T2], lhsT=wbra[:, kc, c * 128:(c + 1) * 128], rhs=oa[:, kc, :],
                    start=(kc == 0), stop=(kc == 7)) for kc in range(8)], [oatk], [patk])
                sc.op("pe", [lambda e, kc=kc, c=c, pb=pb: e.matmul(
                    pb[:, 0:T2], lhsT=wbrb[:, kc, c * 128:(c + 1) * 128], rhs=obT[:, kc, so:so + T2],
                    start=(kc == 0), stop=(kc == 1)) for kc in range(2)], [obtk], [pbtk])
                t1, t1tk = t1r.next()
                t2, t2tk = t2r.next()
                sc.op("dve", lambda e, t1=t1, pa=pa, c=c: e.tensor_tensor(out=t1[:], in0=pa[:, 0:T2],
                                                                          in1=g_t[:, c, :], op=ALU.mult),
                      [patk, gtk], [t1tk])
                sc.op("dve", lambda e, t2=t2, pb=pb, c=c: e.tensor_tensor(out=t2[:], in0=pb[:, 0:T2],
                                                                          in1=g_t[:, 8 + c, :], op=ALU.mult),
                      [pbtk, gtk], [t2tk])
                sc.op("pool", lambda e, t1=t1, t2=t2, c=c: e.tensor_tensor(out=mx[:, c, :], in0=t1[:], in1=t2[:],
                                                                            op=ALU.add), [t1tk, t2tk], [mxtk])
            for bi in range(T2 // 128):
                r0 = tok0 + bi * 128
                x_t, xtk_ = xr.next()
                sc.dma("sp", x_t[:], x_d[r0:r0 + 128, :], xtk_, writes=[xtk_])
                pr, prtk = pre.next()
                for n in range(2):
                    ps, pstk = ps_ring.next()
                    sc.op("pe", [lambda e, kc=kc, n=n, ps=ps, bi=bi: e.matmul(
                        ps[:, 0:512], lhsT=mx[:, kc, bi * 128:(bi + 1) * 128], rhs=wout[:, kc, n * 512:(n + 1) * 512],
                        start=(kc == 0), stop=(kc == 7)) for kc in range(8)], [mxtk], [pstk])
                    sc.op("dve", lambda e, n=n, ps=ps, pr=pr, x_t=x_t: e.scalar_tensor_tensor(
                        out=pr[:, n * 512:(n + 1) * 512], in0=x_t[:, n * 512:(n + 1) * 512], scalar=ALPHA,
                        in1=ps[:, 0:512], op0=ALU.mult, op1=ALU.add), [pstk, xtk_], [prtk])
                hs_, hstk = hst.next()
                sm, smtk = small.next()
                layer_norm(pr[:], prtk, ln_g[:], ln_b[:], hs_[:], hstk, sm, smtk)
                sc.dma(STQ, h1_d[r0:r0 + 128, :], hs_[:], hstk, reads=[hstk])

        NBT = T2 // 128
        for sbi in range(S // SB):
            mixer_b(sbi)
            units = []
            for tt in range(sbi * (SB // T2), (sbi + 1) * (SB // T2)):
                oa, oatk = oaT.next()
                for bi in range(NBT):
                    for k in range(4):
                        units.append({"i": tt * NBT + bi, "k": k, "bi": bi, "tt": tt, "oa": oa, "oatk": oatk})
            n = len(units)
            a_prep(units[0]["i"])
            a_S(units[0])
            for j in range(n):
                u = units[j]
                if u["k"] == 1 and u["i"] + 1 < (sbi + 1) * (SB // 128):
                    a_prep(u["i"] + 1)
                if j + 1 < n:
                    a_S(units[j + 1])
                a_PV(u)
                if u["k"] == 3 and u["bi"] == NBT - 1:
                    dense2(u["tt"], u["oa"], u["oatk"], sbi)
        sc.barrier()

    if 2 in phases:
        with contextlib.ExitStack() as st:
            phase2(st)

    def phase3(st):
      if True:
        wst = Ring([(sb(st, "wst%d" % i, [128, 1024], F32), Tk("wst%d" % i)) for i in range(2)])
        ident, masks, ones, ctk = consts(st, wst)
        wfg, _ = load_weight(st, "wfg", wffg_d, 8, DFF, wst)
        wfu, _ = load_weight(st, "wfu", wffu_d, 8, DFF, wst)
        wfd, _ = load_weight(st, "wfd", wffd_d, NFC, D, wst)
        ln_g, lgtk = bcast_load(st, "ln2g", ln2g_d)
        ln_b, lbtk = bcast_load(st, "ln2b", ln2b_d)
        T3 = 256
        hr = Ring([(sb(st, "hr%d" % i, [128, D], F32), Tk("hr%d" % i)) for i in range(4)])
        hbf = Ring([(sb(st, "hbf%d" % i, [128, D], BF16), Tk("hbf%d" % i)) for i in range(2)])
        hT = [(sb(st, "hT%d" % i, [128, 8, T3], BF16), [Tk("hT%d_%d" % (i, b)) for b in range(2)]) for i in range(2)]
        sg = Ring([(sb(st, "sg%d" % i, [128, T3], F32), Tk("sg%d" % i)) for i in range(2)])
        guT = Ring([(sb(st, "guT%d" % i, [128, NFC, T3], BF16), Tk("guT%d" % i)) for i in range(1)])
        pre = Ring([(sb(st, "pre3_%d" % i, [128, D], F32), Tk("pre3_%d" % i)) for i in range(2)])
        ost = Ring([(sb(st, "ost%d" % i, [128, D], F32), Tk("ost%d" % i)) for i in range(2)])
        small = Ring([(sb(st, "sm3_%d" % i, [128, 16], F32), Tk("sm3_%d" % i)) for i in range(2)])
        sc.barrier()
        NT3 = S // T3
        hrs = {}

        def prep3(t):
            ht, httk = hT[t % 2]
            for b in range(2):
                h_t, htk = hr.next()
                hrs[(t, b)] = (h_t, htk)
                r0 = t * T3 + b * 128
                sc.dma("sp", h_t[:], h1_d[r0:r0 + 128, :], htk, writes=[htk])
                hb, hbtk = hbf.next()
                sc.op("pool", lambda e, hb=hb, h_t=h_t: e.tensor_copy(out=hb[:], in_=h_t[:]), [htk], [hbtk])
                pt, pttk = pt_ring.next()
                sc.op("pe", [lambda e, kc=kc, pt=pt, hb=hb: e.transpose(out=pt[:, kc * 128:(kc + 1) * 128],
                                                                        in_=hb[:, kc * 128:(kc + 1) * 128],
                                                                        identity=ident[:]) for kc in range(8)],
                      [hbtk], [pttk])
                sc.op("act", lambda e, pt=pt, b=b: e.activation(
                    out=ht[:, :, b * 128:(b + 1) * 128], in_=pt[:].rearrange("p (k t) -> p k t", k=8), func=AF.Copy),
                    [pttk], [httk[b]])

        def compute3(t):
            ht, httk = hT[t % 2]
            gu, gutk = guT.next()
            for c in range(NFC):
                (pg, pgtk), (pu, putk) = ps_ring.next(), ps_ring.next()
                sc.op("pe", [lambda e, kc=kc, c=c, pg=pg: e.matmul(
                    pg[:, 0:T3], lhsT=wfg[:, kc, c * 128:(c + 1) * 128], rhs=ht[:, kc, :],
                    start=(kc == 0), stop=(kc == 7)) for kc in range(8)], httk, [pgtk])
                sc.op("pe", [lambda e, kc=kc, c=c, pu=pu: e.matmul(
                    pu[:, 0:T3], lhsT=wfu[:, kc, c * 128:(c + 1) * 128], rhs=ht[:, kc, :],
                    start=(kc == 0), stop=(kc == 7)) for kc in range(8)], httk, [putk])
                s_t, stk_ = sg.next()
                sc.op("act", lambda e, s_t=s_t, pg=pg: e.activation(out=s_t[:], in_=pg[:, 0:T3], func=AF.Silu),
                      [pgtk], [stk_])
                sc.op("dve", lambda e, s_t=s_t, pu=pu, c=c: e.tensor_tensor(out=gu[:, c, :], in0=pu[:, 0:T3],
                                                                            in1=s_t[:], op=ALU.mult),
                      [putk, stk_], [gutk])
            for b in range(2):
                h_t, htk = hrs.pop((t, b))
                r0 = t * T3 + b * 128
                pr, prtk = pre.next()
                for n in range(2):
                    ps, pstk = ps_ring.next()
                    sc.op("pe", [lambda e, c=c, n=n, ps=ps, b=b: e.matmul(
                        ps[:, 0:512], lhsT=gu[:, c, b * 128:(b + 1) * 128], rhs=wfd[:, c, n * 512:(n + 1) * 512],
                        start=(c == 0), stop=(c == NFC - 1)) for c in range(NFC)], [gutk], [pstk])
                    sc.op("dve", lambda e, n=n, ps=ps, pr=pr, h_t=h_t: e.scalar_tensor_tensor(
                        out=pr[:, n * 512:(n + 1) * 512], in0=h_t[:, n * 512:(n + 1) * 512], scalar=ALPHA,
                        in1=ps[:, 0:512], op0=ALU.mult, op1=ALU.add), [pstk, htk], [prtk])
                o_t, otk = ost.next()
                sm, smtk = small.next()
                layer_norm(pr[:], prtk, ln_g[:], ln_b[:], o_t[:], otk, sm, smtk)
                sc.dma(STQ, out_d[r0:r0 + 128, :], o_t[:], otk, reads=[otk])

        prep3(0)
        for t in range(NT3):
            if t + 1 < NT3:
                prep3(t + 1)
            compute3(t)
        sc.barrier()

    if 3 in phases:
        with contextlib.ExitStack() as st:
            phase3(st)

    sc.emit()
    return nc


def host_consts(S):
    half = 8
    inv = (500000.0 ** (-(np.arange(half, dtype=np.float32)) / np.float32(half))).astype(np.float32)
    ang = np.arange(S, dtype=np.float32)[:, None] * inv[None, :]
    cos = np.cos(ang).astype(np.float32)
    sin = np.sin(ang).astype(np.float32)
    csn = np.concatenate([cos, cos, -sin, sin], axis=1).astype(np.float32)
    kk = np.arange(128)[:, None]
    qq = np.arange(128)[None, :]
    m = np.zeros((128, 4, 128), np.float32)
    m[:, 0] = kk >= qq
    m[:, 1] = kk <= qq
    m[:, 2] = (kk >= qq) & (kk >= 64)
    m[:, 3] = (kk <= qq) & (kk < 64)
    ident = np.eye(128, dtype=np.float32)
    return csn, m, ident


def win_perm():
    cols = []
    for p in range(2):
        for m_ in range(4):
            for half in range(2):
                h = 4 * (2 * p + half) + m_
                cols.extend(range(h * 64, (h + 1) * 64))
    cols.extend(range(1024, 1280))
    cols.extend(range(1536, 2304))
    cols.extend(range(2304, 3072))
    cols.extend(range(1280, 1536))
    cols.extend(range(3072, 3840))
    return np.asarray(cols)


def make_in_maps(inputs, S, ncores):
    f = lambda a: np.ascontiguousarray(np.asarray(a, dtype=np.float32))
    csn, m, ident = host_consts(S)
    shared = {
        "w_in": f(np.asarray(inputs["w_in"])[0][:, win_perm()]),
        "w_gate": f(inputs["w_gate"][0]),
        "bg": f(np.asarray(inputs["b_gate"])[0].reshape(16, 128).T),
        "a_sink": f(inputs["a_sink"][0]),
        "w_br_a": f(inputs["w_br_a"][0]),
        "w_br_b": f(inputs["w_br_b"][0]),
        "w_out": f(inputs["w_out"][0]),
        "ln1_g": f(inputs["ln1_g"][0]),
        "ln1_b": f(inputs["ln1_b"][0]),
        "w_ff_gate": f(inputs["w_ff_gate"][0]),
        "w_ff_up": f(inputs["w_ff_up"][0]),
        "w_ff_down": f(inputs["w_ff_down"][0]),
        "ln2_g": f(inputs["ln2_g"][0]),
        "ln2_b": f(inputs["ln2_b"][0]),
        "csn": csn, "masks": m, "ident": ident,
    }
    x = np.asarray(inputs["x"])
    maps = []
    for c in range(ncores):
        d = dict(shared)
        d["x"] = f(x[c, :S])
        maps.append(d)
    return maps


_NC_CACHE = {}


def kernel(**inputs):
    S = 8192
    n = 8
    if S not in _NC_CACHE:
        _NC_CACHE[S] = build(S)
    nc = _NC_CACHE[S]
    in_maps = make_in_maps(inputs, S, n)
    res = run_bass_kernel_spmd(nc, in_maps, core_ids=list(range(n)))
    out = np.stack([np.asarray(r["out"], dtype=np.float32).reshape(S, D) for r in res.results], axis=0)
    return out
```

```python
import contextlib
import numpy as np
import concourse.bass as bass
import concourse.mybir as mybir
from concourse.bass_utils import run_bass_kernel_spmd

F32 = mybir.dt.float32
BF16 = mybir.dt.bfloat16
AF = mybir.ActivationFunctionType
ALU = mybir.AluOpType

D = 1024
HD = 64
IN_W = 3840
ROW = 4864
C_QA, C_KA, C_QB, C_KB, C_VA, C_VB = 0, 1024, 1280, 2048, 2816, 3328
DFF = 2816
NFC = DFF // 128
ALPHA = 2.0 ** 0.25
EPS = 1e-5
SB = 2048
ENG = ("pe", "act", "dve", "pool", "sp")
import os as _os0
STQ = _os0.environ.get("STQ", "pool")


class Tk:
    __slots__ = ("w", "r", "sem", "name")

    def __init__(self, name="", sem=None):
        self.w = None
        self.r = {}
        self.sem = sem
        self.name = name


class Sched:
    def __init__(self, nc):
        self.nc = nc
        self.ops = {e: [] for e in ENG}
        self.cnt = {e: 0 for e in ENG}
        self.waited = {e: {} for e in ENG}
        self.dma_tot = {}
        self.sems = {}
        for e in ("pe", "act", "dve", "pool"):
            self.sems[e] = nc.alloc_semaphore(name="sem_" + e)
        self.ndma = 0

    def dma_sem(self):
        k = "dma%d" % self.ndma
        self.ndma += 1
        self.sems[k] = self.nc.alloc_semaphore(name=k)
        self.dma_tot[k] = 0
        return k

    def _deps(self, eng, reads, writes, pe_accum, skip_same_w=False):
        deps = {}

        def add(ev):
            if ev is None:
                return
            s, v = ev
            if deps.get(s, 0) < v:
                deps[s] = v

        for t in reads:
            add(t.w)
        for t in writes:
            if not ((pe_accum and t.w is not None and t.w[0] == "pe") or
                    (skip_same_w and t.w is not None and t.w[0] == eng)):
                add(t.w)
            for s, v in t.r.items():
                add((s, v))
        waits = []
        wd = self.waited[eng]
        for s, v in deps.items():
            if s == eng and eng in ("pe", "sp"):
                continue
            if wd.get(s, 0) < v:
                wd[s] = v
                waits.append((s, v))
        return waits

    def _mark(self, ev, reads, writes):
        s, v = ev
        for t in reads:
            if t.r.get(s, 0) < v:
                t.r[s] = v
        for t in writes:
            t.w = ev
            t.r = {}

    def op(self, eng, fns, reads=(), writes=(), pe_accum=False, ssw=False):
        if not isinstance(fns, (list, tuple)):
            fns = [fns]
        waits = self._deps(eng, reads, writes, pe_accum, ssw)
        self.cnt[eng] += 1
        ev = (eng, self.cnt[eng])
        self.ops[eng].append((waits, list(fns), (eng, 1)))
        self._mark(ev, reads, writes)
        return ev

    def dma(self, q, out, in_, tile, reads=(), writes=()):
        if tile.sem is None:
            tile.sem = self.dma_sem()
        waits = self._deps(q, reads, writes, False)
        self.dma_tot[tile.sem] += 16
        ev = (tile.sem, self.dma_tot[tile.sem])
        self.ops[q].append((waits, [lambda e, o=out, i=in_: e.dma_start(out=o, in_=i)], (tile.sem, 16)))
        self._mark(ev, reads, writes)
        return ev

    def barrier(self):
        for e in ENG:
            waits = []
            wd = self.waited[e]
            for f in ("pe", "act", "dve", "pool"):
                if f != e and self.cnt[f] > wd.get(f, 0):
                    wd[f] = self.cnt[f]
                    waits.append((f, self.cnt[f]))
            for k, v in self.dma_tot.items():
                if v > wd.get(k, 0):
                    wd[k] = v
                    waits.append((k, v))
            self.ops[e].append((waits, [], None))

    def emit(self):
        nc = self.nc
        with nc.Block() as block:
            def run(name, eng):
                for waits, fns, inc in self.ops[name]:
                    for s, v in waits:
                        eng.wait_ge(self.sems[s], v)
                    ins = None
                    for f in fns:
                        ins = f(eng)
                    if inc is not None and ins is not None:
                        ins.then_inc(self.sems[inc[0]], inc[1])

            @block.tensor
            def _(e):
                run("pe", e)

            @block.scalar
            def _(e):
                run("act", e)

            @block.vector
            def _(e):
                run("dve", e)

            @block.gpsimd
            def _(e):
                run("pool", e)

            @block.sync
            def _(e):
                run("sp", e)


def sl(start, n, step):
    return slice(start, start + step * (n - 1) + 1, step)


class Ring:
    def __init__(self, items):
        self.items = items
        self.i = 0

    def next(self):
        it = self.items[self.i % len(self.items)]
        self.i += 1
        return it


def build(S=8192, dbg=False, phases=(1, 2, 3)):
    nc = bass.Bass("TRN2", target_bir_lowering=False)
    sc = Sched(nc)
    NB = S // 128

    def din(name, shape, dt=F32):
        return nc.dram_tensor(name, list(shape), dt, kind="ExternalInput").ap()

    x_d = din("x", [S, D])
    win_d = din("w_in", [D, IN_W])
    wgate_d = din("w_gate", [D, 2048])
    bg_d = din("bg", [128, 16])
    sink_d = din("a_sink", [128, 8])
    wbra_d = din("w_br_a", [1024, D])
    wbrb_d = din("w_br_b", [256, D])
    wout_d = din("w_out", [D, D])
    ln1g_d = din("ln1_g", [D])
    ln1b_d = din("ln1_b", [D])
    wffg_d = din("w_ff_gate", [D, DFF])
    wffu_d = din("w_ff_up", [D, DFF])
    wffd_d = din("w_ff_down", [DFF, D])
    ln2g_d = din("ln2_g", [D])
    ln2b_d = din("ln2_b", [D])
    csn_d = din("csn", [S, 32])
    masks_d = din("masks", [128, 4, 128])
    ident_d = din("ident", [128, 128])

    skind = "ExternalOutput" if dbg else "Internal"
    qkv_d = nc.dram_tensor("qkv_s", [S, ROW], BF16, kind=skind).ap()
    gT_d = nc.dram_tensor("gT_s", [16, 128, S], BF16, kind=skind).ap()
    h1_d = nc.dram_tensor("h1_s", [S, D], F32, kind=skind).ap()
    out_d = nc.dram_tensor("out", [S, D], F32, kind="ExternalOutput").ap()

    psb = [nc.alloc_psum_tensor("ps%d" % i, [128, 512], F32) for i in range(6)]
    ptb = [nc.alloc_psum_tensor("pt%d" % i, [128, 1024], BF16) for i in range(2)]
    ps_ring = Ring([(psb[i], Tk("ps%d" % i)) for i in range(6)])
    pt_ring = Ring([(ptb[i], Tk("pt%d" % i)) for i in range(2)])

    uid = [0]

    def sb(st, name, shape, dt):
        uid[0] += 1
        return st.enter_context(nc.sbuf_tensor("sb%d_%s" % (uid[0], name), list(shape), dt))

    rr = {"cast": 0}

    def cast_any(out, in_, reads, writes, engs=("dve", "pool", "act")):
        e = engs[rr["cast"] % len(engs)]
        rr["cast"] += 1
        if e == "act":
            sc.op("act", lambda g: g.activation(out=out, in_=in_, func=AF.Copy), reads, writes)
        else:
            sc.op(e, lambda g: g.tensor_copy(out=out, in_=in_), reads, writes)

    def load_weight(st, name, src, KC, N, wst):
        w = sb(st, name, [128, KC, N], BF16)
        tk = Tk(name)
        for kc in range(KC):
            for n0 in range(0, N, 1024):
                n1 = min(N, n0 + 1024)
                stg, stk = wst.next()
                sc.dma("sp", stg[:, 0:n1 - n0], src[kc * 128:(kc + 1) * 128, n0:n1], stk, writes=[stk])
                cast_any(w[:, kc, n0:n1], stg[:, 0:n1 - n0], [stk], [tk])
        return w, tk

    def consts(st, wst):
        ident = sb(st, "ident", [128, 128], BF16)
        masks = sb(st, "masks", [128, 4, 128], BF16)
        ones = sb(st, "ones", [128, 128], BF16)
        ctk = Tk("consts")
        stg, stk = wst.next()
        sc.dma("sp", stg[:, 0:128], ident_d, stk, writes=[stk])
        sc.op("dve", lambda g: g.tensor_copy(out=ident[:], in_=stg[:, 0:128]), [stk], [ctk])
        stg2, stk2 = wst.next()
        sc.dma("sp", stg2[:, 0:512], masks_d.rearrange("p a b -> p (a b)"), stk2, writes=[stk2])
        sc.op("dve", lambda g: g.tensor_copy(out=masks[:].rearrange("p a b -> p (a b)"), in_=stg2[:, 0:512]),
              [stk2], [ctk])
        sc.op("dve", lambda g: g.memset(ones[:], 1.0), [], [ctk])
        return ident, masks, ones, ctk

    def layer_norm(pre, ptk, g_t, b_t, outt, otk, small, smtk):
        stats = small[:, 0:12]
        mv = small[:, 12:14]
        rstd = small[:, 14:15]
        sc.op("dve", lambda g: g.bn_stats(out=small[:, 0:6], in_=pre[:, 0:512]), [ptk], [smtk])
        sc.op("dve", lambda g: g.bn_stats(out=small[:, 6:12], in_=pre[:, 512:1024]), [ptk], [smtk])
        sc.op("dve", lambda g: g.bn_aggr(out=mv, in_=stats), [smtk], [smtk])
        sc.op("dve", lambda g: g.tensor_scalar_add(out=rstd, in0=small[:, 13:14], scalar1=EPS), [smtk], [smtk])
        sc.op("act", lambda g: g.activation(out=rstd, in_=rstd, func=AF.Ln), [smtk], [smtk])
        sc.op("act", lambda g: g.activation(out=rstd, in_=rstd, func=AF.Exp, scale=-0.5), [smtk], [smtk])
        sc.op("dve", lambda g: g.tensor_scalar(out=pre, in0=pre, scalar1=small[:, 12:13], scalar2=rstd,
                                               op0=ALU.subtract, op1=ALU.mult), [ptk, smtk], [ptk])
        sc.op("pool", lambda g: g.tensor_mul(out=pre, in0=pre, in1=g_t), [ptk], [ptk])
        sc.op("pool", lambda g: g.tensor_add(out=outt, in0=pre, in1=b_t), [ptk], [otk])

    def bcast_load(st, name, src, wst_unused=None):
        t = sb(st, name, [128, D], F32)
        tk = Tk(name)
        sc.dma("sp", t[:], src.partition_broadcast(128), tk, writes=[tk])
        return t, tk

    def phase1(st):
      if True:
        wst = Ring([(sb(st, "wst%d" % i, [128, 1024], F32), Tk("wst%d" % i)) for i in range(3)])
        ident, masks, ones, ctk = consts(st, wst)
        w_bf, wtk = load_weight(st, "w_in_bf", win_d, 8, IN_W, wst)
        wg_bf, wgtk = load_weight(st, "w_g_bf", wgate_d, 8, 2048, wst)
        bg = sb(st, "bg", [128, 16], F32)
        bgtk = Tk("bg")
        sc.dma("sp", bg[:], bg_d, bgtk, writes=[bgtk])
        sc.barrier()

        xin = Ring([(sb(st, "xin%d" % i, [128, D], F32), Tk("xin%d" % i)) for i in range(3)])
        xbf = Ring([(sb(st, "xbf%d" % i, [128, D], BF16), Tk("xbf%d" % i)) for i in range(2)])
        xT = [(sb(st, "xT%d" % i, [128, 8, 512], BF16), [Tk("xT%d_%d" % (i, b)) for b in range(4)]) for i in range(2)]
        cs = [(sb(st, "cs%d" % i, [128, 4, 32], F32), Tk("cs%d" % i)) for i in range(2)]
        stq = Ring([(sb(st, "stq%d" % i, [128, ROW], BF16), [Tk("stq%d_%d" % (i, n)) for n in range(8)])
                    for i in range(2)])
        rot = Ring([(sb(st, "rot%d" % i, [128, 44, 16], F32), Tk("rot%d" % i)) for i in range(2)])
        gst = Ring([(sb(st, "gst%d" % i, [128, 4, 512], BF16), Tk("gst%d" % i)) for i in range(2)])
        rA = Ring([(sb(st, "rA%d" % i, [128, 44, 16], F32), Tk("rA%d" % i)) for i in range(2)])
        rB = Ring([(sb(st, "rB%d" % i, [128, 44, 16], F32), Tk("rB%d" % i)) for i in range(2)])
        NT = S // 512

        def prep(t):
            xt, xtk = xT[t % 2]
            c_t, c_tk = cs[t % 2]
            sc.dma("sp", c_t[:], csn_d[t * 512:(t + 1) * 512, :].rearrange("(b p) c -> p b c", p=128), c_tk,
                   writes=[c_tk])
            for b in range(4):
                xi, xitk = xin.next()
                sc.dma("sp", xi[:], x_d[t * 512 + b * 128: t * 512 + (b + 1) * 128, :], xitk, writes=[xitk])
                xb, xbtk = xbf.next()
                sc.op("dve", lambda g, o=xb, i=xi: g.tensor_copy(out=o[:], in_=i[:]), [xitk], [xbtk])
                pt, pttk = pt_ring.next()
                sc.op("pe", [lambda g, o=pt, i=xb, kc=kc: g.transpose(out=o[:, kc * 128:(kc + 1) * 128],
                                                                       in_=i[:, kc * 128:(kc + 1) * 128],
                                                                       identity=ident[:])
                             for kc in range(8)], [xbtk], [pttk])
                sc.op("act", lambda g, o=xt, i=pt, b=b: g.activation(
                    out=o[:, :, b * 128:(b + 1) * 128], in_=i[:].rearrange("p (k t) -> p k t", k=8), func=AF.Copy),
                    [pttk], [xtk[b]])

        def v_chunk(ps, pstk, pcol0, nh, sq, sqtk, col0):
            pv = ps[:, pcol0:pcol0 + nh * 64].rearrange("p (h d) -> p h d", d=64).unsqueeze(2).broadcast_to(
                [128, nh, 2, 64])
            ov = sq[:, col0:col0 + nh * 128].rearrange("p (h r d) -> p h r d", r=2, d=64)
            sc.op("act", lambda g: g.activation(out=ov, in_=pv, func=AF.Copy), [pstk], [sqtk])

        def compute(t):
            xt, xtk = xT[t % 2]
            c_t, c_tk = cs[t % 2]
            for b in range(4):
                sq, sqtks = stq.next()
                rt_, rtk = rot.next()
                for n in range(8):
                    wdt = 512 if n < 7 else 256
                    ps, pstk = ps_ring.next()
                    sc.op("pe", [lambda g, o=ps, kc=kc, n=n, wdt=wdt, b=b: g.matmul(
                        o[:, 0:wdt], lhsT=xt[:, kc, b * 128:(b + 1) * 128], rhs=w_bf[:, kc, n * 512:n * 512 + wdt],
                        start=(kc == 0), stop=(kc == 7)) for kc in range(8)], [xtk[b]], [pstk])
                    nh = 8 if n <= 4 else (4 if n == 5 else 0)
                    if nh:
                        sc.op("act", lambda g, ps=ps, sq=sq, n=n, nh=nh: g.activation(
                            out=sq[:, n * 512:n * 512 + nh * 64], in_=ps[:, 0:nh * 64], func=AF.Copy),
                            [pstk], [sqtks[n]])
                        sc.op("act", lambda g, ps=ps, rt_=rt_, n=n, nh=nh: g.activation(
                            out=rt_[:, 8 * n:8 * n + nh, :],
                            in_=ps[:, 0:nh * 64].rearrange("p (h d) -> p h d", d=64)[:, :, 0:16], func=AF.Copy),
                            [pstk], [rtk])
                    if n == 5:
                        v_chunk(ps, pstk, 256, 4, sq, sqtks[n], C_VA)
                    elif n == 6:
                        v_chunk(ps, pstk, 0, 8, sq, sqtks[n], C_VB)
                    elif n == 7:
                        v_chunk(ps, pstk, 0, 4, sq, sqtks[n], C_VB + 1024)
                a_t, atk = rA.next()
                b_t, btk = rB.next()
                cc = c_t[:, b, 0:16].unsqueeze(1).broadcast_to([128, 44, 16])
                ns = c_t[:, b, 16:24].unsqueeze(1).broadcast_to([128, 44, 8])
                ps_ = c_t[:, b, 24:32].unsqueeze(1).broadcast_to([128, 44, 8])
                sc.op("dve", lambda g, a_t=a_t, rt_=rt_, cc=cc: g.tensor_tensor(
                    out=a_t[:], in0=rt_[:], in1=cc, op=ALU.mult), [rtk, c_tk], [atk])
                sc.op("pool", lambda g, b_t=b_t, rt_=rt_, ns=ns: g.tensor_tensor(
                    out=b_t[:, :, 0:8], in0=rt_[:, :, 8:16], in1=ns, op=ALU.mult), [rtk, c_tk], [btk])
                sc.op("pool", lambda g, b_t=b_t, rt_=rt_, ps_=ps_: g.tensor_tensor(
                    out=b_t[:, :, 8:16], in0=rt_[:, :, 0:8], in1=ps_, op=ALU.mult), [rtk, c_tk], [btk])
                ov = sq[:, 0:2816].rearrange("p (h d) -> p h d", d=64)[:, :, 0:16]
                sc.op("dve", lambda g, ov=ov, a_t=a_t, b_t=b_t: g.tensor_tensor(
                    out=ov, in0=a_t[:], in1=b_t[:], op=ALU.add), [atk, btk], sqtks[0:6])
                r0 = t * 512 + b * 128
                sc.dma(STQ, qkv_d[r0:r0 + 128, :], sq[:], sqtks[0], reads=sqtks)
            for gc in range(16):
                ps, pstk = ps_ring.next()
                sc.op("pe", [lambda g, o=ps, kc=kc, gc=gc: g.matmul(
                    o[:, 0:512], lhsT=wg_bf[:, kc, gc * 128:(gc + 1) * 128], rhs=xt[:, kc, :],
                    start=(kc == 0), stop=(kc == 7)) for kc in range(8)], xtk, [pstk])
                if gc % 4 == 0:
                    gs, gstk = gst.next()
                sc.op("act", lambda g, o=gs, i=ps, gc=gc: g.activation(
                    out=o[:, gc % 4, :], in_=i[:, 0:512], func=AF.Sigmoid, bias=bg[:, gc:gc + 1]), [pstk, bgtk], [gstk])
                if gc % 4 == 3:
                    c0 = gc - 3
                    sc.dma(STQ, gT_d[c0:c0 + 4, :, t * 512:(t + 1) * 512].rearrange("c p t -> p c t"), gs[:],
                           gstk, reads=[gstk])

        import os
        cut = int(os.environ.get("P1CUT", "9"))
        if cut >= 1:
            prep(0)
        for t in range(NT if cut >= 2 else 0):
            if t + 1 < NT:
                prep(t + 1)
            compute(t)
        sc.barrier()

    if 1 in phases:
        with contextlib.ExitStack() as st:
            phase1(st)

    def phase2(st):
      if True:
        wst = Ring([(sb(st, "wst%d" % i, [128, 1024], F32), Tk("wst%d" % i)) for i in range(2)])
        ident, masks, ones, ctk = consts(st, wst)
        wbra, _ = load_weight(st, "wbra", wbra_d, 8, D, wst)
        wbrb, _ = load_weight(st, "wbrb", wbrb_d, 2, D, wst)
        wout, _ = load_weight(st, "wout", wout_d, 8, D, wst)
        ln_g, lgtk = bcast_load(st, "ln1g", ln1g_d)
        ln_b, lbtk = bcast_load(st, "ln1b", ln1b_d)
        es = sb(st, "es", [128, 8], F32)
        estk = Tk("es")
        sc.dma("sp", es[:], sink_d, estk, writes=[estk])
        sc.op("act", lambda g: g.activation(out=es[:], in_=es[:], func=AF.Exp), [estk], [estk])
        mp = sb(st, "mp", [128, 4, 2, 128], BF16)
        for f in range(2):
            for l in range(2):
                i = f * 2 + l
                sc.op("dve", lambda g, i=i, f=f: g.tensor_copy(out=mp[:, i, 0, :], in_=masks[:, 2 if f else 0, :]),
                      [ctk], [ctk])
                sc.op("dve", lambda g, i=i, l=l: g.tensor_copy(out=mp[:, i, 1, :], in_=masks[:, 3 if l else 1, :]),
                      [ctk], [ctk])
        accU = sb(st, "accU", [128, 2, SB], F32)
        accL = sb(st, "accL", [128, 2, SB], F32)
        acctk = Tk("acc")
        obT = sb(st, "obT", [128, 2, SB], BF16)
        obtk = Tk("obT")
        qb = [(sb(st, "qb%d" % i, [128, 256], BF16), Tk("qb%d" % i)) for i in range(3)]
        kbt = [(sb(st, "kb%d" % i, [128, 2, 256], BF16), Tk("kb%d" % i)) for i in range(3)]
        vbt = [(sb(st, "vb%d" % i, [128, 2, 512], BF16), Tk("vb%d" % i)) for i in range(3)]
        for i in range(3):
            sc.op("pool", lambda g, i=i: g.memset(qb[i][0][:], 0.0), [], [qb[i][1]])
            sc.op("pool", lambda g, i=i: g.memset(kbt[i][0][:], 0.0), [], [kbt[i][1]])
            sc.op("pool", lambda g, i=i: g.memset(vbt[i][0][:], 0.0), [], [vbt[i][1]])
        qkT = Ring([(sb(st, "qkT%d" % i, [128, 6, 128], BF16), Tk("qkT%d" % i)) for i in range(3)])
        pTb = Ring([(sb(st, "pTb%d" % i, [128, 2, 2, 2, 128], BF16), Tk("pTb%d" % i)) for i in range(2)])
        qa = Ring([(sb(st, "qa%d" % i, [128, 1024], BF16), Tk("qa%d" % i)) for i in range(2)])
        ka = [(sb(st, "ka%d" % i, [128, 256], BF16), Tk("ka%d" % i)) for i in range(4)]
        va = [(sb(st, "va%d" % i, [128, 512], BF16), Tk("va%d" % i)) for i in range(4)]
        kaT = [(sb(st, "kaT%d" % i, [128, 2, 128], BF16), Tk("kaT%d" % i)) for i in range(4)]
        qaT = Ring([(sb(st, "qaT%d" % i, [128, 8, 128], BF16), Tk("qaT%d" % i)) for i in range(2)])
        pA = Ring([(sb(st, "pA%d" % i, [128, 512], BF16), Tk("pA%d" % i)) for i in range(6)])
        rtmp = Ring([(sb(st, "rtmp%d" % i, [128, 2, 128], F32), Tk("rtmp%d" % i)) for i in range(3)])
        T2 = 256
        oaT = Ring([(sb(st, "oaT%d" % i, [128, 8, T2], BF16), Tk("oaT%d" % i)) for i in range(2)])
        gt = Ring([(sb(st, "gt%d" % i, [128, 16, T2], BF16), Tk("gt%d" % i)) for i in range(2)])
        xr = Ring([(sb(st, "xr%d" % i, [128, D], F32), Tk("xr%d" % i)) for i in range(3)])
        t1r = Ring([(sb(st, "t1_%d" % i, [128, T2], F32), Tk("t1_%d" % i)) for i in range(2)])
        t2r = Ring([(sb(st, "t2_%d" % i, [128, T2], F32), Tk("t2_%d" % i)) for i in range(2)])
        mixT = Ring([(sb(st, "mixT%d" % i, [128, 8, T2], BF16), Tk("mixT%d" % i)) for i in range(2)])
        pre = Ring([(sb(st, "pre%d" % i, [128, D], F32), Tk("pre%d" % i)) for i in range(2)])
        hst = Ring([(sb(st, "hst%d" % i, [128, D], F32), Tk("hst%d" % i)) for i in range(2)])
        small = Ring([(sb(st, "sm%d" % i, [128, 16], F32), Tk("sm%d" % i)) for i in range(2)])
        sc.barrier()

        s_ring = Ring([ps_ring.items[i] for i in range(4)])
        ol_ring = Ring([ps_ring.items[i] for i in (4, 5)])
        bcount = [0]

        def b_T(u):
            sbi, g, d, r, J, j0 = u["args"]
            slot = bcount[0] % 3
            bcount[0] += 1
            q_t, qtk = qb[slot]
            k_t, ktk = kbt[slot]
            v_t, vtk = vbt[slot]
            cq = C_QB + g * 256
            ck = C_KB + g * 256
            cv = C_VB + g * 512
            t0 = r + d * j0
            sc.dma("sp", q_t[:], qkv_d[sl(t0, 128, d), cq:cq + 256], qtk, writes=[qtk])
            first = last = 0
            for kb in range(2):
                jk0 = j0 - 64 + kb * 128
                lo = max(0, -jk0)
                hi = min(128, J - jk0)
                if kb == 0 and lo > 0:
                    first = 1
                if kb == 1 and hi < 128:
                    last = 1
                ts = r + d * (jk0 + lo)
                sc.dma("sp", k_t[lo:hi, kb, :], qkv_d[sl(ts, hi - lo, d), ck:ck + 256], ktk, writes=[ktk])
                sc.dma("sp", v_t[lo:hi, kb, :], qkv_d[sl(ts, hi - lo, d), cv:cv + 512], vtk, writes=[vtk])
            pt, pttk = pt_ring.next()
            fns = []
            for c in range(2):
                fns.append(lambda e, c=c: e.transpose(out=pt[:, c * 128:(c + 1) * 128],
                                                      in_=q_t[:, c * 128:(c + 1) * 128], identity=ident[:]))
            for kb in range(2):
                for c in range(2):
                    i = 2 + kb * 2 + c
                    fns.append(lambda e, c=c, kb=kb, i=i: e.transpose(out=pt[:, i * 128:(i + 1) * 128],
                                                                      in_=k_t[:, kb, c * 128:(c + 1) * 128],
                                                                      identity=ident[:]))
            sc.op("pe", fns, [qtk, ktk], [pttk])
            qk, qktk = qkT.next()
            sc.op("act", lambda e: e.activation(out=qk[:].rearrange("p a b -> p (a b)"), in_=pt[:, 0:768],
                                                func=AF.Copy), [pttk], [qktk])
            u.update(qk=qk, qktk=qktk, v_t=v_t, vtk=vtk, first=first, last=last, t0=t0)

        def b_S(u):
            qk, qktk = u["qk"], u["qktk"]
            sbank = [s_ring.next(), s_ring.next()]
            for half in range(2):
                ps, pstk = sbank[half]
                fns = []
                for c in range(2):
                    for kb in range(2):
                        fns.append(lambda e, c=c, kb=kb, half=half, ps=ps: e.matmul(
                            ps[:, (c * 2 + kb) * 128:(c * 2 + kb + 1) * 128],
                            lhsT=qk[half * 64:(half + 1) * 64, 2 + kb * 2 + c, :],
                            rhs=qk[half * 64:(half + 1) * 64, c, :], start=True, stop=True))
                sc.op("pe", fns, [qktk], [pstk])
            p_t, ptk_ = pTb.next()
            for half in range(2):
                ps, pstk = sbank[half]
                sc.op("act", lambda e, half=half, ps=ps: e.activation(
                    out=p_t[:, half].rearrange("p c k q -> p (c k q)"), in_=ps[:, 0:512], func=AF.Exp, scale=0.125),
                    [pstk], [ptk_], ssw=True)
            pv4 = p_t[:].rearrange("p h c k q -> p (h c) k q")
            mview = mp[:, u["first"] * 2 + u["last"]].unsqueeze(1).broadcast_to([128, 4, 2, 128])
            sc.op("dve", lambda e: e.tensor_tensor(out=pv4, in0=pv4, in1=mview, op=ALU.mult), [ptk_, ctk], [ptk_])
            u.update(p_t=p_t, ptk_=ptk_)

        def b_PV(u):
            sbi, g, d, r, J, j0 = u["args"]
            p_t, ptk_, v_t, vtk = u["p_t"], u["ptk_"], u["v_t"], u["vtk"]
            ol, oltk = ol_ring.next()
            fns = []
            for c in range(2):
                for typ in range(2):
                    for kb in range(2):
                        for half in range(2):
                            h = 2 * c + half
                            hs = slice(half * 64, (half + 1) * 64)
                            col = typ * 256 + c * 128
                            if typ == 0:
                                fns.append(lambda e, hs=hs, col=col, h=h, half=half, c=c, kb=kb: e.matmul(
                                    ol[hs, col:col + 128], lhsT=v_t[:, kb, h * 128:h * 128 + 64],
                                    rhs=p_t[:, half, c, kb, :], start=(kb == 0), stop=(kb == 1)))
                            else:
                                fns.append(lambda e, hs=hs, col=col, half=half, c=c, kb=kb: e.matmul(
                                    ol[hs, col:col + 128], lhsT=ones[:, 0:64],
                                    rhs=p_t[:, half, c, kb, :], start=(kb == 0), stop=(kb == 1)))
            sc.op("pe", fns, [ptk_, vtk, ctk], [oltk])
            a0 = u["t0"] - sbi * SB
            for typ, acc in ((0, accU), (1, accL)):
                av = acc[:, :, sl(a0, 128, d)]
                pv = ol[:, typ * 256:(typ + 1) * 256].rearrange("p (c q) -> p c q", c=2)
                sc.op("dve", lambda e, av=av, pv=pv: e.tensor_tensor(out=av, in0=av, in1=pv, op=ALU.add),
                      [acctk, oltk], [acctk], ssw=True)

        def mixer_b(sbi):
            sc.op("pool", lambda e: e.memset(accU[:], 0.0), [], [acctk])
            sc.op("pool", lambda e: e.memset(accL[:], 0.0), [], [acctk])
            units = []
            for g, d in enumerate((1, 4, 16)):
                J = S // d
                nj = SB // d // 128
                for r in range(d):
                    for jb in range(nj):
                        units.append({"args": (sbi, g, d, r, J, sbi * (SB // d) + jb * 128)})
            n = len(units)
            b_T(units[0])
            b_T(units[1])
            b_S(units[0])
            for i in range(n):
                if i + 2 < n:
                    b_T(units[i + 2])
                if i + 1 < n:
                    b_S(units[i + 1])
                b_PV(units[i])
            sc.op("act", lambda e: e.activation(out=accL[:], in_=accL[:], func=AF.Ln), [acctk], [acctk])
            sc.op("act", lambda e: e.activation(out=accL[:], in_=accL[:], func=AF.Exp, scale=-1.0), [acctk], [acctk])
            sc.op("dve", lambda e: e.tensor_tensor(out=obT[:], in0=accU[:], in1=accL[:], op=ALU.mult),
                  [acctk], [obtk])

        def a_load_kv(blk):
            k_t, ktk = ka[blk % 4]
            v_t, vtk = va[blk % 4]
            sc.dma("sp", k_t[:], qkv_d[blk * 128:(blk + 1) * 128, C_KA:C_KA + 256], ktk, writes=[ktk])
            sc.dma("sp", v_t[:], qkv_d[blk * 128:(blk + 1) * 128, C_VA:C_VA + 512], vtk, writes=[vtk])
            pt, pttk = pt_ring.next()
            sc.op("pe", [lambda e, c=c: e.transpose(out=pt[:, c * 128:(c + 1) * 128], in_=k_t[:, c * 128:(c + 1) * 128],
                                                    identity=ident[:]) for c in range(2)], [ktk], [pttk])
            kt_t, kttk = kaT[blk % 4]
            sc.op("act", lambda e: e.activation(out=kt_t[:].rearrange("p a b -> p (a b)"), in_=pt[:, 0:256],
                                                func=AF.Copy), [pttk], [kttk])

        blkctx = {}

        def a_prep(i):
            if i == 0:
                a_load_kv(0)
            if i + 1 < NB:
                a_load_kv(i + 1)
            q_t, qtk = qa.next()
            sc.dma("sp", q_t[:], qkv_d[i * 128:(i + 1) * 128, 0:1024], qtk, writes=[qtk])
            pt, pttk = pt_ring.next()
            sc.op("pe", [lambda e, c=c: e.transpose(out=pt[:, c * 128:(c + 1) * 128], in_=q_t[:, c * 128:(c + 1) * 128],
                                                    identity=ident[:]) for c in range(8)], [qtk], [pttk])
            qT, qTtk = qaT.next()
            sc.op("act", lambda e: e.activation(out=qT[:].rearrange("p a b -> p (a b)"), in_=pt[:, 0:1024],
                                                func=AF.Copy), [pttk], [qTtk])
            blkctx[i] = (qT, qTtk)

        def a_S(u):
            i, k = u["i"], u["k"]
            qT, qTtk = blkctx[i]
            kbs = [kb for kb in range(3) if 0 <= i - 1 + kb < NB]
            p, half = k // 2, k % 2
            hs = slice(half * 64, (half + 1) * 64)
            plist = []
            for kb in kbs:
                blk = i - 1 + kb
                ps, pstk = s_ring.next()
                kt_t, kttk = kaT[blk % 4]
                sc.op("pe", lambda e, ps=ps, kt_t=kt_t: e.matmul(
                    ps[:, 0:512], lhsT=kt_t[hs, p, :], rhs=qT[hs, p * 4:(p + 1) * 4, :], start=True, stop=True),
                    [kttk, qTtk], [pstk])
                pp, pptk = pA.next()
                sc.op("act", lambda e, ps=ps, pp=pp: e.activation(out=pp[:], in_=ps[:, 0:512], func=AF.Exp,
                                                                  scale=0.125), [pstk], [pptk])
                if kb != 1:
                    mv = masks[:, 0 if kb == 0 else 1, :].unsqueeze(1).broadcast_to([128, 4, 128])
                    ppv = pp[:].rearrange("p (m q) -> p m q", m=4)
                    sc.op("dve",
                          lambda e, ppv=ppv, mv=mv: e.tensor_tensor(out=ppv, in0=ppv, in1=mv, op=ALU.mult),
                          [pptk, ctk], [pptk])
                plist.append((pp, pptk, blk))
            u["plist"] = plist

        def a_PV(u):
            i, k, oa, oatk, bi = u["i"], u["k"], u["oa"], u["oatk"], u["bi"]
            plist = u["plist"]
            ol, oltk = ol_ring.next()
            n = len(plist)
            fns = []
            for typ in range(2):
                for j, (pp, pptk, blk) in enumerate(plist):
                    ppv = pp[:].rearrange("p (m q) -> p m q", m=4)
                    for hf in range(2):
                        hs = slice(hf * 64, (hf + 1) * 64)
                        if typ == 0:
                            fns.append(lambda e, j=j, ppv=ppv, blk=blk, hf=hf, hs=hs: e.matmul(
                                ol[hs, 0:256], lhsT=va[blk % 4][0][:, k * 128:k * 128 + 64], rhs=ppv[:, hf::2, :],
                                start=(j == 0), stop=(j == n - 1)))
                        else:
                            fns.append(lambda e, j=j, ppv=ppv, hf=hf, hs=hs: e.matmul(
                                ol[hs, 256:512], lhsT=ones[:, 0:64], rhs=ppv[:, hf::2, :],
                                start=(j == 0), stop=(j == n - 1)))
            sc.op("pe", fns, [x[1] for x in plist] + [va[x[2] % 4][1] for x in plist] + [ctk], [oltk])
            rt, rttk = rtmp.next()
            esv = es[:, 2 * k:2 * k + 2].unsqueeze(2).broadcast_to([128, 2, 128])
            lv = ol[:, 256:512].rearrange("p (m q) -> p m q", m=2)
            ov = ol[:, 0:256].rearrange("p (m q) -> p m q", m=2)
            sc.op("dve", lambda e: e.tensor_tensor(out=rt[:], in0=lv, in1=esv, op=ALU.add), [oltk, estk], [rttk])
            sc.op("act", lambda e: e.activation(out=rt[:], in_=rt[:], func=AF.Ln), [rttk], [rttk])
            sc.op("act", lambda e: e.activation(out=rt[:], in_=rt[:], func=AF.Exp, scale=-1.0), [rttk], [rttk])
            sc.op("dve", lambda e: e.tensor_tensor(
                out=oa[:, 2 * k:2 * k + 2, bi * 128:(bi + 1) * 128], in0=ov, in1=rt[:], op=ALU.mult),
                [oltk, rttk], [oatk], ssw=True)

        def dense2(tt, oa, oatk, sbi):
            tok0 = tt * T2
            g_t, gtk = gt.next()
            sc.dma("sp", g_t[:], gT_d[:, :, tok0:tok0 + T2].rearrange("c p t -> p c t"), gtk, writes=[gtk])
            mx, mxtk = mixT.next()
            so = tok0 - sbi * SB
            for c in range(8):
                (pa, patk), (pb, pbtk) = s_ring.next(), s_ring.next()
                sc.op("pe", [lambda e, kc=kc, c=c, pa=pa: e.matmul(
                    pa[:, 0:T2], lhsT=wbra[:, kc, c * 128:(c + 1) * 128], rhs=oa[:, kc, :],
                    start=(kc == 0), stop=(kc == 7)) for kc in range(8)], [oatk], [patk])
                sc.op("pe", [lambda e, kc=kc, c=c, pb=pb: e.matmul(
                    pb[:, 0:T2], lhsT=wbrb[:, kc, c * 128:(c + 1) * 128], rhs=obT[:, kc, so:so + T2],
                    start=(kc == 0), stop=(kc == 1)) for kc in range(2)], [obtk], [pbtk])
                t1, t1tk = t1r.next()
                t2, t2tk = t2r.next()
                sc.op("dve", lambda e, t1=t1, pa=pa, c=c: e.tensor_tensor(out=t1[:], in0=pa[:, 0:T2],
                                                                          in1=g_t[:, c, :], op=ALU.mult),
                      [patk, gtk], [t1tk])
                sc.op("dve", lambda e, t2=t2, pb=pb, c=c: e.tensor_tensor(out=t2[:], in0=pb[:, 0:T2],
                                                                          in1=g_t[:, 8 + c, :], op=ALU.mult),
                      [pbtk, gtk], [t2tk])
                sc.op("pool", lambda e, t1=t1, t2=t2, c=c: e.tensor_tensor(out=mx[:, c, :], in0=t1[:], in1=t2[:],
                                                                            op=ALU.add), [t1tk, t2tk], [mxtk])
            for bi in range(T2 // 128):
                r0 = tok0 + bi * 128
                x_t, xtk_ = xr.next()
                sc.dma("sp", x_t[:], x_d[r0:r0 + 128, :], xtk_, writes=[xtk_])
                pr, prtk = pre.next()
                for n in range(2):
                    ps, pstk = s_ring.next()
                    sc.op("pe", [lambda e, kc=kc, n=n, ps=ps, bi=bi: e.matmul(
                        ps[:, 0:512], lhsT=mx[:, kc, bi * 128:(bi + 1) * 128], rhs=wout[:, kc, n * 512:(n + 1) * 512],
                        start=(kc == 0), stop=(kc == 7)) for kc in range(8)], [mxtk], [pstk])
                    sc.op("dve", lambda e, n=n, ps=ps, pr=pr, x_t=x_t: e.scalar_tensor_tensor(
                        out=pr[:, n * 512:(n + 1) * 512], in0=x_t[:, n * 512:(n + 1) * 512], scalar=ALPHA,
                        in1=ps[:, 0:512], op0=ALU.mult, op1=ALU.add), [pstk, xtk_], [prtk])
                hs_, hstk = hst.next()
                sm, smtk = small.next()
                layer_norm(pr[:], prtk, ln_g[:], ln_b[:], hs_[:], hstk, sm, smtk)
                sc.dma(STQ, h1_d[r0:r0 + 128, :], hs_[:], hstk, reads=[hstk])

        NBT = T2 // 128
        for sbi in range(S // SB):
            mixer_b(sbi)
            units = []
            for tt in range(sbi * (SB // T2), (sbi + 1) * (SB // T2)):
                oa, oatk = oaT.next()
                for bi in range(NBT):
                    for k in range(4):
                        units.append({"i": tt * NBT + bi, "k": k, "bi": bi, "tt": tt, "oa": oa, "oatk": oatk})
            n = len(units)
            a_prep(units[0]["i"])
            a_S(units[0])
            for j in range(n):
                u = units[j]
                if u["k"] == 1 and u["i"] + 1 < (sbi + 1) * (SB // 128):
                    a_prep(u["i"] + 1)
                if j + 1 < n:
                    a_S(units[j + 1])
                a_PV(u)
                if u["k"] == 3 and u["bi"] == NBT - 1:
                    dense2(u["tt"], u["oa"], u["oatk"], sbi)
        sc.barrier()

    if 2 in phases:
        with contextlib.ExitStack() as st:
            phase2(st)

    def phase3(st):
      if True:
        wst = Ring([(sb(st, "wst%d" % i, [128, 1024], F32), Tk("wst%d" % i)) for i in range(2)])
        ident, masks, ones, ctk = consts(st, wst)
        wfg, _ = load_weight(st, "wfg", wffg_d, 8, DFF, wst)
        wfu, _ = load_weight(st, "wfu", wffu_d, 8, DFF, wst)
        wfd, _ = load_weight(st, "wfd", wffd_d, NFC, D, wst)
        ln_g, lgtk = bcast_load(st, "ln2g", ln2g_d)
        ln_b, lbtk = bcast_load(st, "ln2b", ln2b_d)
        T3 = 256
        hr = Ring([(sb(st, "hr%d" % i, [128, D], F32), Tk("hr%d" % i)) for i in range(4)])
        hbf = Ring([(sb(st, "hbf%d" % i, [128, D], BF16), Tk("hbf%d" % i)) for i in range(2)])
        hT = [(sb(st, "hT%d" % i, [128, 8, T3], BF16), [Tk("hT%d_%d" % (i, b)) for b in range(2)]) for i in range(2)]
        sg = Ring([(sb(st, "sg%d" % i, [128, T3], F32), Tk("sg%d" % i)) for i in range(2)])
        guT = Ring([(sb(st, "guT%d" % i, [128, NFC, T3], BF16), Tk("guT%d" % i)) for i in range(1)])
        pre = Ring([(sb(st, "pre3_%d" % i, [128, D], F32), Tk("pre3_%d" % i)) for i in range(2)])
        ost = Ring([(sb(st, "ost%d" % i, [128, D], F32), Tk("ost%d" % i)) for i in range(2)])
        small = Ring([(sb(st, "sm3_%d" % i, [128, 16], F32), Tk("sm3_%d" % i)) for i in range(2)])
        sc.barrier()
        NT3 = S // T3
        hrs = {}

        def prep3(t):
            ht, httk = hT[t % 2]
            for b in range(2):
                h_t, htk = hr.next()
                hrs[(t, b)] = (h_t, htk)
                r0 = t * T3 + b * 128
                sc.dma("sp", h_t[:], h1_d[r0:r0 + 128, :], htk, writes=[htk])
                hb, hbtk = hbf.next()
                sc.op("pool", lambda e, hb=hb, h_t=h_t: e.tensor_copy(out=hb[:], in_=h_t[:]), [htk], [hbtk])
                pt, pttk = pt_ring.next()
                sc.op("pe", [lambda e, kc=kc, pt=pt, hb=hb: e.transpose(out=pt[:, kc * 128:(kc + 1) * 128],
                                                                        in_=hb[:, kc * 128:(kc + 1) * 128],
                                                                        identity=ident[:]) for kc in range(8)],
                      [hbtk], [pttk])
                sc.op("act", lambda e, pt=pt, b=b: e.activation(
                    out=ht[:, :, b * 128:(b + 1) * 128], in_=pt[:].rearrange("p (k t) -> p k t", k=8), func=AF.Copy),
                    [pttk], [httk[b]])

        def compute3(t):
            ht, httk = hT[t % 2]
            gu, gutk = guT.next()
            for c in range(NFC):
                (pg, pgtk), (pu, putk) = ps_ring.next(), ps_ring.next()
                sc.op("pe", [lambda e, kc=kc, c=c, pg=pg: e.matmul(
                    pg[:, 0:T3], lhsT=wfg[:, kc, c * 128:(c + 1) * 128], rhs=ht[:, kc, :],
                    start=(kc == 0), stop=(kc == 7)) for kc in range(8)], httk, [pgtk])
                sc.op("pe", [lambda e, kc=kc, c=c, pu=pu: e.matmul(
                    pu[:, 0:T3], lhsT=wfu[:, kc, c * 128:(c + 1) * 128], rhs=ht[:, kc, :],
                    start=(kc == 0), stop=(kc == 7)) for kc in range(8)], httk, [putk])
                s_t, stk_ = sg.next()
                sc.op("act", lambda e, s_t=s_t, pg=pg: e.activation(out=s_t[:], in_=pg[:, 0:T3], func=AF.Silu),
                      [pgtk], [stk_])
                sc.op("dve", lambda e, s_t=s_t, pu=pu, c=c: e.tensor_tensor(out=gu[:, c, :], in0=pu[:, 0:T3],
                                                                            in1=s_t[:], op=ALU.mult),
                      [putk, stk_], [gutk])
            for b in range(2):
                h_t, htk = hrs.pop((t, b))
                r0 = t * T3 + b * 128
                pr, prtk = pre.next()
                for n in range(2):
                    ps, pstk = ps_ring.next()
                    sc.op("pe", [lambda e, c=c, n=n, ps=ps, b=b: e.matmul(
                        ps[:, 0:512], lhsT=gu[:, c, b * 128:(b + 1) * 128], rhs=wfd[:, c, n * 512:(n + 1) * 512],
                        start=(c == 0), stop=(c == NFC - 1)) for c in range(NFC)], [gutk], [pstk])
                    sc.op("dve", lambda e, n=n, ps=ps, pr=pr, h_t=h_t: e.scalar_tensor_tensor(
                        out=pr[:, n * 512:(n + 1) * 512], in0=h_t[:, n * 512:(n + 1) * 512], scalar=ALPHA,
                        in1=ps[:, 0:512], op0=ALU.mult, op1=ALU.add), [pstk, htk], [prtk])
                o_t, otk = ost.next()
                sm, smtk = small.next()
                layer_norm(pr[:], prtk, ln_g[:], ln_b[:], o_t[:], otk, sm, smtk)
                sc.dma(STQ, out_d[r0:r0 + 128, :], o_t[:], otk, reads=[otk])

        prep3(0)
        for t in range(NT3):
            if t + 1 < NT3:
                prep3(t + 1)
            compute3(t)
        sc.barrier()

    if 3 in phases:
        with contextlib.ExitStack() as st:
            phase3(st)

    sc.emit()
    return nc


def host_consts(S):
    half = 8
    inv = (500000.0 ** (-(np.arange(half, dtype=np.float32)) / np.float32(half))).astype(np.float32)
    ang = np.arange(S, dtype=np.float32)[:, None] * inv[None, :]
    cos = np.cos(ang).astype(np.float32)
    sin = np.sin(ang).astype(np.float32)
    csn = np.concatenate([cos, cos, -sin, sin], axis=1).astype(np.float32)
    kk = np.arange(128)[:, None]
    qq = np.arange(128)[None, :]
    m = np.zeros((128, 4, 128), np.float32)
    m[:, 0] = kk >= qq
    m[:, 1] = kk <= qq
    m[:, 2] = (kk >= qq) & (kk >= 64)
    m[:, 3] = (kk <= qq) & (kk < 64)
    ident = np.eye(128, dtype=np.float32)
    return csn, m, ident


def win_perm():
    cols = []
    for p in range(2):
        for m_ in range(4):
            for half in range(2):
                h = 4 * (2 * p + half) + m_
                cols.extend(range(h * 64, (h + 1) * 64))
    cols.extend(range(1024, 1280))
    cols.extend(range(1536, 2304))
    cols.extend(range(2304, 3072))
    cols.extend(range(1280, 1536))
    cols.extend(range(3072, 3840))
    return np.asarray(cols)


def make_in_maps(inputs, S, ncores):
    f = lambda a: np.ascontiguousarray(np.asarray(a, dtype=np.float32))
    csn, m, ident = host_consts(S)
    shared = {
        "w_in": f(np.asarray(inputs["w_in"])[0][:, win_perm()]),
        "w_gate": f(inputs["w_gate"][0]),
        "bg": f(np.asarray(inputs["b_gate"])[0].reshape(16, 128).T),
        "a_sink": f(np.asarray(inputs["a_sink"])[0].reshape(4, 2, 2).transpose(2, 0, 1)[:, None].repeat(64, axis=1).reshape(128, 8)),
        "w_br_a": f(inputs["w_br_a"][0]),
        "w_br_b": f(inputs["w_br_b"][0]),
        "w_out": f(inputs["w_out"][0]),
        "ln1_g": f(inputs["ln1_g"][0]),
        "ln1_b": f(inputs["ln1_b"][0]),
        "w_ff_gate": f(inputs["w_ff_gate"][0]),
        "w_ff_up": f(inputs["w_ff_up"][0]),
        "w_ff_down": f(inputs["w_ff_down"][0]),
        "ln2_g": f(inputs["ln2_g"][0]),
        "ln2_b": f(inputs["ln2_b"][0]),
        "csn": csn, "masks": m, "ident": ident,
    }
    x = np.asarray(inputs["x"])
    maps = []
    for c in range(ncores):
        d = dict(shared)
        d["x"] = f(x[c, :S])
        maps.append(d)
    return maps


_NC_CACHE = {}


def kernel(**inputs):
    S = 8192
    n = 8
    if S not in _NC_CACHE:
        _NC_CACHE[S] = build(S)
    nc = _NC_CACHE[S]
    in_maps = make_in_maps(inputs, S, n)
    res = run_bass_kernel_spmd(nc, in_maps, core_ids=list(range(n)))
    out = np.stack([np.asarray(r["out"], dtype=np.float32).reshape(S, D) for r in res.results], axis=0)
    return out
```

```python
import contextlib
import numpy as np
import concourse.bass as bass
import concourse.mybir as mybir
from concourse.bass_utils import run_bass_kernel_spmd

F32 = mybir.dt.float32
BF16 = mybir.dt.bfloat16
AF = mybir.ActivationFunctionType
ALU = mybir.AluOpType

D = 1024
HD = 64
IN_W = 3840
ROW = 4864
C_QA, C_KA, C_QB, C_KB, C_VA, C_VB = 0, 1024, 1280, 2048, 2816, 3328
DFF = 2816
NFC = DFF // 128
ALPHA = 2.0 ** 0.25
EPS = 1e-5
SB = 2048
ENG = ("pe", "act", "dve", "pool", "sp")
import os as _os0
STQ = _os0.environ.get("STQ", "pool")


class Tk:
    __slots__ = ("w", "r", "sem", "name")

    def __init__(self, name="", sem=None):
        self.w = None
        self.r = {}
        self.sem = sem
        self.name = name


class Sched:
    def __init__(self, nc):
        self.nc = nc
        self.ops = {e: [] for e in ENG}
        self.cnt = {e: 0 for e in ENG}
        self.waited = {e: {} for e in ENG}
        self.dma_tot = {}
        self.sems = {}
        for e in ("pe", "act", "dve", "pool"):
            self.sems[e] = nc.alloc_semaphore(name="sem_" + e)
        self.ndma = 0

    def dma_sem(self):
        k = "dma%d" % self.ndma
        self.ndma += 1
        self.sems[k] = self.nc.alloc_semaphore(name=k)
        self.dma_tot[k] = 0
        return k

    def _deps(self, eng, reads, writes, pe_accum, skip_same_w=False):
        deps = {}

        def add(ev):
            if ev is None:
                return
            s, v = ev
            if deps.get(s, 0) < v:
                deps[s] = v

        for t in reads:
            add(t.w)
        for t in writes:
            if not ((pe_accum and t.w is not None and t.w[0] == "pe") or
                    (skip_same_w and t.w is not None and t.w[0] == eng)):
                add(t.w)
            for s, v in t.r.items():
                add((s, v))
        waits = []
        wd = self.waited[eng]
        for s, v in deps.items():
            if s == eng and eng in ("pe", "sp"):
                continue
            if wd.get(s, 0) < v:
                wd[s] = v
                waits.append((s, v))
        return waits

    def _mark(self, ev, reads, writes):
        s, v = ev
        for t in reads:
            if t.r.get(s, 0) < v:
                t.r[s] = v
        for t in writes:
            t.w = ev
            t.r = {}

    def op(self, eng, fns, reads=(), writes=(), pe_accum=False, ssw=False):
        if not isinstance(fns, (list, tuple)):
            fns = [fns]
        waits = self._deps(eng, reads, writes, pe_accum, ssw)
        self.cnt[eng] += 1
        ev = (eng, self.cnt[eng])
        self.ops[eng].append((waits, list(fns), (eng, 1)))
        self._mark(ev, reads, writes)
        return ev

    def dma(self, q, out, in_, tile, reads=(), writes=()):
        if tile.sem is None:
            tile.sem = self.dma_sem()
        waits = self._deps(q, reads, writes, False)
        self.dma_tot[tile.sem] += 16
        ev = (tile.sem, self.dma_tot[tile.sem])
        self.ops[q].append((waits, [lambda e, o=out, i=in_: e.dma_start(out=o, in_=i)], (tile.sem, 16)))
        self._mark(ev, reads, writes)
        return ev

    def barrier(self):
        for e in ENG:
            waits = []
            wd = self.waited[e]
            for f in ("pe", "act", "dve", "pool"):
                if f != e and self.cnt[f] > wd.get(f, 0):
                    wd[f] = self.cnt[f]
                    waits.append((f, self.cnt[f]))
            for k, v in self.dma_tot.items():
                if v > wd.get(k, 0):
                    wd[k] = v
                    waits.append((k, v))
            self.ops[e].append((waits, [], None))

    def emit(self):
        nc = self.nc
        with nc.Block() as block:
            def run(name, eng):
                for waits, fns, inc in self.ops[name]:
                    for s, v in waits:
                        eng.wait_ge(self.sems[s], v)
                    ins = None
                    for f in fns:
                        ins = f(eng)
                    if inc is not None and ins is not None:
                        ins.then_inc(self.sems[inc[0]], inc[1])

            @block.tensor
            def _(e):
                run("pe", e)

            @block.scalar
            def _(e):
                run("act", e)

            @block.vector
            def _(e):
                run("dve", e)

            @block.gpsimd
            def _(e):
                run("pool", e)

            @block.sync
            def _(e):
                run("sp", e)


def sl(start, n, step):
    return slice(start, start + step * (n - 1) + 1, step)


class Ring:
    def __init__(self, items):
        self.items = items
        self.i = 0

    def next(self):
        it = self.items[self.i % len(self.items)]
        self.i += 1
        return it


def build(S=8192, dbg=False, phases=(1, 2, 3)):
    nc = bass.Bass("TRN2", target_bir_lowering=False)
    sc = Sched(nc)
    NB = S // 128

    def din(name, shape, dt=F32):
        return nc.dram_tensor(name, list(shape), dt, kind="ExternalInput").ap()

    x_d = din("x", [S, D])
    win_d = din("w_in", [D, IN_W])
    wgate_d = din("w_gate", [D, 2048])
    bg_d = din("bg", [128, 16])
    sink_d = din("a_sink", [128, 8])
    wbra_d = din("w_br_a", [1024, D])
    wbrb_d = din("w_br_b", [256, D])
    wout_d = din("w_out", [D, D])
    ln1g_d = din("ln1_g", [D])
    ln1b_d = din("ln1_b", [D])
    wffg_d = din("w_ff_gate", [D, DFF])
    wffu_d = din("w_ff_up", [D, DFF])
    wffd_d = din("w_ff_down", [DFF, D])
    ln2g_d = din("ln2_g", [D])
    ln2b_d = din("ln2_b", [D])
    csn_d = din("csn", [S, 32])
    masks_d = din("masks", [128, 4, 128])
    ident_d = din("ident", [128, 128])

    skind = "ExternalOutput" if dbg else "Internal"
    qkv_d = nc.dram_tensor("qkv_s", [S, ROW], BF16, kind=skind).ap()
    gT_d = nc.dram_tensor("gT_s", [16, 128, S], BF16, kind=skind).ap()
    h1_d = nc.dram_tensor("h1_s", [S, D], F32, kind=skind).ap()
    out_d = nc.dram_tensor("out", [S, D], F32, kind="ExternalOutput").ap()

    psb = [nc.alloc_psum_tensor("ps%d" % i, [128, 512], F32) for i in range(6)]
    ptb = [nc.alloc_psum_tensor("pt%d" % i, [128, 1024], BF16) for i in range(2)]
    ps_ring = Ring([(psb[i], Tk("ps%d" % i)) for i in range(6)])
    pt_ring = Ring([(ptb[i], Tk("pt%d" % i)) for i in range(2)])

    uid = [0]

    def sb(st, name, shape, dt):
        uid[0] += 1
        return st.enter_context(nc.sbuf_tensor("sb%d_%s" % (uid[0], name), list(shape), dt))

    rr = {"cast": 0}

    def cast_any(out, in_, reads, writes, engs=("dve", "pool", "act")):
        e = engs[rr["cast"] % len(engs)]
        rr["cast"] += 1
        if e == "act":
            sc.op("act", lambda g: g.activation(out=out, in_=in_, func=AF.Copy), reads, writes)
        else:
            sc.op(e, lambda g: g.tensor_copy(out=out, in_=in_), reads, writes)

    def load_weight(st, name, src, KC, N, wst):
        w = sb(st, name, [128, KC, N], BF16)
        tk = Tk(name)
        for kc in range(KC):
            for n0 in range(0, N, 1024):
                n1 = min(N, n0 + 1024)
                stg, stk = wst.next()
                sc.dma("sp", stg[:, 0:n1 - n0], src[kc * 128:(kc + 1) * 128, n0:n1], stk, writes=[stk])
                cast_any(w[:, kc, n0:n1], stg[:, 0:n1 - n0], [stk], [tk])
        return w, tk

    def consts(st, wst):
        ident = sb(st, "ident", [128, 128], BF16)
        masks = sb(st, "masks", [128, 4, 128], BF16)
        ones = sb(st, "ones", [128, 128], BF16)
        ctk = Tk("consts")
        stg, stk = wst.next()
        sc.dma("sp", stg[:, 0:128], ident_d, stk, writes=[stk])
        sc.op("dve", lambda g: g.tensor_copy(out=ident[:], in_=stg[:, 0:128]), [stk], [ctk])
        stg2, stk2 = wst.next()
        sc.dma("sp", stg2[:, 0:512], masks_d.rearrange("p a b -> p (a b)"), stk2, writes=[stk2])
        sc.op("dve", lambda g: g.tensor_copy(out=masks[:].rearrange("p a b -> p (a b)"), in_=stg2[:, 0:512]),
              [stk2], [ctk])
        sc.op("dve", lambda g: g.memset(ones[:], 1.0), [], [ctk])
        return ident, masks, ones, ctk

    def layer_norm(pre, ptk, g_t, b_t, outt, otk, small, smtk):
        stats = small[:, 0:12]
        mv = small[:, 12:14]
        rstd = small[:, 14:15]
        sc.op("dve", lambda g: g.bn_stats(out=small[:, 0:6], in_=pre[:, 0:512]), [ptk], [smtk])
        sc.op("dve", lambda g: g.bn_stats(out=small[:, 6:12], in_=pre[:, 512:1024]), [ptk], [smtk])
        sc.op("dve", lambda g: g.bn_aggr(out=mv, in_=stats), [smtk], [smtk])
        sc.op("dve", lambda g: g.tensor_scalar_add(out=rstd, in0=small[:, 13:14], scalar1=EPS), [smtk], [smtk])
        sc.op("act", lambda g: g.activation(out=rstd, in_=rstd, func=AF.Ln), [smtk], [smtk])
        sc.op("act", lambda g: g.activation(out=rstd, in_=rstd, func=AF.Exp, scale=-0.5), [smtk], [smtk])
        sc.op("dve", lambda g: g.tensor_scalar(out=pre, in0=pre, scalar1=small[:, 12:13], scalar2=rstd,
                                               op0=ALU.subtract, op1=ALU.mult), [ptk, smtk], [ptk])
        sc.op("pool", lambda g: g.tensor_mul(out=pre, in0=pre, in1=g_t), [ptk], [ptk])
        sc.op("pool", lambda g: g.tensor_add(out=outt, in0=pre, in1=b_t), [ptk], [otk])

    def bcast_load(st, name, src, wst_unused=None):
        t = sb(st, name, [128, D], F32)
        tk = Tk(name)
        sc.dma("sp", t[:], src.partition_broadcast(128), tk, writes=[tk])
        return t, tk

    def phase1(st):
      if True:
        wst = Ring([(sb(st, "wst%d" % i, [128, 1024], F32), Tk("wst%d" % i)) for i in range(3)])
        ident, masks, ones, ctk = consts(st, wst)
        w_bf, wtk = load_weight(st, "w_in_bf", win_d, 8, IN_W, wst)
        wg_bf, wgtk = load_weight(st, "w_g_bf", wgate_d, 8, 2048, wst)
        bg = sb(st, "bg", [128, 16], F32)
        bgtk = Tk("bg")
        sc.dma("sp", bg[:], bg_d, bgtk, writes=[bgtk])
        sc.barrier()

        xin = Ring([(sb(st, "xin%d" % i, [128, D], F32), Tk("xin%d" % i)) for i in range(3)])
        xbf = Ring([(sb(st, "xbf%d" % i, [128, D], BF16), Tk("xbf%d" % i)) for i in range(2)])
        xT = [(sb(st, "xT%d" % i, [128, 8, 512], BF16), [Tk("xT%d_%d" % (i, b)) for b in range(4)]) for i in range(2)]
        cs = [(sb(st, "cs%d" % i, [128, 4, 32], F32), Tk("cs%d" % i)) for i in range(2)]
        stq = Ring([(sb(st, "stq%d" % i, [128, ROW], BF16), [Tk("stq%d_%d" % (i, n)) for n in range(8)])
                    for i in range(2)])
        rot = Ring([(sb(st, "rot%d" % i, [128, 44, 16], F32), Tk("rot%d" % i)) for i in range(2)])
        gst = Ring([(sb(st, "gst%d" % i, [128, 4, 512], BF16), Tk("gst%d" % i)) for i in range(2)])
        rA = Ring([(sb(st, "rA%d" % i, [128, 44, 16], F32), Tk("rA%d" % i)) for i in range(2)])
        rB = Ring([(sb(st, "rB%d" % i, [128, 44, 16], F32), Tk("rB%d" % i)) for i in range(2)])
        NT = S // 512

        def prep(t):
            xt, xtk = xT[t % 2]
            c_t, c_tk = cs[t % 2]
            sc.dma("sp", c_t[:], csn_d[t * 512:(t + 1) * 512, :].rearrange("(b p) c -> p b c", p=128), c_tk,
                   writes=[c_tk])
            for b in range(4):
                xi, xitk = xin.next()
                sc.dma("sp", xi[:], x_d[t * 512 + b * 128: t * 512 + (b + 1) * 128, :], xitk, writes=[xitk])
                xb, xbtk = xbf.next()
                sc.op("dve", lambda g, o=xb, i=xi: g.tensor_copy(out=o[:], in_=i[:]), [xitk], [xbtk])
                pt, pttk = pt_ring.next()
                sc.op("pe", [lambda g, o=pt, i=xb, kc=kc: g.transpose(out=o[:, kc * 128:(kc + 1) * 128],
                                                                       in_=i[:, kc * 128:(kc + 1) * 128],
                                                                       identity=ident[:])
                             for kc in range(8)], [xbtk], [pttk])
                sc.op("act", lambda g, o=xt, i=pt, b=b: g.activation(
                    out=o[:, :, b * 128:(b + 1) * 128], in_=i[:].rearrange("p (k t) -> p k t", k=8), func=AF.Copy),
                    [pttk], [xtk[b]])

        def v_chunk(ps, pstk, pcol0, nh, sq, sqtk, col0):
            pv = ps[:, pcol0:pcol0 + nh * 64].rearrange("p (h d) -> p h d", d=64).unsqueeze(2).broadcast_to(
                [128, nh, 2, 64])
            ov = sq[:, col0:col0 + nh * 128].rearrange("p (h r d) -> p h r d", r=2, d=64)
            sc.op("act", lambda g: g.activation(out=ov, in_=pv, func=AF.Copy), [pstk], [sqtk])

        def compute(t):
            xt, xtk = xT[t % 2]
            c_t, c_tk = cs[t % 2]
            for b in range(4):
                sq, sqtks = stq.next()
                rt_, rtk = rot.next()
                for n in range(8):
                    wdt = 512 if n < 7 else 256
                    ps, pstk = ps_ring.next()
                    sc.op("pe", [lambda g, o=ps, kc=kc, n=n, wdt=wdt, b=b: g.matmul(
                        o[:, 0:wdt], lhsT=xt[:, kc, b * 128:(b + 1) * 128], rhs=w_bf[:, kc, n * 512:n * 512 + wdt],
                        start=(kc == 0), stop=(kc == 7)) for kc in range(8)], [xtk[b]], [pstk])
                    nh = 8 if n <= 4 else (4 if n == 5 else 0)
                    if nh:
                        sc.op("act", lambda g, ps=ps, sq=sq, n=n, nh=nh: g.activation(
                            out=sq[:, n * 512:n * 512 + nh * 64], in_=ps[:, 0:nh * 64], func=AF.Copy),
                            [pstk], [sqtks[n]])
                        sc.op("act", lambda g, ps=ps, rt_=rt_, n=n, nh=nh: g.activation(
                            out=rt_[:, 8 * n:8 * n + nh, :],
                            in_=ps[:, 0:nh * 64].rearrange("p (h d) -> p h d", d=64)[:, :, 0:16], func=AF.Copy),
                            [pstk], [rtk])
                    if n == 5:
                        v_chunk(ps, pstk, 256, 4, sq, sqtks[n], C_VA)
                    elif n == 6:
                        v_chunk(ps, pstk, 0, 8, sq, sqtks[n], C_VB)
                    elif n == 7:
                        v_chunk(ps, pstk, 0, 4, sq, sqtks[n], C_VB + 1024)
                a_t, atk = rA.next()
                b_t, btk = rB.next()
                cc = c_t[:, b, 0:16].unsqueeze(1).broadcast_to([128, 44, 16])
                ns = c_t[:, b, 16:24].unsqueeze(1).broadcast_to([128, 44, 8])
                ps_ = c_t[:, b, 24:32].unsqueeze(1).broadcast_to([128, 44, 8])
                sc.op("dve", lambda g, a_t=a_t, rt_=rt_, cc=cc: g.tensor_tensor(
                    out=a_t[:], in0=rt_[:], in1=cc, op=ALU.mult), [rtk, c_tk], [atk])
                sc.op("pool", lambda g, b_t=b_t, rt_=rt_, ns=ns: g.tensor_tensor(
                    out=b_t[:, :, 0:8], in0=rt_[:, :, 8:16], in1=ns, op=ALU.mult), [rtk, c_tk], [btk])
                sc.op("pool", lambda g, b_t=b_t, rt_=rt_, ps_=ps_: g.tensor_tensor(
                    out=b_t[:, :, 8:16], in0=rt_[:, :, 0:8], in1=ps_, op=ALU.mult), [rtk, c_tk], [btk])
                ov = sq[:, 0:2816].rearrange("p (h d) -> p h d", d=64)[:, :, 0:16]
                sc.op("dve", lambda g, ov=ov, a_t=a_t, b_t=b_t: g.tensor_tensor(
                    out=ov, in0=a_t[:], in1=b_t[:], op=ALU.add), [atk, btk], sqtks[0:6])
                r0 = t * 512 + b * 128
                sc.dma(STQ, qkv_d[r0:r0 + 128, :], sq[:], sqtks[0], reads=sqtks)
            for gc in range(16):
                ps, pstk = ps_ring.next()
                sc.op("pe", [lambda g, o=ps, kc=kc, gc=gc: g.matmul(
                    o[:, 0:512], lhsT=wg_bf[:, kc, gc * 128:(gc + 1) * 128], rhs=xt[:, kc, :],
                    start=(kc == 0), stop=(kc == 7)) for kc in range(8)], xtk, [pstk])
                if gc % 4 == 0:
                    gs, gstk = gst.next()
                sc.op("act", lambda g, o=gs, i=ps, gc=gc: g.activation(
                    out=o[:, gc % 4, :], in_=i[:, 0:512], func=AF.Sigmoid, bias=bg[:, gc:gc + 1]), [pstk, bgtk], [gstk])
                if gc % 4 == 3:
                    c0 = gc - 3
                    sc.dma(STQ, gT_d[c0:c0 + 4, :, t * 512:(t + 1) * 512].rearrange("c p t -> p c t"), gs[:],
                           gstk, reads=[gstk])

        import os
        cut = int(os.environ.get("P1CUT", "9"))
        if cut >= 1:
            prep(0)
        for t in range(NT if cut >= 2 else 0):
            if t + 1 < NT:
                prep(t + 1)
            compute(t)
        sc.barrier()

    if 1 in phases:
        with contextlib.ExitStack() as st:
            phase1(st)

    def phase2(st):
      if True:
        wst = Ring([(sb(st, "wst%d" % i, [128, 1024], F32), Tk("wst%d" % i)) for i in range(2)])
        ident, masks, ones, ctk = consts(st, wst)
        wbra, _ = load_weight(st, "wbra", wbra_d, 8, D, wst)
        wbrb, _ = load_weight(st, "wbrb", wbrb_d, 2, D, wst)
        wout, _ = load_weight(st, "wout", wout_d, 8, D, wst)
        ln_g, lgtk = bcast_load(st, "ln1g", ln1g_d)
        ln_b, lbtk = bcast_load(st, "ln1b", ln1b_d)
        es = sb(st, "es", [128, 8], F32)
        estk = Tk("es")
        sc.dma("sp", es[:], sink_d, estk, writes=[estk])
        sc.op("act", lambda g: g.activation(out=es[:], in_=es[:], func=AF.Exp), [estk], [estk])
        mp = sb(st, "mp", [128, 4, 2, 128], BF16)
        for f in range(2):
            for l in range(2):
                i = f * 2 + l
                sc.op("dve", lambda g, i=i, f=f: g.tensor_copy(out=mp[:, i, 0, :], in_=masks[:, 2 if f else 0, :]),
                      [ctk], [ctk])
                sc.op("dve", lambda g, i=i, l=l: g.tensor_copy(out=mp[:, i, 1, :], in_=masks[:, 3 if l else 1, :]),
                      [ctk], [ctk])
        accU = sb(st, "accU", [128, 2, SB], F32)
        accL = sb(st, "accL", [128, 2, SB], F32)
        acctk = Tk("acc")
        obT = sb(st, "obT", [128, 2, SB], BF16)
        obtk = Tk("obT")
        NBS = 5
        qb = [(sb(st, "qb%d" % i, [128, 256], BF16), Tk("qb%d" % i)) for i in range(NBS)]
        kbt = [(sb(st, "kb%d" % i, [128, 2, 256], BF16), Tk("kb%d" % i)) for i in range(NBS)]
        vbt = [(sb(st, "vb%d" % i, [128, 2, 512], BF16), Tk("vb%d" % i)) for i in range(NBS)]
        for i in range(NBS):
            sc.op("pool", lambda g, i=i: g.memset(qb[i][0][:], 0.0), [], [qb[i][1]])
            sc.op("pool", lambda g, i=i: g.memset(kbt[i][0][:], 0.0), [], [kbt[i][1]])
            sc.op("pool", lambda g, i=i: g.memset(vbt[i][0][:], 0.0), [], [vbt[i][1]])
        qkT = Ring([(sb(st, "qkT%d" % i, [128, 6, 128], BF16), Tk("qkT%d" % i)) for i in range(3)])
        pTb = Ring([(sb(st, "pTb%d" % i, [128, 2, 2, 2, 128], BF16), Tk("pTb%d" % i)) for i in range(2)])
        qa = Ring([(sb(st, "qa%d" % i, [128, 1024], BF16), Tk("qa%d" % i)) for i in range(2)])
        ka = [(sb(st, "ka%d" % i, [128, 256], BF16), Tk("ka%d" % i)) for i in range(4)]
        va = [(sb(st, "va%d" % i, [128, 512], BF16), Tk("va%d" % i)) for i in range(4)]
        kaT = [(sb(st, "kaT%d" % i, [128, 2, 128], BF16), Tk("kaT%d" % i)) for i in range(4)]
        qaT = Ring([(sb(st, "qaT%d" % i, [128, 8, 128], BF16), Tk("qaT%d" % i)) for i in range(2)])
        pA = Ring([(sb(st, "pA%d" % i, [128, 512], BF16), Tk("pA%d" % i)) for i in range(6)])
        rtmp = Ring([(sb(st, "rtmp%d" % i, [128, 2, 128], F32), Tk("rtmp%d" % i)) for i in range(3)])
        T2 = 256
        oaT = Ring([(sb(st, "oaT%d" % i, [128, 8, T2], BF16), Tk("oaT%d" % i)) for i in range(2)])
        gt = Ring([(sb(st, "gt%d" % i, [128, 16, T2], BF16), Tk("gt%d" % i)) for i in range(2)])
        xr = Ring([(sb(st, "xr%d" % i, [128, D], F32), Tk("xr%d" % i)) for i in range(2)])
        t1r = Ring([(sb(st, "t1_%d" % i, [128, T2], F32), Tk("t1_%d" % i)) for i in range(2)])
        t2r = Ring([(sb(st, "t2_%d" % i, [128, T2], F32), Tk("t2_%d" % i)) for i in range(2)])
        mixT = Ring([(sb(st, "mixT%d" % i, [128, 8, T2], BF16), Tk("mixT%d" % i)) for i in range(2)])
        pre = Ring([(sb(st, "pre%d" % i, [128, D], F32), Tk("pre%d" % i)) for i in range(2)])
        hst = Ring([(sb(st, "hst%d" % i, [128, D], F32), Tk("hst%d" % i)) for i in range(2)])
        small = Ring([(sb(st, "sm%d" % i, [128, 16], F32), Tk("sm%d" % i)) for i in range(2)])
        sc.barrier()

        s_ring = Ring([ps_ring.items[i] for i in range(4)])
        ol_ring = Ring([ps_ring.items[i] for i in (4, 5)])
        bcount = [0]

        def b_T(u):
            sbi, g, d, r, J, j0 = u["args"]
            slot = bcount[0] % NBS
            bcount[0] += 1
            q_t, qtk = qb[slot]
            k_t, ktk = kbt[slot]
            v_t, vtk = vbt[slot]
            cq = C_QB + g * 256
            ck = C_KB + g * 256
            cv = C_VB + g * 512
            t0 = r + d * j0
            sc.dma("sp", q_t[:], qkv_d[sl(t0, 128, d), cq:cq + 256], qtk, writes=[qtk])
            first = last = 0
            for kb in range(2):
                jk0 = j0 - 64 + kb * 128
                lo = max(0, -jk0)
                hi = min(128, J - jk0)
                if kb == 0 and lo > 0:
                    first = 1
                if kb == 1 and hi < 128:
                    last = 1
                ts = r + d * (jk0 + lo)
                sc.dma("sp", k_t[lo:hi, kb, :], qkv_d[sl(ts, hi - lo, d), ck:ck + 256], ktk, writes=[ktk])
                sc.dma("sp", v_t[lo:hi, kb, :], qkv_d[sl(ts, hi - lo, d), cv:cv + 512], vtk, writes=[vtk])
            pt, pttk = pt_ring.next()
            fns = []
            for c in range(2):
                fns.append(lambda e, c=c: e.transpose(out=pt[:, c * 128:(c + 1) * 128],
                                                      in_=q_t[:, c * 128:(c + 1) * 128], identity=ident[:]))
            for kb in range(2):
                for c in range(2):
                    i = 2 + kb * 2 + c
                    fns.append(lambda e, c=c, kb=kb, i=i: e.transpose(out=pt[:, i * 128:(i + 1) * 128],
                                                                      in_=k_t[:, kb, c * 128:(c + 1) * 128],
                                                                      identity=ident[:]))
            sc.op("pe", fns, [qtk, ktk], [pttk])
            qk, qktk = qkT.next()
            sc.op("act", lambda e: e.activation(out=qk[:].rearrange("p a b -> p (a b)"), in_=pt[:, 0:768],
                                                func=AF.Copy), [pttk], [qktk])
            u.update(qk=qk, qktk=qktk, v_t=v_t, vtk=vtk, first=first, last=last, t0=t0)

        def b_S(u):
            qk, qktk = u["qk"], u["qktk"]
            sbank = [s_ring.next(), s_ring.next()]
            for half in range(2):
                ps, pstk = sbank[half]
                fns = []
                for c in range(2):
                    for kb in range(2):
                        fns.append(lambda e, c=c, kb=kb, half=half, ps=ps: e.matmul(
                            ps[:, (c * 2 + kb) * 128:(c * 2 + kb + 1) * 128],
                            lhsT=qk[half * 64:(half + 1) * 64, 2 + kb * 2 + c, :],
                            rhs=qk[half * 64:(half + 1) * 64, c, :], start=True, stop=True))
                sc.op("pe", fns, [qktk], [pstk])
            p_t, ptk_ = pTb.next()
            for half in range(2):
                ps, pstk = sbank[half]
                sc.op("act", lambda e, half=half, ps=ps: e.activation(
                    out=p_t[:, half].rearrange("p c k q -> p (c k q)"), in_=ps[:, 0:512], func=AF.Exp, scale=0.125),
                    [pstk], [ptk_], ssw=True)
            pv4 = p_t[:].rearrange("p h c k q -> p (h c) k q")
            mview = mp[:, u["first"] * 2 + u["last"]].unsqueeze(1).broadcast_to([128, 4, 2, 128])
            sc.op("dve", lambda e: e.tensor_tensor(out=pv4, in0=pv4, in1=mview, op=ALU.mult), [ptk_, ctk], [ptk_])
            u.update(p_t=p_t, ptk_=ptk_)

        def b_PV(u):
            sbi, g, d, r, J, j0 = u["args"]
            p_t, ptk_, v_t, vtk = u["p_t"], u["ptk_"], u["v_t"], u["vtk"]
            ol, oltk = ol_ring.next()
            fns = []
            for c in range(2):
                for typ in range(2):
                    for kb in range(2):
                        for half in range(2):
                            h = 2 * c + half
                            hs = slice(half * 64, (half + 1) * 64)
                            col = typ * 256 + c * 128
                            if typ == 0:
                                fns.append(lambda e, hs=hs, col=col, h=h, half=half, c=c, kb=kb: e.matmul(
                                    ol[hs, col:col + 128], lhsT=v_t[:, kb, h * 128:h * 128 + 64],
                                    rhs=p_t[:, half, c, kb, :], start=(kb == 0), stop=(kb == 1)))
                            else:
                                fns.append(lambda e, hs=hs, col=col, half=half, c=c, kb=kb: e.matmul(
                                    ol[hs, col:col + 128], lhsT=ones[:, 0:64],
                                    rhs=p_t[:, half, c, kb, :], start=(kb == 0), stop=(kb == 1)))
            sc.op("pe", fns, [ptk_, vtk, ctk], [oltk])
            a0 = u["t0"] - sbi * SB
            for typ, acc in ((0, accU), (1, accL)):
                av = acc[:, :, sl(a0, 128, d)]
                pv = ol[:, typ * 256:(typ + 1) * 256].rearrange("p (c q) -> p c q", c=2)
                sc.op("dve", lambda e, av=av, pv=pv: e.tensor_tensor(out=av, in0=av, in1=pv, op=ALU.add),
                      [acctk, oltk], [acctk], ssw=True)

        def mixer_b(sbi):
            sc.op("pool", lambda e: e.memset(accU[:], 0.0), [], [acctk])
            sc.op("pool", lambda e: e.memset(accL[:], 0.0), [], [acctk])
            units = []
            for g, d in enumerate((1, 4, 16)):
                J = S // d
                nj = SB // d // 128
                for r in range(d):
                    for jb in range(nj):
                        units.append({"args": (sbi, g, d, r, J, sbi * (SB // d) + jb * 128)})
            n = len(units)
            b_T(units[0])
            b_T(units[1])
            b_S(units[0])
            for i in range(n):
                if i + 2 < n:
                    b_T(units[i + 2])
                if i + 1 < n:
                    b_S(units[i + 1])
                b_PV(units[i])
            sc.op("act", lambda e: e.activation(out=accL[:], in_=accL[:], func=AF.Ln), [acctk], [acctk])
            sc.op("act", lambda e: e.activation(out=accL[:], in_=accL[:], func=AF.Exp, scale=-1.0), [acctk], [acctk])
            sc.op("dve", lambda e: e.tensor_tensor(out=obT[:], in0=accU[:], in1=accL[:], op=ALU.mult),
                  [acctk], [obtk])

        def a_load_kv(blk):
            k_t, ktk = ka[blk % 4]
            v_t, vtk = va[blk % 4]
            sc.dma("sp", k_t[:], qkv_d[blk * 128:(blk + 1) * 128, C_KA:C_KA + 256], ktk, writes=[ktk])
            sc.dma("sp", v_t[:], qkv_d[blk * 128:(blk + 1) * 128, C_VA:C_VA + 512], vtk, writes=[vtk])
            pt, pttk = pt_ring.next()
            sc.op("pe", [lambda e, c=c: e.transpose(out=pt[:, c * 128:(c + 1) * 128], in_=k_t[:, c * 128:(c + 1) * 128],
                                                    identity=ident[:]) for c in range(2)], [ktk], [pttk])
            kt_t, kttk = kaT[blk % 4]
            sc.op("act", lambda e: e.activation(out=kt_t[:].rearrange("p a b -> p (a b)"), in_=pt[:, 0:256],
                                                func=AF.Copy), [pttk], [kttk])

        blkctx = {}

        def a_prep(i):
            if i == 0:
                a_load_kv(0)
            if i + 1 < NB:
                a_load_kv(i + 1)
            q_t, qtk = qa.next()
            sc.dma("sp", q_t[:], qkv_d[i * 128:(i + 1) * 128, 0:1024], qtk, writes=[qtk])
            pt, pttk = pt_ring.next()
            sc.op("pe", [lambda e, c=c: e.transpose(out=pt[:, c * 128:(c + 1) * 128], in_=q_t[:, c * 128:(c + 1) * 128],
                                                    identity=ident[:]) for c in range(8)], [qtk], [pttk])
            qT, qTtk = qaT.next()
            sc.op("act", lambda e: e.activation(out=qT[:].rearrange("p a b -> p (a b)"), in_=pt[:, 0:1024],
                                                func=AF.Copy), [pttk], [qTtk])
            blkctx[i] = (qT, qTtk)

        def a_S(u):
            i, k = u["i"], u["k"]
            qT, qTtk = blkctx[i]
            kbs = [kb for kb in range(3) if 0 <= i - 1 + kb < NB]
            p, half = k // 2, k % 2
            hs = slice(half * 64, (half + 1) * 64)
            plist = []
            for kb in kbs:
                blk = i - 1 + kb
                ps, pstk = s_ring.next()
                kt_t, kttk = kaT[blk % 4]
                sc.op("pe", lambda e, ps=ps, kt_t=kt_t: e.matmul(
                    ps[:, 0:512], lhsT=kt_t[hs, p, :], rhs=qT[hs, p * 4:(p + 1) * 4, :], start=True, stop=True),
                    [kttk, qTtk], [pstk])
                pp, pptk = pA.next()
                sc.op("act", lambda e, ps=ps, pp=pp: e.activation(out=pp[:], in_=ps[:, 0:512], func=AF.Exp,
                                                                  scale=0.125), [pstk], [pptk])
                if kb != 1:
                    mv = masks[:, 0 if kb == 0 else 1, :].unsqueeze(1).broadcast_to([128, 4, 128])
                    ppv = pp[:].rearrange("p (m q) -> p m q", m=4)
                    sc.op("dve",
                          lambda e, ppv=ppv, mv=mv: e.tensor_tensor(out=ppv, in0=ppv, in1=mv, op=ALU.mult),
                          [pptk, ctk], [pptk])
                plist.append((pp, pptk, blk))
            u["plist"] = plist

        def a_PV(u):
            i, k, oa, oatk, bi = u["i"], u["k"], u["oa"], u["oatk"], u["bi"]
            plist = u["plist"]
            ol, oltk = ol_ring.next()
            n = len(plist)
            fns = []
            for typ in range(2):
                for j, (pp, pptk, blk) in enumerate(plist):
                    ppv = pp[:].rearrange("p (m q) -> p m q", m=4)
                    for hf in range(2):
                        hs = slice(hf * 64, (hf + 1) * 64)
                        if typ == 0:
                            fns.append(lambda e, j=j, ppv=ppv, blk=blk, hf=hf, hs=hs: e.matmul(
                                ol[hs, 0:256], lhsT=va[blk % 4][0][:, k * 128:k * 128 + 64], rhs=ppv[:, hf::2, :],
                                start=(j == 0), stop=(j == n - 1)))
                        else:
                            fns.append(lambda e, j=j, ppv=ppv, hf=hf, hs=hs: e.matmul(
                                ol[hs, 256:512], lhsT=ones[:, 0:64], rhs=ppv[:, hf::2, :],
                                start=(j == 0), stop=(j == n - 1)))
            sc.op("pe", fns, [x[1] for x in plist] + [va[x[2] % 4][1] for x in plist] + [ctk], [oltk])
            rt, rttk = rtmp.next()
            esv = es[:, 2 * k:2 * k + 2].unsqueeze(2).broadcast_to([128, 2, 128])
            lv = ol[:, 256:512].rearrange("p (m q) -> p m q", m=2)
            ov = ol[:, 0:256].rearrange("p (m q) -> p m q", m=2)
            sc.op("dve", lambda e: e.tensor_tensor(out=rt[:], in0=lv, in1=esv, op=ALU.add), [oltk, estk], [rttk])
            sc.op("act", lambda e: e.activation(out=rt[:], in_=rt[:], func=AF.Ln), [rttk], [rttk])
            sc.op("act", lambda e: e.activation(out=rt[:], in_=rt[:], func=AF.Exp, scale=-1.0), [rttk], [rttk])
            sc.op("dve", lambda e: e.tensor_tensor(
                out=oa[:, 2 * k:2 * k + 2, bi * 128:(bi + 1) * 128], in0=ov, in1=rt[:], op=ALU.mult),
                [oltk, rttk], [oatk], ssw=True)

        def dense2(tt, oa, oatk, sbi):
            tok0 = tt * T2
            g_t, gtk = gt.next()
            sc.dma("sp", g_t[:], gT_d[:, :, tok0:tok0 + T2].rearrange("c p t -> p c t"), gtk, writes=[gtk])
            mx, mxtk = mixT.next()
            so = tok0 - sbi * SB
            for c in range(8):
                (pa, patk), (pb, pbtk) = s_ring.next(), s_ring.next()
                sc.op("pe", [lambda e, kc=kc, c=c, pa=pa: e.matmul(
                    pa[:, 0:T2], lhsT=wbra[:, kc, c * 128:(c + 1) * 128], rhs=oa[:, kc, :],
                    start=(kc == 0), stop=(kc == 7)) for kc in range(8)], [oatk], [patk])
                sc.op("pe", [lambda e, kc=kc, c=c, pb=pb: e.matmul(
                    pb[:, 0:T2], lhsT=wbrb[:, kc, c * 128:(c + 1) * 128], rhs=obT[:, kc, so:so + T2],
                    start=(kc == 0), stop=(kc == 1)) for kc in range(2)], [obtk], [pbtk])
                t1, t1tk = t1r.next()
                t2, t2tk = t2r.next()
                sc.op("dve", lambda e, t1=t1, pa=pa, c=c: e.tensor_tensor(out=t1[:], in0=pa[:, 0:T2],
                                                                          in1=g_t[:, c, :], op=ALU.mult),
                      [patk, gtk], [t1tk])
                sc.op("dve", lambda e, t2=t2, pb=pb, c=c: e.tensor_tensor(out=t2[:], in0=pb[:, 0:T2],
                                                                          in1=g_t[:, 8 + c, :], op=ALU.mult),
                      [pbtk, gtk], [t2tk])
                sc.op("pool", lambda e, t1=t1, t2=t2, c=c: e.tensor_tensor(out=mx[:, c, :], in0=t1[:], in1=t2[:],
                                                                            op=ALU.add), [t1tk, t2tk], [mxtk])
            for bi in range(T2 // 128):
                r0 = tok0 + bi * 128
                x_t, xtk_ = xr.next()
                sc.dma("sp", x_t[:], x_d[r0:r0 + 128, :], xtk_, writes=[xtk_])
                pr, prtk = pre.next()
                for n in range(2):
                    ps, pstk = s_ring.next()
                    sc.op("pe", [lambda e, kc=kc, n=n, ps=ps, bi=bi: e.matmul(
                        ps[:, 0:512], lhsT=mx[:, kc, bi * 128:(bi + 1) * 128], rhs=wout[:, kc, n * 512:(n + 1) * 512],
                        start=(kc == 0), stop=(kc == 7)) for kc in range(8)], [mxtk], [pstk])
                    sc.op("dve", lambda e, n=n, ps=ps, pr=pr, x_t=x_t: e.scalar_tensor_tensor(
                        out=pr[:, n * 512:(n + 1) * 512], in0=x_t[:, n * 512:(n + 1) * 512], scalar=ALPHA,
                        in1=ps[:, 0:512], op0=ALU.mult, op1=ALU.add), [pstk, xtk_], [prtk])
                hs_, hstk = hst.next()
                sm, smtk = small.next()
                layer_norm(pr[:], prtk, ln_g[:], ln_b[:], hs_[:], hstk, sm, smtk)
                sc.dma(STQ, h1_d[r0:r0 + 128, :], hs_[:], hstk, reads=[hstk])

        NBT = T2 // 128
        for sbi in range(S // SB):
            mixer_b(sbi)
            units = []
            for tt in range(sbi * (SB // T2), (sbi + 1) * (SB // T2)):
                oa, oatk = oaT.next()
                for bi in range(NBT):
                    for k in range(4):
                        units.append({"i": tt * NBT + bi, "k": k, "bi": bi, "tt": tt, "oa": oa, "oatk": oatk})
            n = len(units)
            a_prep(units[0]["i"])
            a_S(units[0])
            for j in range(n):
                u = units[j]
                if u["k"] == 1 and u["i"] + 1 < (sbi + 1) * (SB // 128):
                    a_prep(u["i"] + 1)
                if j + 1 < n:
                    a_S(units[j + 1])
                a_PV(u)
                if u["k"] == 3 and u["bi"] == NBT - 1:
                    dense2(u["tt"], u["oa"], u["oatk"], sbi)
        sc.barrier()

    if 2 in phases:
        with contextlib.ExitStack() as st:
            phase2(st)

    def phase3(st):
      if True:
        wst = Ring([(sb(st, "wst%d" % i, [128, 1024], F32), Tk("wst%d" % i)) for i in range(2)])
        ident, masks, ones, ctk = consts(st, wst)
        wfg, _ = load_weight(st, "wfg", wffg_d, 8, DFF, wst)
        wfu, _ = load_weight(st, "wfu", wffu_d, 8, DFF, wst)
        wfd, _ = load_weight(st, "wfd", wffd_d, NFC, D, wst)
        ln_g, lgtk = bcast_load(st, "ln2g", ln2g_d)
        ln_b, lbtk = bcast_load(st, "ln2b", ln2b_d)
        T3 = 256
        hr = Ring([(sb(st, "hr%d" % i, [128, D], F32), Tk("hr%d" % i)) for i in range(4)])
        hbf = Ring([(sb(st, "hbf%d" % i, [128, D], BF16), Tk("hbf%d" % i)) for i in range(2)])
        hT = [(sb(st, "hT%d" % i, [128, 8, T3], BF16), [Tk("hT%d_%d" % (i, b)) for b in range(2)]) for i in range(2)]
        sg = Ring([(sb(st, "sg%d" % i, [128, T3], F32), Tk("sg%d" % i)) for i in range(2)])
        guT = Ring([(sb(st, "guT%d" % i, [128, NFC, T3], BF16), Tk("guT%d" % i)) for i in range(1)])
        pre = Ring([(sb(st, "pre3_%d" % i, [128, D], F32), Tk("pre3_%d" % i)) for i in range(2)])
        ost = Ring([(sb(st, "ost%d" % i, [128, D], F32), Tk("ost%d" % i)) for i in range(2)])
        small = Ring([(sb(st, "sm3_%d" % i, [128, 16], F32), Tk("sm3_%d" % i)) for i in range(2)])
        sc.barrier()
        NT3 = S // T3
        hrs = {}

        def prep3(t):
            ht, httk = hT[t % 2]
            for b in range(2):
                h_t, htk = hr.next()
                hrs[(t, b)] = (h_t, htk)
                r0 = t * T3 + b * 128
                sc.dma("sp", h_t[:], h1_d[r0:r0 + 128, :], htk, writes=[htk])
                hb, hbtk = hbf.next()
                sc.op("pool", lambda e, hb=hb, h_t=h_t: e.tensor_copy(out=hb[:], in_=h_t[:]), [htk], [hbtk])
                pt, pttk = pt_ring.next()
                sc.op("pe", [lambda e, kc=kc, pt=pt, hb=hb: e.transpose(out=pt[:, kc * 128:(kc + 1) * 128],
                                                                        in_=hb[:, kc * 128:(kc + 1) * 128],
                                                                        identity=ident[:]) for kc in range(8)],
                      [hbtk], [pttk])
                sc.op("act", lambda e, pt=pt, b=b: e.activation(
                    out=ht[:, :, b * 128:(b + 1) * 128], in_=pt[:].rearrange("p (k t) -> p k t", k=8), func=AF.Copy),
                    [pttk], [httk[b]])

        def compute3(t):
            ht, httk = hT[t % 2]
            gu, gutk = guT.next()
            for c in range(NFC):
                (pg, pgtk), (pu, putk) = ps_ring.next(), ps_ring.next()
                sc.op("pe", [lambda e, kc=kc, c=c, pg=pg: e.matmul(
                    pg[:, 0:T3], lhsT=wfg[:, kc, c * 128:(c + 1) * 128], rhs=ht[:, kc, :],
                    start=(kc == 0), stop=(kc == 7)) for kc in range(8)], httk, [pgtk])
                sc.op("pe", [lambda e, kc=kc, c=c, pu=pu: e.matmul(
                    pu[:, 0:T3], lhsT=wfu[:, kc, c * 128:(c + 1) * 128], rhs=ht[:, kc, :],
                    start=(kc == 0), stop=(kc == 7)) for kc in range(8)], httk, [putk])
                s_t, stk_ = sg.next()
                sc.op("act", lambda e, s_t=s_t, pg=pg: e.activation(out=s_t[:], in_=pg[:, 0:T3], func=AF.Silu),
                      [pgtk], [stk_])
                sc.op("dve", lambda e, s_t=s_t, pu=pu, c=c: e.tensor_tensor(out=gu[:, c, :], in0=pu[:, 0:T3],
                                                                            in1=s_t[:], op=ALU.mult),
                      [putk, stk_], [gutk])
            for b in range(2):
                h_t, htk = hrs.pop((t, b))
                r0 = t * T3 + b * 128
                pr, prtk = pre.next()
                for n in range(2):
                    ps, pstk = ps_ring.next()
                    sc.op("pe", [lambda e, c=c, n=n, ps=ps, b=b: e.matmul(
                        ps[:, 0:512], lhsT=gu[:, c, b * 128:(b + 1) * 128], rhs=wfd[:, c, n * 512:(n + 1) * 512],
                        start=(c == 0), stop=(c == NFC - 1)) for c in range(NFC)], [gutk], [pstk])
                    sc.op("dve", lambda e, n=n, ps=ps, pr=pr, h_t=h_t: e.scalar_tensor_tensor(
                        out=pr[:, n * 512:(n + 1) * 512], in0=h_t[:, n * 512:(n + 1) * 512], scalar=ALPHA,
                        in1=ps[:, 0:512], op0=ALU.mult, op1=ALU.add), [pstk, htk], [prtk])
                o_t, otk = ost.next()
                sm, smtk = small.next()
                layer_norm(pr[:], prtk, ln_g[:], ln_b[:], o_t[:], otk, sm, smtk)
                sc.dma(STQ, out_d[r0:r0 + 128, :], o_t[:], otk, reads=[otk])

        prep3(0)
        for t in range(NT3):
            if t + 1 < NT3:
                prep3(t + 1)
            compute3(t)
        sc.barrier()

    if 3 in phases:
        with contextlib.ExitStack() as st:
            phase3(st)

    sc.emit()
    return nc


def host_consts(S):
    half = 8
    inv = (500000.0 ** (-(np.arange(half, dtype=np.float32)) / np.float32(half))).astype(np.float32)
    ang = np.arange(S, dtype=np.float32)[:, None] * inv[None, :]
    cos = np.cos(ang).astype(np.float32)
    sin = np.sin(ang).astype(np.float32)
    csn = np.concatenate([cos, cos, -sin, sin], axis=1).astype(np.float32)
    kk = np.arange(128)[:, None]
    qq = np.arange(128)[None, :]
    m = np.zeros((128, 4, 128), np.float32)
    m[:, 0] = kk >= qq
    m[:, 1] = kk <= qq
    m[:, 2] = (kk >= qq) & (kk >= 64)
    m[:, 3] = (kk <= qq) & (kk < 64)
    ident = np.eye(128, dtype=np.float32)
    return csn, m, ident


def win_perm():
    cols = []
    for p in range(2):
        for m_ in range(4):
            for half in range(2):
                h = 4 * (2 * p + half) + m_
                cols.extend(range(h * 64, (h + 1) * 64))
    cols.extend(range(1024, 1280))
    cols.extend(range(1536, 2304))
    cols.extend(range(2304, 3072))
    cols.extend(range(1280, 1536))
    cols.extend(range(3072, 3840))
    return np.asarray(cols)


def make_in_maps(inputs, S, ncores):
    f = lambda a: np.ascontiguousarray(np.asarray(a, dtype=np.float32))
    csn, m, ident = host_consts(S)
    shared = {
        "w_in": f(np.asarray(inputs["w_in"])[0][:, win_perm()]),
        "w_gate": f(inputs["w_gate"][0]),
        "bg": f(np.asarray(inputs["b_gate"])[0].reshape(16, 128).T),
        "a_sink": f(np.asarray(inputs["a_sink"])[0].reshape(4, 2, 2).transpose(2, 0, 1)[:, None].repeat(64, axis=1).reshape(128, 8)),
        "w_br_a": f(inputs["w_br_a"][0]),
        "w_br_b": f(inputs["w_br_b"][0]),
        "w_out": f(inputs["w_out"][0]),
        "ln1_g": f(inputs["ln1_g"][0]),
        "ln1_b": f(inputs["ln1_b"][0]),
        "w_ff_gate": f(inputs["w_ff_gate"][0]),
        "w_ff_up": f(inputs["w_ff_up"][0]),
        "w_ff_down": f(inputs["w_ff_down"][0]),
        "ln2_g": f(inputs["ln2_g"][0]),
        "ln2_b": f(inputs["ln2_b"][0]),
        "csn": csn, "masks": m, "ident": ident,
    }
    x = np.asarray(inputs["x"])
    maps = []
    for c in range(ncores):
        d = dict(shared)
        d["x"] = f(x[c, :S])
        maps.append(d)
    return maps


_NC_CACHE = {}


def kernel(**inputs):
    S = 8192
    n = 8
    if S not in _NC_CACHE:
        _NC_CACHE[S] = build(S)
    nc = _NC_CACHE[S]
    in_maps = make_in_maps(inputs, S, n)
    res = run_bass_kernel_spmd(nc, in_maps, core_ids=list(range(n)))
    out = np.stack([np.asarray(r["out"], dtype=np.float32).reshape(S, D) for r in res.results], axis=0)
    return out
```

```python
import contextlib
import numpy as np
import concourse.bass as bass
import concourse.mybir as mybir
from concourse.bass_utils import run_bass_kernel_spmd

F32 = mybir.dt.float32
BF16 = mybir.dt.bfloat16
AF = mybir.ActivationFunctionType
ALU = mybir.AluOpType

D = 1024
HD = 64
IN_W = 3840
ROW = 4864
C_QA, C_KA, C_QB, C_KB, C_VA, C_VB = 0, 1024, 1280, 2048, 2816, 3328
DFF = 2816
NFC = DFF // 128
ALPHA = 2.0 ** 0.25
EPS = 1e-5
SB = 2048
ENG = ("pe", "act", "dve", "pool", "sp")
import os as _os0
STQ = _os0.environ.get("STQ", "pool")


class Tk:
    __slots__ = ("w", "r", "sem", "name")

    def __init__(self, name="", sem=None):
        self.w = None
        self.r = {}
        self.sem = sem
        self.name = name


class Sched:
    def __init__(self, nc):
        self.nc = nc
        self.ops = {e: [] for e in ENG}
        self.cnt = {e: 0 for e in ENG}
        self.waited = {e: {} for e in ENG}
        self.dma_tot = {}
        self.sems = {}
        for e in ("pe", "act", "dve", "pool"):
            self.sems[e] = nc.alloc_semaphore(name="sem_" + e)
        self.ndma = 0

    def dma_sem(self):
        k = "dma%d" % self.ndma
        self.ndma += 1
        self.sems[k] = self.nc.alloc_semaphore(name=k)
        self.dma_tot[k] = 0
        return k

    def _deps(self, eng, reads, writes, pe_accum, skip_same_w=False):
        deps = {}

        def add(ev):
            if ev is None:
                return
            s, v = ev
            if deps.get(s, 0) < v:
                deps[s] = v

        for t in reads:
            add(t.w)
        for t in writes:
            if not ((pe_accum and t.w is not None and t.w[0] == "pe") or
                    (skip_same_w and t.w is not None and t.w[0] == eng)):
                add(t.w)
            for s, v in t.r.items():
                add((s, v))
        waits = []
        wd = self.waited[eng]
        for s, v in deps.items():
            if s == eng and eng in ("pe", "sp"):
                continue
            if wd.get(s, 0) < v:
                wd[s] = v
                waits.append((s, v))
        return waits

    def _mark(self, ev, reads, writes):
        s, v = ev
        for t in reads:
            if t.r.get(s, 0) < v:
                t.r[s] = v
        for t in writes:
            t.w = ev
            t.r = {}

    def op(self, eng, fns, reads=(), writes=(), pe_accum=False, ssw=False):
        if not isinstance(fns, (list, tuple)):
            fns = [fns]
        waits = self._deps(eng, reads, writes, pe_accum, ssw)
        self.cnt[eng] += 1
        ev = (eng, self.cnt[eng])
        self.ops[eng].append((waits, list(fns), (eng, 1)))
        self._mark(ev, reads, writes)
        return ev

    def dma(self, q, out, in_, tile, reads=(), writes=()):
        if tile.sem is None:
            tile.sem = self.dma_sem()
        waits = self._deps(q, reads, writes, False)
        self.dma_tot[tile.sem] += 16
        ev = (tile.sem, self.dma_tot[tile.sem])
        self.ops[q].append((waits, [lambda e, o=out, i=in_: e.dma_start(out=o, in_=i)], (tile.sem, 16)))
        self._mark(ev, reads, writes)
        return ev

    def barrier(self):
        for e in ENG:
            waits = []
            wd = self.waited[e]
            for f in ("pe", "act", "dve", "pool"):
                if f != e and self.cnt[f] > wd.get(f, 0):
                    wd[f] = self.cnt[f]
                    waits.append((f, self.cnt[f]))
            for k, v in self.dma_tot.items():
                if v > wd.get(k, 0):
                    wd[k] = v
                    waits.append((k, v))
            self.ops[e].append((waits, [], None))

    def emit(self):
        nc = self.nc
        with nc.Block() as block:
            def run(name, eng):
                for waits, fns, inc in self.ops[name]:
                    for s, v in waits:
                        eng.wait_ge(self.sems[s], v)
                    ins = None
                    for f in fns:
                        ins = f(eng)
                    if inc is not None and ins is not None:
                        ins.then_inc(self.sems[inc[0]], inc[1])

            @block.tensor
            def _(e):
                run("pe", e)

            @block.scalar
            def _(e):
                run("act", e)

            @block.vector
            def _(e):
                run("dve", e)

            @block.gpsimd
            def _(e):
                run("pool", e)

            @block.sync
            def _(e):
                run("sp", e)


def sl(start, n, step):
    return slice(start, start + step * (n - 1) + 1, step)


class Ring:
    def __init__(self, items):
        self.items = items
        self.i = 0

    def next(self):
        it = self.items[self.i % len(self.items)]
        self.i += 1
        return it


def build(S=8192, dbg=False, phases=(1, 2, 3)):
    nc = bass.Bass("TRN2", target_bir_lowering=False)
    sc = Sched(nc)
    NB = S // 128

    def din(name, shape, dt=F32):
        return nc.dram_tensor(name, list(shape), dt, kind="ExternalInput").ap()

    x_d = din("x", [S, D])
    win_d = din("w_in", [D, IN_W])
    wgate_d = din("w_gate", [D, 2048])
    bg_d = din("bg", [128, 16])
    sink_d = din("a_sink", [128, 8])
    wbra_d = din("w_br_a", [1024, D])
    wbrb_d = din("w_br_b", [256, D])
    wout_d = din("w_out", [D, D])
    ln1g_d = din("ln1_g", [D])
    ln1b_d = din("ln1_b", [D])
    wffg_d = din("w_ff_gate", [D, DFF])
    wffu_d = din("w_ff_up", [D, DFF])
    wffd_d = din("w_ff_down", [DFF, D])
    ln2g_d = din("ln2_g", [D])
    ln2b_d = din("ln2_b", [D])
    csn_d = din("csn", [S, 32])
    masks_d = din("masks", [128, 4, 128])
    ident_d = din("ident", [128, 128])

    skind = "ExternalOutput" if dbg else "Internal"
    qkv_d = nc.dram_tensor("qkv_s", [S, ROW], BF16, kind=skind).ap()
    gT_d = nc.dram_tensor("gT_s", [16, 128, S], BF16, kind=skind).ap()
    h1_d = nc.dram_tensor("h1_s", [S, D], F32, kind=skind).ap()
    out_d = nc.dram_tensor("out", [S, D], F32, kind="ExternalOutput").ap()

    psb = [nc.alloc_psum_tensor("ps%d" % i, [128, 512], F32) for i in range(6)]
    ptb = [nc.alloc_psum_tensor("pt%d" % i, [128, 1024], BF16) for i in range(2)]
    ps_ring = Ring([(psb[i], Tk("ps%d" % i)) for i in range(6)])
    pt_ring = Ring([(ptb[i], Tk("pt%d" % i)) for i in range(2)])

    uid = [0]

    def sb(st, name, shape, dt):
        uid[0] += 1
        return st.enter_context(nc.sbuf_tensor("sb%d_%s" % (uid[0], name), list(shape), dt))

    rr = {"cast": 0}

    def cast_any(out, in_, reads, writes, engs=("dve", "pool", "act")):
        e = engs[rr["cast"] % len(engs)]
        rr["cast"] += 1
        if e == "act":
            sc.op("act", lambda g: g.activation(out=out, in_=in_, func=AF.Copy), reads, writes)
        else:
            sc.op(e, lambda g: g.tensor_copy(out=out, in_=in_), reads, writes)

    def load_weight(st, name, src, KC, N, wst):
        w = sb(st, name, [128, KC, N], BF16)
        tk = Tk(name)
        for kc in range(KC):
            for n0 in range(0, N, 1024):
                n1 = min(N, n0 + 1024)
                stg, stk = wst.next()
                sc.dma("sp", stg[:, 0:n1 - n0], src[kc * 128:(kc + 1) * 128, n0:n1], stk, writes=[stk])
                cast_any(w[:, kc, n0:n1], stg[:, 0:n1 - n0], [stk], [tk])
        return w, tk

    def consts(st, wst):
        ident = sb(st, "ident", [128, 128], BF16)
        masks = sb(st, "masks", [128, 4, 128], BF16)
        ones = sb(st, "ones", [128, 128], BF16)
        ctk = Tk("consts")
        stg, stk = wst.next()
        sc.dma("sp", stg[:, 0:128], ident_d, stk, writes=[stk])
        sc.op("dve", lambda g: g.tensor_copy(out=ident[:], in_=stg[:, 0:128]), [stk], [ctk])
        stg2, stk2 = wst.next()
        sc.dma("sp", stg2[:, 0:512], masks_d.rearrange("p a b -> p (a b)"), stk2, writes=[stk2])
        sc.op("dve", lambda g: g.tensor_copy(out=masks[:].rearrange("p a b -> p (a b)"), in_=stg2[:, 0:512]),
              [stk2], [ctk])
        sc.op("dve", lambda g: g.memset(ones[:], 1.0), [], [ctk])
        return ident, masks, ones, ctk

    def layer_norm(pre, ptk, g_t, b_t, outt, otk, small, smtk):
        stats = small[:, 0:12]
        mv = small[:, 12:14]
        rstd = small[:, 14:15]
        sc.op("dve", lambda g: g.bn_stats(out=small[:, 0:6], in_=pre[:, 0:512]), [ptk], [smtk])
        sc.op("dve", lambda g: g.bn_stats(out=small[:, 6:12], in_=pre[:, 512:1024]), [ptk], [smtk])
        sc.op("dve", lambda g: g.bn_aggr(out=mv, in_=stats), [smtk], [smtk])
        sc.op("dve", lambda g: g.tensor_scalar_add(out=rstd, in0=small[:, 13:14], scalar1=EPS), [smtk], [smtk])
        sc.op("act", lambda g: g.activation(out=rstd, in_=rstd, func=AF.Ln), [smtk], [smtk])
        sc.op("act", lambda g: g.activation(out=rstd, in_=rstd, func=AF.Exp, scale=-0.5), [smtk], [smtk])
        sc.op("dve", lambda g: g.tensor_scalar(out=pre, in0=pre, scalar1=small[:, 12:13], scalar2=rstd,
                                               op0=ALU.subtract, op1=ALU.mult), [ptk, smtk], [ptk])
        sc.op("pool", lambda g: g.tensor_mul(out=pre, in0=pre, in1=g_t), [ptk], [ptk])
        sc.op("pool", lambda g: g.tensor_add(out=outt, in0=pre, in1=b_t), [ptk], [otk])

    def bcast_load(st, name, src, wst_unused=None):
        t = sb(st, name, [128, D], F32)
        tk = Tk(name)
        sc.dma("sp", t[:], src.partition_broadcast(128), tk, writes=[tk])
        return t, tk

    def phase1(st):
      if True:
        wst = Ring([(sb(st, "wst%d" % i, [128, 1024], F32), Tk("wst%d" % i)) for i in range(3)])
        ident, masks, ones, ctk = consts(st, wst)
        w_bf, wtk = load_weight(st, "w_in_bf", win_d, 8, IN_W, wst)
        wg_bf, wgtk = load_weight(st, "w_g_bf", wgate_d, 8, 2048, wst)
        bg = sb(st, "bg", [128, 16], F32)
        bgtk = Tk("bg")
        sc.dma("sp", bg[:], bg_d, bgtk, writes=[bgtk])
        sc.barrier()

        xin = Ring([(sb(st, "xin%d" % i, [128, D], F32), Tk("xin%d" % i)) for i in range(3)])
        xbf = Ring([(sb(st, "xbf%d" % i, [128, D], BF16), Tk("xbf%d" % i)) for i in range(2)])
        xT = [(sb(st, "xT%d" % i, [128, 8, 512], BF16), [Tk("xT%d_%d" % (i, b)) for b in range(4)]) for i in range(2)]
        cs = [(sb(st, "cs%d" % i, [128, 4, 32], F32), Tk("cs%d" % i)) for i in range(2)]
        stq = Ring([(sb(st, "stq%d" % i, [128, ROW], BF16), [Tk("stq%d_%d" % (i, n)) for n in range(8)])
                    for i in range(2)])
        rot = Ring([(sb(st, "rot%d" % i, [128, 44, 16], F32), Tk("rot%d" % i)) for i in range(2)])
        gst = Ring([(sb(st, "gst%d" % i, [128, 4, 512], BF16), Tk("gst%d" % i)) for i in range(2)])
        rA = Ring([(sb(st, "rA%d" % i, [128, 44, 16], F32), Tk("rA%d" % i)) for i in range(2)])
        rB = Ring([(sb(st, "rB%d" % i, [128, 44, 16], F32), Tk("rB%d" % i)) for i in range(2)])
        NT = S // 512

        def prep(t):
            xt, xtk = xT[t % 2]
            c_t, c_tk = cs[t % 2]
            sc.dma("sp", c_t[:], csn_d[t * 512:(t + 1) * 512, :].rearrange("(b p) c -> p b c", p=128), c_tk,
                   writes=[c_tk])
            for b in range(4):
                xi, xitk = xin.next()
                sc.dma("sp", xi[:], x_d[t * 512 + b * 128: t * 512 + (b + 1) * 128, :], xitk, writes=[xitk])
                xb, xbtk = xbf.next()
                sc.op("dve", lambda g, o=xb, i=xi: g.tensor_copy(out=o[:], in_=i[:]), [xitk], [xbtk])
                pt, pttk = pt_ring.next()
                sc.op("pe", [lambda g, o=pt, i=xb, kc=kc: g.transpose(out=o[:, kc * 128:(kc + 1) * 128],
                                                                       in_=i[:, kc * 128:(kc + 1) * 128],
                                                                       identity=ident[:])
                             for kc in range(8)], [xbtk], [pttk])
                sc.op("act", lambda g, o=xt, i=pt, b=b: g.activation(
                    out=o[:, :, b * 128:(b + 1) * 128], in_=i[:].rearrange("p (k t) -> p k t", k=8), func=AF.Copy),
                    [pttk], [xtk[b]])

        def v_chunk(ps, pstk, pcol0, nh, sq, sqtk, col0):
            pv = ps[:, pcol0:pcol0 + nh * 64].rearrange("p (h d) -> p h d", d=64).unsqueeze(2).broadcast_to(
                [128, nh, 2, 64])
            ov = sq[:, col0:col0 + nh * 128].rearrange("p (h r d) -> p h r d", r=2, d=64)
            sc.op("act", lambda g: g.activation(out=ov, in_=pv, func=AF.Copy), [pstk], [sqtk])

        def compute(t):
            xt, xtk = xT[t % 2]
            c_t, c_tk = cs[t % 2]
            for b in range(4):
                sq, sqtks = stq.next()
                rt_, rtk = rot.next()
                for n in range(8):
                    wdt = 512 if n < 7 else 256
                    ps, pstk = ps_ring.next()
                    sc.op("pe", [lambda g, o=ps, kc=kc, n=n, wdt=wdt, b=b: g.matmul(
                        o[:, 0:wdt], lhsT=xt[:, kc, b * 128:(b + 1) * 128], rhs=w_bf[:, kc, n * 512:n * 512 + wdt],
                        start=(kc == 0), stop=(kc == 7)) for kc in range(8)], [xtk[b]], [pstk])
                    nh = 8 if n <= 4 else (4 if n == 5 else 0)
                    if nh:
                        sc.op("act", lambda g, ps=ps, sq=sq, n=n, nh=nh: g.activation(
                            out=sq[:, n * 512:n * 512 + nh * 64], in_=ps[:, 0:nh * 64], func=AF.Copy),
                            [pstk], [sqtks[n]])
                        sc.op("act", lambda g, ps=ps, rt_=rt_, n=n, nh=nh: g.activation(
                            out=rt_[:, 8 * n:8 * n + nh, :],
                            in_=ps[:, 0:nh * 64].rearrange("p (h d) -> p h d", d=64)[:, :, 0:16], func=AF.Copy),
                            [pstk], [rtk])
                    if n == 5:
                        v_chunk(ps, pstk, 256, 4, sq, sqtks[n], C_VA)
                    elif n == 6:
                        v_chunk(ps, pstk, 0, 8, sq, sqtks[n], C_VB)
                    elif n == 7:
                        v_chunk(ps, pstk, 0, 4, sq, sqtks[n], C_VB + 1024)
                a_t, atk = rA.next()
                b_t, btk = rB.next()
                cc = c_t[:, b, 0:16].unsqueeze(1).broadcast_to([128, 44, 16])
                ns = c_t[:, b, 16:24].unsqueeze(1).broadcast_to([128, 44, 8])
                ps_ = c_t[:, b, 24:32].unsqueeze(1).broadcast_to([128, 44, 8])
                sc.op("dve", lambda g, a_t=a_t, rt_=rt_, cc=cc: g.tensor_tensor(
                    out=a_t[:], in0=rt_[:], in1=cc, op=ALU.mult), [rtk, c_tk], [atk])
                sc.op("pool", lambda g, b_t=b_t, rt_=rt_, ns=ns: g.tensor_tensor(
                    out=b_t[:, :, 0:8], in0=rt_[:, :, 8:16], in1=ns, op=ALU.mult), [rtk, c_tk], [btk])
                sc.op("pool", lambda g, b_t=b_t, rt_=rt_, ps_=ps_: g.tensor_tensor(
                    out=b_t[:, :, 8:16], in0=rt_[:, :, 0:8], in1=ps_, op=ALU.mult), [rtk, c_tk], [btk])
                ov = sq[:, 0:2816].rearrange("p (h d) -> p h d", d=64)[:, :, 0:16]
                sc.op("dve", lambda g, ov=ov, a_t=a_t, b_t=b_t: g.tensor_tensor(
                    out=ov, in0=a_t[:], in1=b_t[:], op=ALU.add), [atk, btk], sqtks[0:6])
                r0 = t * 512 + b * 128
                sc.dma(STQ, qkv_d[r0:r0 + 128, :], sq[:], sqtks[0], reads=sqtks)
            for gc in range(16):
                ps, pstk = ps_ring.next()
                sc.op("pe", [lambda g, o=ps, kc=kc, gc=gc: g.matmul(
                    o[:, 0:512], lhsT=wg_bf[:, kc, gc * 128:(gc + 1) * 128], rhs=xt[:, kc, :],
                    start=(kc == 0), stop=(kc == 7)) for kc in range(8)], xtk, [pstk])
                if gc % 4 == 0:
                    gs, gstk = gst.next()
                sc.op("act", lambda g, o=gs, i=ps, gc=gc: g.activation(
                    out=o[:, gc % 4, :], in_=i[:, 0:512], func=AF.Sigmoid, bias=bg[:, gc:gc + 1]), [pstk, bgtk], [gstk])
                if gc % 4 == 3:
                    c0 = gc - 3
                    sc.dma(STQ, gT_d[c0:c0 + 4, :, t * 512:(t + 1) * 512].rearrange("c p t -> p c t"), gs[:],
                           gstk, reads=[gstk])

        import os
        cut = int(os.environ.get("P1CUT", "9"))
        if cut >= 1:
            prep(0)
        for t in range(NT if cut >= 2 else 0):
            if t + 1 < NT:
                prep(t + 1)
            compute(t)
        sc.barrier()

    if 1 in phases:
        with contextlib.ExitStack() as st:
            phase1(st)

    def phase2(st):
      if True:
        wst = Ring([(sb(st, "wst%d" % i, [128, 1024], F32), Tk("wst%d" % i)) for i in range(2)])
        ident, masks, ones, ctk = consts(st, wst)
        wbra, _ = load_weight(st, "wbra", wbra_d, 8, D, wst)
        wbrb, _ = load_weight(st, "wbrb", wbrb_d, 2, D, wst)
        wout, _ = load_weight(st, "wout", wout_d, 8, D, wst)
        ln_g, lgtk = bcast_load(st, "ln1g", ln1g_d)
        ln_b, lbtk = bcast_load(st, "ln1b", ln1b_d)
        es = sb(st, "es", [128, 8], F32)
        estk = Tk("es")
        sc.dma("sp", es[:], sink_d, estk, writes=[estk])
        sc.op("act", lambda g: g.activation(out=es[:], in_=es[:], func=AF.Exp), [estk], [estk])
        mp = sb(st, "mp", [128, 4, 2, 128], BF16)
        for f in range(2):
            for l in range(2):
                i = f * 2 + l
                sc.op("dve", lambda g, i=i, f=f: g.tensor_copy(out=mp[:, i, 0, :], in_=masks[:, 2 if f else 0, :]),
                      [ctk], [ctk])
                sc.op("dve", lambda g, i=i, l=l: g.tensor_copy(out=mp[:, i, 1, :], in_=masks[:, 3 if l else 1, :]),
                      [ctk], [ctk])
        accU = sb(st, "accU", [128, 2, SB], F32)
        accL = sb(st, "accL", [128, 2, SB], F32)
        acctk = Tk("acc")
        obT = sb(st, "obT", [128, 2, SB], BF16)
        obtk = Tk("obT")
        NBS = 5
        qb = [(sb(st, "qb%d" % i, [128, 256], BF16), Tk("qb%d" % i)) for i in range(NBS)]
        kbt = [(sb(st, "kb%d" % i, [128, 2, 256], BF16), Tk("kb%d" % i)) for i in range(NBS)]
        vbt = [(sb(st, "vb%d" % i, [128, 2, 512], BF16), Tk("vb%d" % i)) for i in range(NBS)]
        for i in range(NBS):
            sc.op("pool", lambda g, i=i: g.memset(qb[i][0][:], 0.0), [], [qb[i][1]])
            sc.op("pool", lambda g, i=i: g.memset(kbt[i][0][:], 0.0), [], [kbt[i][1]])
            sc.op("pool", lambda g, i=i: g.memset(vbt[i][0][:], 0.0), [], [vbt[i][1]])
        qkT = Ring([(sb(st, "qkT%d" % i, [128, 6, 128], BF16), Tk("qkT%d" % i)) for i in range(3)])
        pTb = Ring([(sb(st, "pTb%d" % i, [128, 2, 2, 2, 128], BF16), Tk("pTb%d" % i)) for i in range(2)])
        qa = Ring([(sb(st, "qa%d" % i, [128, 1024], BF16), Tk("qa%d" % i)) for i in range(2)])
        ka = [(sb(st, "ka%d" % i, [128, 256], BF16), Tk("ka%d" % i)) for i in range(4)]
        va = [(sb(st, "va%d" % i, [128, 512], BF16), Tk("va%d" % i)) for i in range(4)]
        kaT = [(sb(st, "kaT%d" % i, [128, 2, 128], BF16), Tk("kaT%d" % i)) for i in range(4)]
        qaT = Ring([(sb(st, "qaT%d" % i, [128, 8, 128], BF16), Tk("qaT%d" % i)) for i in range(2)])
        pA = Ring([(sb(st, "pA%d" % i, [128, 512], BF16), Tk("pA%d" % i)) for i in range(6)])
        rtmp = Ring([(sb(st, "rtmp%d" % i, [128, 2, 128], F32), Tk("rtmp%d" % i)) for i in range(3)])
        T2 = 256
        oaT = Ring([(sb(st, "oaT%d" % i, [128, 8, T2], BF16), Tk("oaT%d" % i)) for i in range(2)])
        gt = Ring([(sb(st, "gt%d" % i, [128, 16, T2], BF16), Tk("gt%d" % i)) for i in range(2)])
        xr = Ring([(sb(st, "xr%d" % i, [128, D], F32), Tk("xr%d" % i)) for i in range(2)])
        t1r = Ring([(sb(st, "t1_%d" % i, [128, T2], F32), Tk("t1_%d" % i)) for i in range(2)])
        t2r = Ring([(sb(st, "t2_%d" % i, [128, T2], F32), Tk("t2_%d" % i)) for i in range(2)])
        mixT = Ring([(sb(st, "mixT%d" % i, [128, 8, T2], BF16), Tk("mixT%d" % i)) for i in range(2)])
        pre = Ring([(sb(st, "pre%d" % i, [128, D], F32), Tk("pre%d" % i)) for i in range(2)])
        hst = Ring([(sb(st, "hst%d" % i, [128, D], F32), Tk("hst%d" % i)) for i in range(2)])
        small = Ring([(sb(st, "sm%d" % i, [128, 16], F32), Tk("sm%d" % i)) for i in range(2)])
        sc.barrier()

        s_ring = Ring([ps_ring.items[i] for i in range(4)])
        ol_ring = Ring([ps_ring.items[i] for i in (4, 5)])
        bcount = [0]

        def b_T(u):
            sbi, g, d, r, J, j0 = u["args"]
            slot = bcount[0] % NBS
            bcount[0] += 1
            q_t, qtk = qb[slot]
            k_t, ktk = kbt[slot]
            v_t, vtk = vbt[slot]
            cq = C_QB + g * 256
            ck = C_KB + g * 256
            cv = C_VB + g * 512
            t0 = r + d * j0
            sc.dma("sp", q_t[:], qkv_d[sl(t0, 128, d), cq:cq + 256], qtk, writes=[qtk])
            first = last = 0
            for kb in range(2):
                jk0 = j0 - 64 + kb * 128
                lo = max(0, -jk0)
                hi = min(128, J - jk0)
                if kb == 0 and lo > 0:
                    first = 1
                if kb == 1 and hi < 128:
                    last = 1
                ts = r + d * (jk0 + lo)
                sc.dma("sp", k_t[lo:hi, kb, :], qkv_d[sl(ts, hi - lo, d), ck:ck + 256], ktk, writes=[ktk])
                sc.dma("sp", v_t[lo:hi, kb, :], qkv_d[sl(ts, hi - lo, d), cv:cv + 512], vtk, writes=[vtk])
            pt, pttk = pt_ring.next()
            fns = []
            for c in range(2):
                fns.append(lambda e, c=c: e.transpose(out=pt[:, c * 128:(c + 1) * 128],
                                                      in_=q_t[:, c * 128:(c + 1) * 128], identity=ident[:]))
            for kb in range(2):
                for c in range(2):
                    i = 2 + kb * 2 + c
                    fns.append(lambda e, c=c, kb=kb, i=i: e.transpose(out=pt[:, i * 128:(i + 1) * 128],
                                                                      in_=k_t[:, kb, c * 128:(c + 1) * 128],
                                                                      identity=ident[:]))
            sc.op("pe", fns, [qtk, ktk], [pttk])
            qk, qktk = qkT.next()
            sc.op("act", lambda e: e.activation(out=qk[:].rearrange("p a b -> p (a b)"), in_=pt[:, 0:768],
                                                func=AF.Copy), [pttk], [qktk])
            u.update(qk=qk, qktk=qktk, v_t=v_t, vtk=vtk, first=first, last=last, t0=t0)

        def b_S(u):
            qk, qktk = u["qk"], u["qktk"]
            sbank = [s_ring.next(), s_ring.next()]
            for half in range(2):
                ps, pstk = sbank[half]
                fns = []
                for c in range(2):
                    for kb in range(2):
                        fns.append(lambda e, c=c, kb=kb, half=half, ps=ps: e.matmul(
                            ps[:, (c * 2 + kb) * 128:(c * 2 + kb + 1) * 128],
                            lhsT=qk[half * 64:(half + 1) * 64, 2 + kb * 2 + c, :],
                            rhs=qk[half * 64:(half + 1) * 64, c, :], start=True, stop=True))
                sc.op("pe", fns, [qktk], [pstk])
            p_t, ptk_ = pTb.next()
            for half in range(2):
                ps, pstk = sbank[half]
                sc.op("act", lambda e, half=half, ps=ps: e.activation(
                    out=p_t[:, half].rearrange("p c k q -> p (c k q)"), in_=ps[:, 0:512], func=AF.Exp, scale=0.125),
                    [pstk], [ptk_], ssw=True)
            pv4 = p_t[:].rearrange("p h c k q -> p (h c) k q")
            mview = mp[:, u["first"] * 2 + u["last"]].unsqueeze(1).broadcast_to([128, 4, 2, 128])
            sc.op("dve", lambda e: e.tensor_tensor(out=pv4, in0=pv4, in1=mview, op=ALU.mult), [ptk_, ctk], [ptk_])
            u.update(p_t=p_t, ptk_=ptk_)

        def b_PV(u):
            sbi, g, d, r, J, j0 = u["args"]
            p_t, ptk_, v_t, vtk = u["p_t"], u["ptk_"], u["v_t"], u["vtk"]
            ol, oltk = ol_ring.next()
            fns = []
            for c in range(2):
                for typ in range(2):
                    for kb in range(2):
                        for half in range(2):
                            h = 2 * c + half
                            hs = slice(half * 64, (half + 1) * 64)
                            col = typ * 256 + c * 128
                            if typ == 0:
                                fns.append(lambda e, hs=hs, col=col, h=h, half=half, c=c, kb=kb: e.matmul(
                                    ol[hs, col:col + 128], lhsT=v_t[:, kb, h * 128:h * 128 + 64],
                                    rhs=p_t[:, half, c, kb, :], start=(kb == 0), stop=(kb == 1)))
                            else:
                                fns.append(lambda e, hs=hs, col=col, half=half, c=c, kb=kb: e.matmul(
                                    ol[hs, col:col + 128], lhsT=ones[:, 0:64],
                                    rhs=p_t[:, half, c, kb, :], start=(kb == 0), stop=(kb == 1)))
            sc.op("pe", fns, [ptk_, vtk, ctk], [oltk])
            a0 = u["t0"] - sbi * SB
            for typ, acc in ((0, accU), (1, accL)):
                av = acc[:, :, sl(a0, 128, d)]
                pv = ol[:, typ * 256:(typ + 1) * 256].rearrange("p (c q) -> p c q", c=2)
                sc.op("dve", lambda e, av=av, pv=pv: e.tensor_tensor(out=av, in0=av, in1=pv, op=ALU.add),
                      [acctk, oltk], [acctk], ssw=True)

        def mixer_b(sbi):
            sc.op("pool", lambda e: e.memset(accU[:], 0.0), [], [acctk])
            sc.op("pool", lambda e: e.memset(accL[:], 0.0), [], [acctk])
            units = []
            for g, d in enumerate((1, 4, 16)):
                J = S // d
                nj = SB // d // 128
                for r in range(d):
                    for jb in range(nj):
                        units.append({"args": (sbi, g, d, r, J, sbi * (SB // d) + jb * 128)})
            n = len(units)
            b_T(units[0])
            b_T(units[1])
            b_S(units[0])
            for i in range(n):
                if i + 2 < n:
                    b_T(units[i + 2])
                if i + 1 < n:
                    b_S(units[i + 1])
                b_PV(units[i])
            sc.op("act", lambda e: e.activation(out=accL[:], in_=accL[:], func=AF.Ln), [acctk], [acctk])
            sc.op("act", lambda e: e.activation(out=accL[:], in_=accL[:], func=AF.Exp, scale=-1.0), [acctk], [acctk])
            sc.op("dve", lambda e: e.tensor_tensor(out=obT[:], in0=accU[:], in1=accL[:], op=ALU.mult),
                  [acctk], [obtk])

        def a_load_kv(blk):
            k_t, ktk = ka[blk % 4]
            v_t, vtk = va[blk % 4]
            sc.dma("sp", k_t[:], qkv_d[blk * 128:(blk + 1) * 128, C_KA:C_KA + 256], ktk, writes=[ktk])
            sc.dma("sp", v_t[:], qkv_d[blk * 128:(blk + 1) * 128, C_VA:C_VA + 512], vtk, writes=[vtk])
            pt, pttk = pt_ring.next()
            sc.op("pe", [lambda e, c=c: e.transpose(out=pt[:, c * 128:(c + 1) * 128], in_=k_t[:, c * 128:(c + 1) * 128],
                                                    identity=ident[:]) for c in range(2)], [ktk], [pttk])
            kt_t, kttk = kaT[blk % 4]
            sc.op("act", lambda e: e.activation(out=kt_t[:].rearrange("p a b -> p (a b)"), in_=pt[:, 0:256],
                                                func=AF.Copy), [pttk], [kttk])

        blkctx = {}

        def a_prep(i):
            if i == 0:
                a_load_kv(0)
            if i + 1 < NB:
                a_load_kv(i + 1)
            q_t, qtk = qa.next()
            sc.dma("sp", q_t[:], qkv_d[i * 128:(i + 1) * 128, 0:1024], qtk, writes=[qtk])
            pt, pttk = pt_ring.next()
            sc.op("pe", [lambda e, c=c: e.transpose(out=pt[:, c * 128:(c + 1) * 128], in_=q_t[:, c * 128:(c + 1) * 128],
                                                    identity=ident[:]) for c in range(8)], [qtk], [pttk])
            qT, qTtk = qaT.next()
            sc.op("act", lambda e: e.activation(out=qT[:].rearrange("p a b -> p (a b)"), in_=pt[:, 0:1024],
                                                func=AF.Copy), [pttk], [qTtk])
            blkctx[i] = (qT, qTtk)

        def a_S(u):
            i, k = u["i"], u["k"]
            qT, qTtk = blkctx[i]
            kbs = [kb for kb in range(3) if 0 <= i - 1 + kb < NB]
            p, half = k // 2, k % 2
            hs = slice(half * 64, (half + 1) * 64)
            plist = []
            for kb in kbs:
                blk = i - 1 + kb
                ps, pstk = s_ring.next()
                kt_t, kttk = kaT[blk % 4]
                sc.op("pe", lambda e, ps=ps, kt_t=kt_t: e.matmul(
                    ps[:, 0:512], lhsT=kt_t[hs, p, :], rhs=qT[hs, p * 4:(p + 1) * 4, :], start=True, stop=True),
                    [kttk, qTtk], [pstk])
                pp, pptk = pA.next()
                sc.op("act", lambda e, ps=ps, pp=pp: e.activation(out=pp[:], in_=ps[:, 0:512], func=AF.Exp,
                                                                  scale=0.125), [pstk], [pptk])
                if kb != 1:
                    mv = masks[:, 0 if kb == 0 else 1, :].unsqueeze(1).broadcast_to([128, 4, 128])
                    ppv = pp[:].rearrange("p (m q) -> p m q", m=4)
                    sc.op("dve",
                          lambda e, ppv=ppv, mv=mv: e.tensor_tensor(out=ppv, in0=ppv, in1=mv, op=ALU.mult),
                          [pptk, ctk], [pptk])
                plist.append((pp, pptk, blk))
            u["plist"] = plist

        def a_PV(u):
            i, k, oa, oatk, bi = u["i"], u["k"], u["oa"], u["oatk"], u["bi"]
            plist = u["plist"]
            ol, oltk = ol_ring.next()
            n = len(plist)
            fns = []
            for typ in range(2):
                for j, (pp, pptk, blk) in enumerate(plist):
                    ppv = pp[:].rearrange("p (m q) -> p m q", m=4)
                    for hf in range(2):
                        hs = slice(hf * 64, (hf + 1) * 64)
                        if typ == 0:
                            fns.append(lambda e, j=j, ppv=ppv, blk=blk, hf=hf, hs=hs: e.matmul(
                                ol[hs, 0:256], lhsT=va[blk % 4][0][:, k * 128:k * 128 + 64], rhs=ppv[:, hf::2, :],
                                start=(j == 0), stop=(j == n - 1)))
                        else:
                            fns.append(lambda e, j=j, ppv=ppv, hf=hf, hs=hs: e.matmul(
                                ol[hs, 256:512], lhsT=ones[:, 0:64], rhs=ppv[:, hf::2, :],
                                start=(j == 0), stop=(j == n - 1)))
            sc.op("pe", fns, [x[1] for x in plist] + [va[x[2] % 4][1] for x in plist] + [ctk], [oltk])
            rt, rttk = rtmp.next()
            esv = es[:, 2 * k:2 * k + 2].unsqueeze(2).broadcast_to([128, 2, 128])
            lv = ol[:, 256:512].rearrange("p (m q) -> p m q", m=2)
            ov = ol[:, 0:256].rearrange("p (m q) -> p m q", m=2)
            sc.op("dve", lambda e: e.tensor_tensor(out=rt[:], in0=lv, in1=esv, op=ALU.add), [oltk, estk], [rttk])
            sc.op("act", lambda e: e.activation(out=rt[:], in_=rt[:], func=AF.Ln), [rttk], [rttk])
            sc.op("act", lambda e: e.activation(out=rt[:], in_=rt[:], func=AF.Exp, scale=-1.0), [rttk], [rttk])
            sc.op("dve", lambda e: e.tensor_tensor(
                out=oa[:, 2 * k:2 * k + 2, bi * 128:(bi + 1) * 128], in0=ov, in1=rt[:], op=ALU.mult),
                [oltk, rttk], [oatk], ssw=True)

        def dense2(tt, oa, oatk, sbi):
            tok0 = tt * T2
            g_t, gtk = gt.next()
            sc.dma("sp", g_t[:], gT_d[:, :, tok0:tok0 + T2].rearrange("c p t -> p c t"), gtk, writes=[gtk])
            mx, mxtk = mixT.next()
            so = tok0 - sbi * SB
            for c in range(8):
                (pa, patk), (pb, pbtk) = s_ring.next(), s_ring.next()
                sc.op("pe", [lambda e, kc=kc, c=c, pa=pa: e.matmul(
                    pa[:, 0:T2], lhsT=wbra[:, kc, c * 128:(c + 1) * 128], rhs=oa[:, kc, :],
                    start=(kc == 0), stop=(kc == 7)) for kc in range(8)], [oatk], [patk])
                sc.op("pe", [lambda e, kc=kc, c=c, pb=pb: e.matmul(
                    pb[:, 0:T2], lhsT=wbrb[:, kc, c * 128:(c + 1) * 128], rhs=obT[:, kc, so:so + T2],
                    start=(kc == 0), stop=(kc == 1)) for kc in range(2)], [obtk], [pbtk])
                t1, t1tk = t1r.next()
                t2, t2tk = t2r.next()
                sc.op("dve", lambda e, t1=t1, pa=pa, c=c: e.tensor_tensor(out=t1[:], in0=pa[:, 0:T2],
                                                                          in1=g_t[:, c, :], op=ALU.mult),
                      [patk, gtk], [t1tk])
                sc.op("dve", lambda e, t2=t2, pb=pb, c=c: e.tensor_tensor(out=t2[:], in0=pb[:, 0:T2],
                                                                          in1=g_t[:, 8 + c, :], op=ALU.mult),
                      [pbtk, gtk], [t2tk])
                sc.op("pool", lambda e, t1=t1, t2=t2, c=c: e.tensor_tensor(out=mx[:, c, :], in0=t1[:], in1=t2[:],
                                                                            op=ALU.add), [t1tk, t2tk], [mxtk])
            for bi in range(T2 // 128):
                r0 = tok0 + bi * 128
                x_t, xtk_ = xr.next()
                sc.dma("sp", x_t[:], x_d[r0:r0 + 128, :], xtk_, writes=[xtk_])
                pr, prtk = pre.next()
                for n in range(2):
                    ps, pstk = s_ring.next()
                    sc.op("pe", [lambda e, kc=kc, n=n, ps=ps, bi=bi: e.matmul(
                        ps[:, 0:512], lhsT=mx[:, kc, bi * 128:(bi + 1) * 128], rhs=wout[:, kc, n * 512:(n + 1) * 512],
                        start=(kc == 0), stop=(kc == 7)) for kc in range(8)], [mxtk], [pstk])
                    sc.op("dve", lambda e, n=n, ps=ps, pr=pr, x_t=x_t: e.scalar_tensor_tensor(
                        out=pr[:, n * 512:(n + 1) * 512], in0=x_t[:, n * 512:(n + 1) * 512], scalar=ALPHA,
                        in1=ps[:, 0:512], op0=ALU.mult, op1=ALU.add), [pstk, xtk_], [prtk])
                hs_, hstk = hst.next()
                sm, smtk = small.next()
                layer_norm(pr[:], prtk, ln_g[:], ln_b[:], hs_[:], hstk, sm, smtk)
                sc.dma(STQ, h1_d[r0:r0 + 128, :], hs_[:], hstk, reads=[hstk])

        NBT = T2 // 128
        for sbi in range(S // SB):
            mixer_b(sbi)
            units = []
            for tt in range(sbi * (SB // T2), (sbi + 1) * (SB // T2)):
                oa, oatk = oaT.next()
                for bi in range(NBT):
                    for k in range(4):
                        units.append({"i": tt * NBT + bi, "k": k, "bi": bi, "tt": tt, "oa": oa, "oatk": oatk})
            n = len(units)
            a_prep(units[0]["i"])
            a_S(units[0])
            pending = None
            for j in range(n):
                u = units[j]
                if u["k"] == 1 and u["i"] + 1 < (sbi + 1) * (SB // 128):
                    a_prep(u["i"] + 1)
                if j + 1 < n:
                    a_S(units[j + 1])
                a_PV(u)
                if pending is not None:
                    dense2(*pending)
                    pending = None
                if u["k"] == 3 and u["bi"] == NBT - 1:
                    pending = (u["tt"], u["oa"], u["oatk"], sbi)
            if pending is not None:
                dense2(*pending)
        sc.barrier()

    if 2 in phases:
        with contextlib.ExitStack() as st:
            phase2(st)

    def phase3(st):
      if True:
        wst = Ring([(sb(st, "wst%d" % i, [128, 1024], F32), Tk("wst%d" % i)) for i in range(2)])
        ident, masks, ones, ctk = consts(st, wst)
        wfg, _ = load_weight(st, "wfg", wffg_d, 8, DFF, wst)
        wfu, _ = load_weight(st, "wfu", wffu_d, 8, DFF, wst)
        wfd, _ = load_weight(st, "wfd", wffd_d, NFC, D, wst)
        ln_g, lgtk = bcast_load(st, "ln2g", ln2g_d)
        ln_b, lbtk = bcast_load(st, "ln2b", ln2b_d)
        T3 = 256
        hc = Ring([(sb(st, "hc%d" % i, [128, D], F32), Tk("hc%d" % i)) for i in range(2)])
        hres = Ring([(sb(st, "hres%d" % i, [128, D], F32), Tk("hres%d" % i)) for i in range(2)])
        hbf = Ring([(sb(st, "hbf%d" % i, [128, D], BF16), Tk("hbf%d" % i)) for i in range(2)])
        hT = [(sb(st, "hT%d" % i, [128, 8, T3], BF16), [Tk("hT%d_%d" % (i, b)) for b in range(2)]) for i in range(2)]
        sg = Ring([(sb(st, "sg%d" % i, [128, T3], F32), Tk("sg%d" % i)) for i in range(2)])
        guT = Ring([(sb(st, "guT%d" % i, [128, NFC, T3], BF16), Tk("guT%d" % i)) for i in range(1)])
        pre = Ring([(sb(st, "pre3_%d" % i, [128, D], F32), Tk("pre3_%d" % i)) for i in range(2)])
        ost = Ring([(sb(st, "ost%d" % i, [128, D], F32), Tk("ost%d" % i)) for i in range(2)])
        small = Ring([(sb(st, "sm3_%d" % i, [128, 16], F32), Tk("sm3_%d" % i)) for i in range(2)])
        sc.barrier()
        NT3 = S // T3
        hrs = {}

        def prep3(t):
            ht, httk = hT[t % 2]
            for b in range(2):
                h_t, htk = hc.next()
                r0 = t * T3 + b * 128
                sc.dma("sp", h_t[:], h1_d[r0:r0 + 128, :], htk, writes=[htk])
                hb, hbtk = hbf.next()
                sc.op("dve", lambda e, hb=hb, h_t=h_t: e.tensor_copy(out=hb[:], in_=h_t[:]), [htk], [hbtk])
                pt, pttk = pt_ring.next()
                sc.op("pe", [lambda e, kc=kc, pt=pt, hb=hb: e.transpose(out=pt[:, kc * 128:(kc + 1) * 128],
                                                                        in_=hb[:, kc * 128:(kc + 1) * 128],
                                                                        identity=ident[:]) for kc in range(8)],
                      [hbtk], [pttk])
                sc.op("act", lambda e, pt=pt, b=b: e.activation(
                    out=ht[:, :, b * 128:(b + 1) * 128], in_=pt[:].rearrange("p (k t) -> p k t", k=8), func=AF.Copy),
                    [pttk], [httk[b]])

        def compute3(t):
            ht, httk = hT[t % 2]
            gu, gutk = guT.next()
            for b in range(2):
                h_t, htk = hres.next()
                hrs[(t, b)] = (h_t, htk)
                r0 = t * T3 + b * 128
                sc.dma("sp", h_t[:], h1_d[r0:r0 + 128, :], htk, writes=[htk])
            for c in range(NFC):
                (pg, pgtk), (pu, putk) = ps_ring.next(), ps_ring.next()
                sc.op("pe", [lambda e, kc=kc, c=c, pg=pg: e.matmul(
                    pg[:, 0:T3], lhsT=wfg[:, kc, c * 128:(c + 1) * 128], rhs=ht[:, kc, :],
                    start=(kc == 0), stop=(kc == 7)) for kc in range(8)], httk, [pgtk])
                sc.op("pe", [lambda e, kc=kc, c=c, pu=pu: e.matmul(
                    pu[:, 0:T3], lhsT=wfu[:, kc, c * 128:(c + 1) * 128], rhs=ht[:, kc, :],
                    start=(kc == 0), stop=(kc == 7)) for kc in range(8)], httk, [putk])
                s_t, stk_ = sg.next()
                sc.op("act", lambda e, s_t=s_t, pg=pg: e.activation(out=s_t[:], in_=pg[:, 0:T3], func=AF.Silu),
                      [pgtk], [stk_])
                sc.op("dve", lambda e, s_t=s_t, pu=pu, c=c: e.tensor_tensor(out=gu[:, c, :], in0=pu[:, 0:T3],
                                                                            in1=s_t[:], op=ALU.mult),
                      [putk, stk_], [gutk])
            for b in range(2):
                h_t, htk = hrs.pop((t, b))
                r0 = t * T3 + b * 128
                pr, prtk = pre.next()
                for n in range(2):
                    ps, pstk = ps_ring.next()
                    sc.op("pe", [lambda e, c=c, n=n, ps=ps, b=b: e.matmul(
                        ps[:, 0:512], lhsT=gu[:, c, b * 128:(b + 1) * 128], rhs=wfd[:, c, n * 512:(n + 1) * 512],
                        start=(c == 0), stop=(c == NFC - 1)) for c in range(NFC)], [gutk], [pstk])
                    sc.op("dve", lambda e, n=n, ps=ps, pr=pr, h_t=h_t: e.scalar_tensor_tensor(
                        out=pr[:, n * 512:(n + 1) * 512], in0=h_t[:, n * 512:(n + 1) * 512], scalar=ALPHA,
                        in1=ps[:, 0:512], op0=ALU.mult, op1=ALU.add), [pstk, htk], [prtk])
                o_t, otk = ost.next()
                sm, smtk = small.next()
                layer_norm(pr[:], prtk, ln_g[:], ln_b[:], o_t[:], otk, sm, smtk)
                sc.dma(STQ, out_d[r0:r0 + 128, :], o_t[:], otk, reads=[otk])

        prep3(0)
        for t in range(NT3):
            if t + 1 < NT3:
                prep3(t + 1)
            compute3(t)
        sc.barrier()

    if 3 in phases:
        with contextlib.ExitStack() as st:
            phase3(st)

    sc.emit()
    return nc


def host_consts(S):
    half = 8
    inv = (500000.0 ** (-(np.arange(half, dtype=np.float32)) / np.float32(half))).astype(np.float32)
    ang = np.arange(S, dtype=np.float32)[:, None] * inv[None, :]
    cos = np.cos(ang).astype(np.float32)
    sin = np.sin(ang).astype(np.float32)
    csn = np.concatenate([cos, cos, -sin, sin], axis=1).astype(np.float32)
    kk = np.arange(128)[:, None]
    qq = np.arange(128)[None, :]
    m = np.zeros((128, 4, 128), np.float32)
    m[:, 0] = kk >= qq
    m[:, 1] = kk <= qq
    m[:, 2] = (kk >= qq) & (kk >= 64)
    m[:, 3] = (kk <= qq) & (kk < 64)
    ident = np.eye(128, dtype=np.float32)
    return csn, m, ident


def win_perm():
    cols = []
    for p in range(2):
        for m_ in range(4):
            for half in range(2):
                h = 4 * (2 * p + half) + m_
                cols.extend(range(h * 64, (h + 1) * 64))
    cols.extend(range(1024, 1280))
    cols.extend(range(1536, 2304))
    cols.extend(range(2304, 3072))
    cols.extend(range(1280, 1536))
    cols.extend(range(3072, 3840))
    return np.asarray(cols)


def make_in_maps(inputs, S, ncores):
    f = lambda a: np.ascontiguousarray(np.asarray(a, dtype=np.float32))
    csn, m, ident = host_consts(S)
    shared = {
        "w_in": f(np.asarray(inputs["w_in"])[0][:, win_perm()]),
        "w_gate": f(inputs["w_gate"][0]),
        "bg": f(np.asarray(inputs["b_gate"])[0].reshape(16, 128).T),
        "a_sink": f(np.asarray(inputs["a_sink"])[0].reshape(4, 2, 2).transpose(2, 0, 1)[:, None].repeat(64, axis=1).reshape(128, 8)),
        "w_br_a": f(inputs["w_br_a"][0]),
        "w_br_b": f(inputs["w_br_b"][0]),
        "w_out": f(inputs["w_out"][0]),
        "ln1_g": f(inputs["ln1_g"][0]),
        "ln1_b": f(inputs["ln1_b"][0]),
        "w_ff_gate": f(inputs["w_ff_gate"][0]),
        "w_ff_up": f(inputs["w_ff_up"][0]),
        "w_ff_down": f(inputs["w_ff_down"][0]),
        "ln2_g": f(inputs["ln2_g"][0]),
        "ln2_b": f(inputs["ln2_b"][0]),
        "csn": csn, "masks": m, "ident": ident,
    }
    x = np.asarray(inputs["x"])
    maps = []
    for c in range(ncores):
        d = dict(shared)
        d["x"] = f(x[c, :S])
        maps.append(d)
    return maps


_NC_CACHE = {}


def kernel(**inputs):
    S = 8192
    n = 8
    if S not in _NC_CACHE:
        _NC_CACHE[S] = build(S)
    nc = _NC_CACHE[S]
    in_maps = make_in_maps(inputs, S, n)
    res = run_bass_kernel_spmd(nc, in_maps, core_ids=list(range(n)))
    out = np.stack([np.asarray(r["out"], dtype=np.float32).reshape(S, D) for r in res.results], axis=0)
    return out
```

```python
import contextlib
import numpy as np
import concourse.bass as bass
import concourse.mybir as mybir
from concourse.bass_utils import run_bass_kernel_spmd

F32 = mybir.dt.float32
BF16 = mybir.dt.bfloat16
AF = mybir.ActivationFunctionType
ALU = mybir.AluOpType

D = 1024
HD = 64
IN_W = 3840
ROW = 4864
C_QA, C_KA, C_QB, C_KB, C_VA, C_VB = 0, 1024, 1280, 2048, 2816, 3328
DFF = 2816
NFC = DFF // 128
ALPHA = 2.0 ** 0.25
EPS = 1e-5
SB = 2048
ENG = ("pe", "act", "dve", "pool", "sp")
import os as _os0
STQ = _os0.environ.get("STQ", "pool")


class Tk:
    __slots__ = ("w", "r", "sem", "name")

    def __init__(self, name="", sem=None):
        self.w = None
        self.r = {}
        self.sem = sem
        self.name = name


class Sched:
    def __init__(self, nc):
        self.nc = nc
        self.ops = {e: [] for e in ENG}
        self.cnt = {e: 0 for e in ENG}
        self.waited = {e: {} for e in ENG}
        self.dma_tot = {}
        self.sems = {}
        for e in ("pe", "act", "dve", "pool"):
            self.sems[e] = nc.alloc_semaphore(name="sem_" + e)
        self.ndma = 0

    def dma_sem(self):
        k = "dma%d" % self.ndma
        self.ndma += 1
        self.sems[k] = self.nc.alloc_semaphore(name=k)
        self.dma_tot[k] = 0
        return k

    def _deps(self, eng, reads, writes, pe_accum, skip_same_w=False):
        deps = {}

        def add(ev):
            if ev is None:
                return
            s, v = ev
            if deps.get(s, 0) < v:
                deps[s] = v

        for t in reads:
            add(t.w)
        for t in writes:
            if not ((pe_accum and t.w is not None and t.w[0] == "pe") or
                    (skip_same_w and t.w is not None and t.w[0] == eng)):
                add(t.w)
            for s, v in t.r.items():
                add((s, v))
        waits = []
        wd = self.waited[eng]
        for s, v in deps.items():
            if s == eng and eng in ("pe", "sp"):
                continue
            if wd.get(s, 0) < v:
                wd[s] = v
                waits.append((s, v))
        return waits

    def _mark(self, ev, reads, writes):
        s, v = ev
        for t in reads:
            if t.r.get(s, 0) < v:
                t.r[s] = v
        for t in writes:
            t.w = ev
            t.r = {}

    def op(self, eng, fns, reads=(), writes=(), pe_accum=False, ssw=False):
        if not isinstance(fns, (list, tuple)):
            fns = [fns]
        waits = self._deps(eng, reads, writes, pe_accum, ssw)
        self.cnt[eng] += 1
        ev = (eng, self.cnt[eng])
        self.ops[eng].append((waits, list(fns), (eng, 1)))
        self._mark(ev, reads, writes)
        return ev

    def dma(self, q, out, in_, tile, reads=(), writes=()):
        if tile.sem is None:
            tile.sem = self.dma_sem()
        waits = self._deps(q, reads, writes, False)
        self.dma_tot[tile.sem] += 16
        ev = (tile.sem, self.dma_tot[tile.sem])
        self.ops[q].append((waits, [lambda e, o=out, i=in_: e.dma_start(out=o, in_=i)], (tile.sem, 16)))
        self._mark(ev, reads, writes)
        return ev

    def barrier(self):
        for e in ENG:
            waits = []
            wd = self.waited[e]
            for f in ("pe", "act", "dve", "pool"):
                if f != e and self.cnt[f] > wd.get(f, 0):
                    wd[f] = self.cnt[f]
                    waits.append((f, self.cnt[f]))
            for k, v in self.dma_tot.items():
                if v > wd.get(k, 0):
                    wd[k] = v
                    waits.append((k, v))
            self.ops[e].append((waits, [], None))

    def emit(self):
        nc = self.nc
        with nc.Block() as block:
            def run(name, eng):
                for waits, fns, inc in self.ops[name]:
                    for s, v in waits:
                        eng.wait_ge(self.sems[s], v)
                    ins = None
                    for f in fns:
                        ins = f(eng)
                    if inc is not None and ins is not None:
                        ins.then_inc(self.sems[inc[0]], inc[1])

            @block.tensor
            def _(e):
                run("pe", e)

            @block.scalar
            def _(e):
                run("act", e)

            @block.vector
            def _(e):
                run("dve", e)

            @block.gpsimd
            def _(e):
                run("pool", e)

            @block.sync
            def _(e):
                run("sp", e)


def sl(start, n, step):
    return slice(start, start + step * (n - 1) + 1, step)


class Ring:
    def __init__(self, items):
        self.items = items
        self.i = 0

    def next(self):
        it = self.items[self.i % len(self.items)]
        self.i += 1
        return it


def build(S=8192, dbg=False, phases=(1, 2, 3)):
    nc = bass.Bass("TRN2", target_bir_lowering=False)
    sc = Sched(nc)
    NB = S // 128

    def din(name, shape, dt=F32):
        return nc.dram_tensor(name, list(shape), dt, kind="ExternalInput").ap()

    x_d = din("x", [S, D])
    win_d = din("w_in", [D, IN_W])
    wgate_d = din("w_gate", [D, 2048])
    bg_d = din("bg", [128, 16])
    sink_d = din("a_sink", [128, 8])
    wbra_d = din("w_br_a", [1024, D])
    wbrb_d = din("w_br_b", [256, D])
    wout_d = din("w_out", [D, D])
    ln1g_d = din("ln1_g", [D])
    ln1b_d = din("ln1_b", [D])
    wffg_d = din("w_ff_gate", [D, DFF])
    wffu_d = din("w_ff_up", [D, DFF])
    wffd_d = din("w_ff_down", [DFF, D])
    ln2g_d = din("ln2_g", [D])
    ln2b_d = din("ln2_b", [D])
    csn_d = din("csn", [S, 32])
    masks_d = din("masks", [128, 4, 128])
    ident_d = din("ident", [128, 128])

    skind = "ExternalOutput" if dbg else "Internal"
    qkv_d = nc.dram_tensor("qkv_s", [S, ROW], BF16, kind=skind).ap()
    gT_d = nc.dram_tensor("gT_s", [16, 128, S], BF16, kind=skind).ap()
    h1_d = nc.dram_tensor("h1_s", [S, D], F32, kind=skind).ap()
    out_d = nc.dram_tensor("out", [S, D], F32, kind="ExternalOutput").ap()

    psb = [nc.alloc_psum_tensor("ps%d" % i, [128, 512], F32) for i in range(6)]
    ptb = [nc.alloc_psum_tensor("pt%d" % i, [128, 1024], BF16) for i in range(2)]
    ps_ring = Ring([(psb[i], Tk("ps%d" % i)) for i in range(6)])
    pt_ring = Ring([(ptb[i], Tk("pt%d" % i)) for i in range(2)])

    uid = [0]

    def sb(st, name, shape, dt):
        uid[0] += 1
        return st.enter_context(nc.sbuf_tensor("sb%d_%s" % (uid[0], name), list(shape), dt))

    rr = {"cast": 0}

    def cast_any(out, in_, reads, writes, engs=("dve", "act")):
        e = engs[rr["cast"] % len(engs)]
        rr["cast"] += 1
        if e == "act":
            sc.op("act", lambda g: g.activation(out=out, in_=in_, func=AF.Copy), reads, writes)
        else:
            sc.op(e, lambda g: g.tensor_copy(out=out, in_=in_), reads, writes)

    def load_weight(st, name, src, KC, N, wst):
        w = sb(st, name, [128, KC, N], BF16)
        tk = Tk(name)
        for kc in range(KC):
            for n0 in range(0, N, 1024):
                n1 = min(N, n0 + 1024)
                stg, stk = wst.next()
                sc.dma("sp", stg[:, 0:n1 - n0], src[kc * 128:(kc + 1) * 128, n0:n1], stk, writes=[stk])
                cast_any(w[:, kc, n0:n1], stg[:, 0:n1 - n0], [stk], [tk])
        return w, tk

    def consts(st, wst):
        ident = sb(st, "ident", [128, 128], BF16)
        masks = sb(st, "masks", [128, 4, 128], BF16)
        ones = sb(st, "ones", [128, 128], BF16)
        ctk = Tk("consts")
        stg, stk = wst.next()
        sc.dma("sp", stg[:, 0:128], ident_d, stk, writes=[stk])
        sc.op("dve", lambda g: g.tensor_copy(out=ident[:], in_=stg[:, 0:128]), [stk], [ctk])
        stg2, stk2 = wst.next()
        sc.dma("sp", stg2[:, 0:512], masks_d.rearrange("p a b -> p (a b)"), stk2, writes=[stk2])
        sc.op("dve", lambda g: g.tensor_copy(out=masks[:].rearrange("p a b -> p (a b)"), in_=stg2[:, 0:512]),
              [stk2], [ctk])
        sc.op("dve", lambda g: g.memset(ones[:], 1.0), [], [ctk])
        return ident, masks, ones, ctk

    def layer_norm_a(pre, ptk, small, smtk):
        stats = small[:, 0:12]
        mv = small[:, 12:14]
        rstd = small[:, 14:15]
        sc.op("dve", lambda g: g.bn_stats(out=small[:, 0:6], in_=pre[:, 0:512]), [ptk], [smtk])
        sc.op("dve", lambda g: g.bn_stats(out=small[:, 6:12], in_=pre[:, 512:1024]), [ptk], [smtk], ssw=True)
        sc.op("dve", lambda g: g.bn_aggr(out=mv, in_=stats), [smtk], [smtk])
        sc.op("dve", lambda g: g.tensor_scalar_add(out=rstd, in0=small[:, 13:14], scalar1=EPS), [smtk], [smtk])

    def layer_norm_b(pre, ptk, g_t, b_t, outt, otk, small, smtk, small2, sm2tk):
        rstd = small[:, 14:15]
        sc.op("dve", lambda g: g.scalar_tensor_tensor(out=pre, in0=pre, scalar=small[:, 12:13], in1=g_t,
                                                      op0=ALU.subtract, op1=ALU.mult), [ptk, smtk], [ptk])
        sc.op("act", lambda g: g.activation(out=small2[:, 0:1], in_=rstd, func=AF.Ln), [smtk], [sm2tk])
        sc.op("act", lambda g: g.activation(out=small2[:, 0:1], in_=small2[:, 0:1], func=AF.Exp, scale=-0.5),
              [sm2tk], [sm2tk])
        sc.op("dve", lambda g: g.scalar_tensor_tensor(out=outt, in0=pre, scalar=small2[:, 0:1], in1=b_t,
                                                      op0=ALU.mult, op1=ALU.add), [ptk, sm2tk], [otk])

    def layer_norm(pre, ptk, g_t, b_t, outt, otk, small, smtk, small2, sm2tk):
        layer_norm_a(pre, ptk, small, smtk)
        layer_norm_b(pre, ptk, g_t, b_t, outt, otk, small, smtk, small2, sm2tk)

    def bcast_load(st, name, src, wst_unused=None):
        t = sb(st, name, [128, D], F32)
        tk = Tk(name)
        sc.dma("sp", t[:], src.partition_broadcast(128), tk, writes=[tk])
        return t, tk

    def phase1(st):
      if True:
        wst = Ring([(sb(st, "wst%d" % i, [128, 1024], F32), Tk("wst%d" % i)) for i in range(3)])
        xin = Ring([(sb(st, "xin%d" % i, [128, D], F32), Tk("xin%d" % i)) for i in range(3)])
        wst = Ring(wst.items + [(t_, Tk("stg")) for t_, _ in xin.items])
        ident, masks, ones, ctk = consts(st, wst)
        w_bf, wtk = load_weight(st, "w_in_bf", win_d, 8, IN_W, wst)
        wg_bf, wgtk = load_weight(st, "w_g_bf", wgate_d, 8, 2048, wst)
        bg = sb(st, "bg", [128, 16], F32)
        bgtk = Tk("bg")
        sc.dma("sp", bg[:], bg_d, bgtk, writes=[bgtk])
        sc.barrier()

        xbf = Ring([(sb(st, "xbf%d" % i, [128, D], BF16), Tk("xbf%d" % i)) for i in range(2)])
        xT = [(sb(st, "xT%d" % i, [128, 8, 512], BF16), [Tk("xT%d_%d" % (i, b)) for b in range(4)]) for i in range(2)]
        cs = [(sb(st, "cs%d" % i, [128, 4, 32], F32), Tk("cs%d" % i)) for i in range(2)]
        stq = Ring([(sb(st, "stq%d" % i, [128, ROW], BF16), [Tk("stq%d_%d" % (i, n)) for n in range(8)])
                    for i in range(2)])
        rot = Ring([(sb(st, "rot%d" % i, [128, 44, 16], F32), Tk("rot%d" % i)) for i in range(2)])
        gst = Ring([(sb(st, "gst%d" % i, [128, 4, 512], BF16), Tk("gst%d" % i)) for i in range(2)])
        rA = Ring([(sb(st, "rA%d" % i, [128, 44, 16], F32), Tk("rA%d" % i)) for i in range(2)])
        rB = Ring([(sb(st, "rB%d" % i, [128, 44, 16], F32), Tk("rB%d" % i)) for i in range(2)])
        NT = S // 512

        def prep(t):
            xt, xtk = xT[t % 2]
            c_t, c_tk = cs[t % 2]
            sc.dma("sp", c_t[:], csn_d[t * 512:(t + 1) * 512, :].rearrange("(b p) c -> p b c", p=128), c_tk,
                   writes=[c_tk])
            for b in range(4):
                xi, xitk = xin.next()
                sc.dma("sp", xi[:], x_d[t * 512 + b * 128: t * 512 + (b + 1) * 128, :], xitk, writes=[xitk])
                xb, xbtk = xbf.next()
                sc.op("dve", lambda g, o=xb, i=xi: g.tensor_copy(out=o[:], in_=i[:]), [xitk], [xbtk])
                pt, pttk = pt_ring.next()
                sc.op("pe", [lambda g, o=pt, i=xb, kc=kc: g.transpose(out=o[:, kc * 128:(kc + 1) * 128],
                                                                       in_=i[:, kc * 128:(kc + 1) * 128],
                                                                       identity=ident[:])
                             for kc in range(8)], [xbtk], [pttk])
                sc.op("act", lambda g, o=xt, i=pt, b=b: g.activation(
                    out=o[:, :, b * 128:(b + 1) * 128], in_=i[:].rearrange("p (k t) -> p k t", k=8), func=AF.Copy),
                    [pttk], [xtk[b]])

        def v_chunk(ps, pstk, pcol0, nh, sq, sqtk, col0):
            pv = ps[:, pcol0:pcol0 + nh * 64].rearrange("p (h d) -> p h d", d=64).unsqueeze(2).broadcast_to(
                [128, nh, 2, 64])
            ov = sq[:, col0:col0 + nh * 128].rearrange("p (h r d) -> p h r d", r=2, d=64)
            sc.op("act", lambda g: g.activation(out=ov, in_=pv, func=AF.Copy), [pstk], [sqtk])

        def compute(t):
            xt, xtk = xT[t % 2]
            c_t, c_tk = cs[t % 2]
            for b in range(4):
                sq, sqtks = stq.next()
                rt_, rtk = rot.next()
                for n in range(8):
                    wdt = 512 if n < 7 else 256
                    ps, pstk = ps_ring.next()
                    sc.op("pe", [lambda g, o=ps, kc=kc, n=n, wdt=wdt, b=b: g.matmul(
                        o[:, 0:wdt], lhsT=xt[:, kc, b * 128:(b + 1) * 128], rhs=w_bf[:, kc, n * 512:n * 512 + wdt],
                        start=(kc == 0), stop=(kc == 7)) for kc in range(8)], [xtk[b]], [pstk])
                    nh = 8 if n <= 4 else (4 if n == 5 else 0)
                    if nh:
                        sc.op("act", lambda g, ps=ps, sq=sq, n=n, nh=nh: g.activation(
                            out=sq[:, n * 512:n * 512 + nh * 64], in_=ps[:, 0:nh * 64], func=AF.Copy),
                            [pstk], [sqtks[n]])
                        sc.op("act", lambda g, ps=ps, rt_=rt_, n=n, nh=nh: g.activation(
                            out=rt_[:, 8 * n:8 * n + nh, :],
                            in_=ps[:, 0:nh * 64].rearrange("p (h d) -> p h d", d=64)[:, :, 0:16], func=AF.Copy),
                            [pstk], [rtk])
                    if n == 5:
                        v_chunk(ps, pstk, 256, 4, sq, sqtks[n], C_VA)
                    elif n == 6:
                        v_chunk(ps, pstk, 0, 8, sq, sqtks[n], C_VB)
                    elif n == 7:
                        v_chunk(ps, pstk, 0, 4, sq, sqtks[n], C_VB + 1024)
                a_t, atk = rA.next()
                b_t, btk = rB.next()
                cc = c_t[:, b, 0:16].unsqueeze(1).broadcast_to([128, 44, 16])
                ns = c_t[:, b, 16:24].unsqueeze(1).broadcast_to([128, 44, 8])
                ps_ = c_t[:, b, 24:32].unsqueeze(1).broadcast_to([128, 44, 8])
                sc.op("dve", lambda g, a_t=a_t, rt_=rt_, cc=cc: g.tensor_tensor(
                    out=a_t[:], in0=rt_[:], in1=cc, op=ALU.mult), [rtk, c_tk], [atk])
                sc.op("pool", lambda g, b_t=b_t, rt_=rt_, ns=ns: g.tensor_tensor(
                    out=b_t[:, :, 0:8], in0=rt_[:, :, 8:16], in1=ns, op=ALU.mult), [rtk, c_tk], [btk])
                sc.op("pool", lambda g, b_t=b_t, rt_=rt_, ps_=ps_: g.tensor_tensor(
                    out=b_t[:, :, 8:16], in0=rt_[:, :, 0:8], in1=ps_, op=ALU.mult), [rtk, c_tk], [btk])
                ov = sq[:, 0:2816].rearrange("p (h d) -> p h d", d=64)[:, :, 0:16]
                sc.op("dve", lambda g, ov=ov, a_t=a_t, b_t=b_t: g.tensor_tensor(
                    out=ov, in0=a_t[:], in1=b_t[:], op=ALU.add), [atk, btk], sqtks[0:6])
                r0 = t * 512 + b * 128
                sc.dma(STQ, qkv_d[r0:r0 + 128, :], sq[:], sqtks[0], reads=sqtks)
            for gc in range(16):
                ps, pstk = ps_ring.next()
                sc.op("pe", [lambda g, o=ps, kc=kc, gc=gc: g.matmul(
                    o[:, 0:512], lhsT=wg_bf[:, kc, gc * 128:(gc + 1) * 128], rhs=xt[:, kc, :],
                    start=(kc == 0), stop=(kc == 7)) for kc in range(8)], xtk, [pstk])
                if gc % 4 == 0:
                    gs, gstk = gst.next()
                sc.op("act", lambda g, o=gs, i=ps, gc=gc: g.activation(
                    out=o[:, gc % 4, :], in_=i[:, 0:512], func=AF.Sigmoid, bias=bg[:, gc:gc + 1]), [pstk, bgtk], [gstk])
                if gc % 4 == 3:
                    c0 = gc - 3
                    sc.dma(STQ, gT_d[c0:c0 + 4, :, t * 512:(t + 1) * 512].rearrange("c p t -> p c t"), gs[:],
                           gstk, reads=[gstk])

        import os
        cut = int(os.environ.get("P1CUT", "9"))
        if cut >= 1:
            prep(0)
        for t in range(NT if cut >= 2 else 0):
            if t + 1 < NT:
                prep(t + 1)
            compute(t)
        sc.barrier()

    if 1 in phases:
        with contextlib.ExitStack() as st:
            phase1(st)

    def phase2(st):
      if True:
        wst = Ring([(sb(st, "wst%d" % i, [128, 1024], F32), Tk("wst%d" % i)) for i in range(2)])
        xr = Ring([(sb(st, "xr%d" % i, [128, D], F32), Tk("xr%d" % i)) for i in range(2)])
        pre = Ring([(sb(st, "pre%d" % i, [128, D], F32), Tk("pre%d" % i)) for i in range(2)])
        hst = Ring([(sb(st, "hst%d" % i, [128, D], F32), Tk("hst%d" % i)) for i in range(2)])
        wst = Ring(wst.items + [(t_, Tk("stg")) for t_, _ in xr.items + pre.items + hst.items])
        ident, masks, ones, ctk = consts(st, wst)
        wbra, _ = load_weight(st, "wbra", wbra_d, 8, D, wst)
        wbrb, _ = load_weight(st, "wbrb", wbrb_d, 2, D, wst)
        wout, _ = load_weight(st, "wout", wout_d, 8, D, wst)
        ln_g, lgtk = bcast_load(st, "ln1g", ln1g_d)
        ln_b, lbtk = bcast_load(st, "ln1b", ln1b_d)
        es = sb(st, "es", [128, 8], F32)
        estk = Tk("es")
        sc.dma("sp", es[:], sink_d, estk, writes=[estk])
        sc.op("act", lambda g: g.activation(out=es[:], in_=es[:], func=AF.Exp), [estk], [estk])
        mp = sb(st, "mp", [128, 4, 2, 128], BF16)
        for f in range(2):
            for l in range(2):
                i = f * 2 + l
                sc.op("dve", lambda g, i=i, f=f: g.tensor_copy(out=mp[:, i, 0, :], in_=masks[:, 2 if f else 0, :]),
                      [ctk], [ctk])
                sc.op("dve", lambda g, i=i, l=l: g.tensor_copy(out=mp[:, i, 1, :], in_=masks[:, 3 if l else 1, :]),
                      [ctk], [ctk])
        accU = sb(st, "accU", [128, 2, SB], F32)
        accL = sb(st, "accL", [128, 2, SB], F32)
        acctk = Tk("acc")
        obT = sb(st, "obT", [128, 2, SB], BF16)
        obtk = Tk("obT")
        NBS = 5
        qb = [(sb(st, "qb%d" % i, [128, 256], BF16), Tk("qb%d" % i)) for i in range(NBS)]
        kbt = [(sb(st, "kb%d" % i, [128, 2, 256], BF16), Tk("kb%d" % i)) for i in range(NBS)]
        vbt = [(sb(st, "vb%d" % i, [128, 2, 512], BF16), Tk("vb%d" % i)) for i in range(NBS)]
        for i in range(NBS):
            sc.op("pool", lambda g, i=i: g.memset(qb[i][0][:], 0.0), [], [qb[i][1]])
            sc.op("pool", lambda g, i=i: g.memset(kbt[i][0][:], 0.0), [], [kbt[i][1]])
            sc.op("pool", lambda g, i=i: g.memset(vbt[i][0][:], 0.0), [], [vbt[i][1]])
        qkT = Ring([(sb(st, "qkT%d" % i, [128, 6, 128], BF16), Tk("qkT%d" % i)) for i in range(3)])
        pTb = Ring([(sb(st, "pTb%d" % i, [128, 2, 2, 2, 128], BF16), Tk("pTb%d" % i)) for i in range(2)])
        qa = Ring([(sb(st, "qa%d" % i, [128, 1024], BF16), Tk("qa%d" % i)) for i in range(2)])
        ka = [(sb(st, "ka%d" % i, [128, 256], BF16), Tk("ka%d" % i)) for i in range(4)]
        va = [(sb(st, "va%d" % i, [128, 512], BF16), Tk("va%d" % i)) for i in range(4)]
        kaT = [(sb(st, "kaT%d" % i, [128, 2, 128], BF16), Tk("kaT%d" % i)) for i in range(4)]
        qaT = Ring([(sb(st, "qaT%d" % i, [128, 8, 128], BF16), Tk("qaT%d" % i)) for i in range(2)])
        pA = Ring([(sb(st, "pA%d" % i, [128, 512], BF16), Tk("pA%d" % i)) for i in range(6)])
        rtmp = Ring([(sb(st, "rtmp%d" % i, [128, 2, 128], F32), Tk("rtmp%d" % i)) for i in range(3)])
        T2 = 256
        oaT = Ring([(sb(st, "oaT%d" % i, [128, 8, T2], BF16), Tk("oaT%d" % i)) for i in range(2)])
        gt = Ring([(sb(st, "gt%d" % i, [128, 16, T2], BF16), Tk("gt%d" % i)) for i in range(2)])
        t1r = Ring([(sb(st, "t1_%d" % i, [128, T2], F32), Tk("t1_%d" % i)) for i in range(2)])
        t2r = Ring([(sb(st, "t2_%d" % i, [128, T2], F32), Tk("t2_%d" % i)) for i in range(2)])
        mixT = Ring([(sb(st, "mixT%d" % i, [128, 8, T2], BF16), Tk("mixT%d" % i)) for i in range(2)])
        small = Ring([(sb(st, "sm%d" % i, [128, 16], F32), Tk("sm%d" % i)) for i in range(2)])
        small2 = Ring([(sb(st, "smb%d" % i, [128, 2], F32), Tk("smb%d" % i)) for i in range(2)])
        sc.barrier()

        s_ring = Ring([ps_ring.items[i] for i in range(4)])
        ol_ring = Ring([ps_ring.items[i] for i in (4, 5)])
        bcount = [0]

        def b_T(u):
            sbi, g, d, r, J, j0 = u["args"]
            slot = bcount[0] % NBS
            bcount[0] += 1
            q_t, qtk = qb[slot]
            k_t, ktk = kbt[slot]
            v_t, vtk = vbt[slot]
            cq = C_QB + g * 256
            ck = C_KB + g * 256
            cv = C_VB + g * 512
            t0 = r + d * j0
            sc.dma("sp", q_t[:], qkv_d[sl(t0, 128, d), cq:cq + 256], qtk, writes=[qtk])
            first = last = 0
            for kb in range(2):
                jk0 = j0 - 64 + kb * 128
                lo = max(0, -jk0)
                hi = min(128, J - jk0)
                if kb == 0 and lo > 0:
                    first = 1
                if kb == 1 and hi < 128:
                    last = 1
                ts = r + d * (jk0 + lo)
                sc.dma("sp", k_t[lo:hi, kb, :], qkv_d[sl(ts, hi - lo, d), ck:ck + 256], ktk, writes=[ktk])
                sc.dma("sp", v_t[lo:hi, kb, :], qkv_d[sl(ts, hi - lo, d), cv:cv + 512], vtk, writes=[vtk])
            pt, pttk = pt_ring.next()
            fns = []
            for c in range(2):
                fns.append(lambda e, c=c: e.transpose(out=pt[:, c * 128:(c + 1) * 128],
                                                      in_=q_t[:, c * 128:(c + 1) * 128], identity=ident[:]))
            for kb in range(2):
                for c in range(2):
                    i = 2 + kb * 2 + c
                    fns.append(lambda e, c=c, kb=kb, i=i: e.transpose(out=pt[:, i * 128:(i + 1) * 128],
                                                                      in_=k_t[:, kb, c * 128:(c + 1) * 128],
                                                                      identity=ident[:]))
            sc.op("pe", fns, [qtk, ktk], [pttk])
            qk, qktk = qkT.next()
            sc.op("act", lambda e: e.activation(out=qk[:].rearrange("p a b -> p (a b)"), in_=pt[:, 0:768],
                                                func=AF.Copy), [pttk], [qktk])
            u.update(qk=qk, qktk=qktk, v_t=v_t, vtk=vtk, first=first, last=last, t0=t0)

        def b_S(u):
            qk, qktk = u["qk"], u["qktk"]
            sbank = [s_ring.next(), s_ring.next()]
            for half in range(2):
                ps, pstk = sbank[half]
                fns = []
                for c in range(2):
                    for kb in range(2):
                        fns.append(lambda e, c=c, kb=kb, half=half, ps=ps: e.matmul(
                            ps[:, (c * 2 + kb) * 128:(c * 2 + kb + 1) * 128],
                            lhsT=qk[half * 64:(half + 1) * 64, 2 + kb * 2 + c, :],
                            rhs=qk[half * 64:(half + 1) * 64, c, :], start=True, stop=True))
                sc.op("pe", fns, [qktk], [pstk])
            p_t, ptk_ = pTb.next()
            for half in range(2):
                ps, pstk = sbank[half]
                sc.op("act", lambda e, half=half, ps=ps: e.activation(
                    out=p_t[:, half].rearrange("p c k q -> p (c k q)"), in_=ps[:, 0:512], func=AF.Exp, scale=0.125),
                    [pstk], [ptk_], ssw=True)
            pv4 = p_t[:].rearrange("p h c k q -> p (h c) k q")
            mview = mp[:, u["first"] * 2 + u["last"]].unsqueeze(1).broadcast_to([128, 4, 2, 128])
            sc.op("dve", lambda e: e.tensor_tensor(out=pv4, in0=pv4, in1=mview, op=ALU.mult), [ptk_, ctk], [ptk_])
            u.update(p_t=p_t, ptk_=ptk_)

        def b_PV(u):
            sbi, g, d, r, J, j0 = u["args"]
            p_t, ptk_, v_t, vtk = u["p_t"], u["ptk_"], u["v_t"], u["vtk"]
            ol, oltk = ol_ring.next()
            fns = []
            for c in range(2):
                for typ in range(2):
                    for kb in range(2):
                        for half in range(2):
                            h = 2 * c + half
                            hs = slice(half * 64, (half + 1) * 64)
                            col = typ * 256 + c * 128
                            if typ == 0:
                                fns.append(lambda e, hs=hs, col=col, h=h, half=half, c=c, kb=kb: e.matmul(
                                    ol[hs, col:col + 128], lhsT=v_t[:, kb, h * 128:h * 128 + 64],
                                    rhs=p_t[:, half, c, kb, :], start=(kb == 0), stop=(kb == 1)))
                            else:
                                fns.append(lambda e, hs=hs, col=col, half=half, c=c, kb=kb: e.matmul(
                                    ol[hs, col:col + 128], lhsT=ones[:, 0:64],
                                    rhs=p_t[:, half, c, kb, :], start=(kb == 0), stop=(kb == 1)))
            sc.op("pe", fns, [ptk_, vtk, ctk], [oltk])
            a0 = u["t0"] - sbi * SB
            for typ, acc in ((0, accU), (1, accL)):
                av = acc[:, :, sl(a0, 128, d)]
                pv = ol[:, typ * 256:(typ + 1) * 256].rearrange("p (c q) -> p c q", c=2)
                sc.op("dve", lambda e, av=av, pv=pv: e.tensor_tensor(out=av, in0=av, in1=pv, op=ALU.add),
                      [acctk, oltk], [acctk], ssw=True)

        def mixer_b(sbi):
            sc.op("pool", lambda e: e.memset(accU[:], 0.0), [], [acctk])
            sc.op("pool", lambda e: e.memset(accL[:], 0.0), [], [acctk])
            units = []
            for g, d in enumerate((1, 4, 16)):
                J = S // d
                nj = SB // d // 128
                for r in range(d):
                    for jb in range(nj):
                        units.append({"args": (sbi, g, d, r, J, sbi * (SB // d) + jb * 128)})
            n = len(units)
            b_T(units[0])
            b_T(units[1])
            b_S(units[0])
            for i in range(n):
                if i + 2 < n:
                    b_T(units[i + 2])
                if i + 1 < n:
                    b_S(units[i + 1])
                b_PV(units[i])
            sc.op("act", lambda e: e.activation(out=accL[:], in_=accL[:], func=AF.Ln), [acctk], [acctk])
            sc.op("act", lambda e: e.activation(out=accL[:], in_=accL[:], func=AF.Exp, scale=-1.0), [acctk], [acctk])
            sc.op("dve", lambda e: e.tensor_tensor(out=obT[:], in0=accU[:], in1=accL[:], op=ALU.mult),
                  [acctk], [obtk])

        def a_load_kv(blk):
            k_t, ktk = ka[blk % 4]
            v_t, vtk = va[blk % 4]
            sc.dma("sp", k_t[:], qkv_d[blk * 128:(blk + 1) * 128, C_KA:C_KA + 256], ktk, writes=[ktk])
            sc.dma("sp", v_t[:], qkv_d[blk * 128:(blk + 1) * 128, C_VA:C_VA + 512], vtk, writes=[vtk])
            pt, pttk = pt_ring.next()
            sc.op("pe", [lambda e, c=c: e.transpose(out=pt[:, c * 128:(c + 1) * 128], in_=k_t[:, c * 128:(c + 1) * 128],
                                                    identity=ident[:]) for c in range(2)], [ktk], [pttk])
            kt_t, kttk = kaT[blk % 4]
            sc.op("act", lambda e: e.activation(out=kt_t[:].rearrange("p a b -> p (a b)"), in_=pt[:, 0:256],
                                                func=AF.Copy), [pttk], [kttk])

        blkctx = {}

        def a_prep(i):
            if i == 0:
                a_load_kv(0)
            if i + 1 < NB:
                a_load_kv(i + 1)
            q_t, qtk = qa.next()
            sc.dma("sp", q_t[:], qkv_d[i * 128:(i + 1) * 128, 0:1024], qtk, writes=[qtk])
            pt, pttk = pt_ring.next()
            sc.op("pe", [lambda e, c=c: e.transpose(out=pt[:, c * 128:(c + 1) * 128], in_=q_t[:, c * 128:(c + 1) * 128],
                                                    identity=ident[:]) for c in range(8)], [qtk], [pttk])
            qT, qTtk = qaT.next()
            sc.op("act", lambda e: e.activation(out=qT[:].rearrange("p a b -> p (a b)"), in_=pt[:, 0:1024],
                                                func=AF.Copy), [pttk], [qTtk])
            blkctx[i] = (qT, qTtk)

        def a_S(u):
            i, k = u["i"], u["k"]
            qT, qTtk = blkctx[i]
            kbs = [kb for kb in range(3) if 0 <= i - 1 + kb < NB]
            p, half = k // 2, k % 2
            hs = slice(half * 64, (half + 1) * 64)
            plist = []
            for kb in kbs:
                blk = i - 1 + kb
                ps, pstk = s_ring.next()
                kt_t, kttk = kaT[blk % 4]
                sc.op("pe", lambda e, ps=ps, kt_t=kt_t: e.matmul(
                    ps[:, 0:512], lhsT=kt_t[hs, p, :], rhs=qT[hs, p * 4:(p + 1) * 4, :], start=True, stop=True),
                    [kttk, qTtk], [pstk])
                pp, pptk = pA.next()
                sc.op("act", lambda e, ps=ps, pp=pp: e.activation(out=pp[:], in_=ps[:, 0:512], func=AF.Exp,
                                                                  scale=0.125), [pstk], [pptk])
                if kb != 1:
                    mv = masks[:, 0 if kb == 0 else 1, :].unsqueeze(1).broadcast_to([128, 4, 128])
                    ppv = pp[:].rearrange("p (m q) -> p m q", m=4)
                    sc.op("dve",
                          lambda e, ppv=ppv, mv=mv: e.tensor_tensor(out=ppv, in0=ppv, in1=mv, op=ALU.mult),
                          [pptk, ctk], [pptk])
                plist.append((pp, pptk, blk))
            u["plist"] = plist

        def a_PV(u):
            i, k, oa, oatk, bi = u["i"], u["k"], u["oa"], u["oatk"], u["bi"]
            plist = u["plist"]
            ol, oltk = ol_ring.next()
            n = len(plist)
            fns = []
            for typ in range(2):
                for j, (pp, pptk, blk) in enumerate(plist):
                    ppv = pp[:].rearrange("p (m q) -> p m q", m=4)
                    for hf in range(2):
                        hs = slice(hf * 64, (hf + 1) * 64)
                        if typ == 0:
                            fns.append(lambda e, j=j, ppv=ppv, blk=blk, hf=hf, hs=hs: e.matmul(
                                ol[hs, 0:256], lhsT=va[blk % 4][0][:, k * 128:k * 128 + 64], rhs=ppv[:, hf::2, :],
                                start=(j == 0), stop=(j == n - 1)))
                        else:
                            fns.append(lambda e, j=j, ppv=ppv, hf=hf, hs=hs: e.matmul(
                                ol[hs, 256:512], lhsT=ones[:, 0:64], rhs=ppv[:, hf::2, :],
                                start=(j == 0), stop=(j == n - 1)))
            sc.op("pe", fns, [x[1] for x in plist] + [va[x[2] % 4][1] for x in plist] + [ctk], [oltk])
            rt, rttk = rtmp.next()
            esv = es[:, 2 * k:2 * k + 2].unsqueeze(2).broadcast_to([128, 2, 128])
            lv = ol[:, 256:512].rearrange("p (m q) -> p m q", m=2)
            ov = ol[:, 0:256].rearrange("p (m q) -> p m q", m=2)
            sc.op("dve", lambda e: e.tensor_tensor(out=rt[:], in0=lv, in1=esv, op=ALU.add), [oltk, estk], [rttk])
            sc.op("act", lambda e: e.activation(out=rt[:], in_=rt[:], func=AF.Ln), [rttk], [rttk])
            sc.op("act", lambda e: e.activation(out=rt[:], in_=rt[:], func=AF.Exp, scale=-1.0), [rttk], [rttk])
            sc.op("dve", lambda e: e.tensor_tensor(
                out=oa[:, 2 * k:2 * k + 2, bi * 128:(bi + 1) * 128], in0=ov, in1=rt[:], op=ALU.mult),
                [oltk, rttk], [oatk], ssw=True)

        def dense2(tt, oa, oatk, sbi):
            tok0 = tt * T2
            g_t, gtk = gt.next()
            sc.dma("sp", g_t[:], gT_d[:, :, tok0:tok0 + T2].rearrange("c p t -> p c t"), gtk, writes=[gtk])
            mx, mxtk = mixT.next()
            so = tok0 - sbi * SB
            for c in range(8):
                (pa, patk), (pb, pbtk) = s_ring.next(), s_ring.next()
                sc.op("pe", [lambda e, kc=kc, c=c, pa=pa: e.matmul(
                    pa[:, 0:T2], lhsT=wbra[:, kc, c * 128:(c + 1) * 128], rhs=oa[:, kc, :],
                    start=(kc == 0), stop=(kc == 7)) for kc in range(8)], [oatk], [patk])
                sc.op("pe", [lambda e, kc=kc, c=c, pb=pb: e.matmul(
                    pb[:, 0:T2], lhsT=wbrb[:, kc, c * 128:(c + 1) * 128], rhs=obT[:, kc, so:so + T2],
                    start=(kc == 0), stop=(kc == 1)) for kc in range(2)], [obtk], [pbtk])
                t1, t1tk = t1r.next()
                t2, t2tk = t2r.next()
                sc.op("dve", lambda e, t1=t1, pa=pa, c=c: e.tensor_tensor(out=t1[:], in0=pa[:, 0:T2],
                                                                          in1=g_t[:, c, :], op=ALU.mult),
                      [patk, gtk], [t1tk])
                sc.op("dve", lambda e, t2=t2, pb=pb, c=c: e.tensor_tensor(out=t2[:], in0=pb[:, 0:T2],
                                                                          in1=g_t[:, 8 + c, :], op=ALU.mult),
                      [pbtk, gtk], [t2tk])
                sc.op("pool", lambda e, t1=t1, t2=t2, c=c: e.tensor_tensor(out=mx[:, c, :], in0=t1[:], in1=t2[:],
                                                                            op=ALU.add), [t1tk, t2tk], [mxtk])
            blks = []
            for bi in range(T2 // 128):
                r0 = tok0 + bi * 128
                x_t, xtk_ = xr.next()
                sc.dma("sp", x_t[:], x_d[r0:r0 + 128, :], xtk_, writes=[xtk_])
                pr, prtk = pre.next()
                for n in range(2):
                    ps, pstk = s_ring.next()
                    sc.op("pe", [lambda e, kc=kc, n=n, ps=ps, bi=bi: e.matmul(
                        ps[:, 0:512], lhsT=mx[:, kc, bi * 128:(bi + 1) * 128], rhs=wout[:, kc, n * 512:(n + 1) * 512],
                        start=(kc == 0), stop=(kc == 7)) for kc in range(8)], [mxtk], [pstk])
                    sc.op("dve", lambda e, n=n, ps=ps, pr=pr, x_t=x_t: e.scalar_tensor_tensor(
                        out=pr[:, n * 512:(n + 1) * 512], in0=x_t[:, n * 512:(n + 1) * 512], scalar=ALPHA,
                        in1=ps[:, 0:512], op0=ALU.mult, op1=ALU.add), [pstk, xtk_], [prtk], ssw=True)
                blks.append((r0, pr, prtk))
            for r0, pr, prtk in blks:
                sm, smtk = small.next()
                layer_norm_a(pr[:], prtk, sm, smtk)
                ln_pending.append((r0, pr, prtk, sm, smtk))

        ln_pending = []

        def flush_ln():
            while ln_pending:
                r0, pr, prtk, sm, smtk = ln_pending.pop(0)
                hs_, hstk = hst.next()
                sm2, sm2tk = small2.next()
                layer_norm_b(pr[:], prtk, ln_g[:], ln_b[:], hs_[:], hstk, sm, smtk, sm2, sm2tk)
                sc.dma(STQ, h1_d[r0:r0 + 128, :], hs_[:], hstk, reads=[hstk])

        NBT = T2 // 128
        for sbi in range(S // SB):
            mixer_b(sbi)
            units = []
            for tt in range(sbi * (SB // T2), (sbi + 1) * (SB // T2)):
                oa, oatk = oaT.next()
                for bi in range(NBT):
                    for k in range(4):
                        units.append({"i": tt * NBT + bi, "k": k, "bi": bi, "tt": tt, "oa": oa, "oatk": oatk})
            n = len(units)
            a_prep(units[0]["i"])
            a_S(units[0])
            pending = None
            for j in range(n):
                u = units[j]
                if u["k"] == 1 and u["i"] + 1 < (sbi + 1) * (SB // 128):
                    a_prep(u["i"] + 1)
                if j + 1 < n:
                    a_S(units[j + 1])
                a_PV(u)
                if ln_pending and u["k"] == 1:
                    flush_ln()
                if pending is not None:
                    dense2(*pending)
                    pending = None
                if u["k"] == 3 and u["bi"] == NBT - 1:
                    pending = (u["tt"], u["oa"], u["oatk"], sbi)
            if pending is not None:
                dense2(*pending)
            flush_ln()
        sc.barrier()

    if 2 in phases:
        with contextlib.ExitStack() as st:
            phase2(st)

    def phase3(st):
      if True:
        wst = Ring([(sb(st, "wst%d" % i, [128, 1024], F32), Tk("wst%d" % i)) for i in range(2)])
        hc = Ring([(sb(st, "hc%d" % i, [128, D], F32), Tk("hc%d" % i)) for i in range(2)])
        hres = Ring([(sb(st, "hres%d" % i, [128, D], F32), Tk("hres%d" % i)) for i in range(2)])
        pre = Ring([(sb(st, "pre3_%d" % i, [128, D], F32), Tk("pre3_%d" % i)) for i in range(2)])
        ost = Ring([(sb(st, "ost%d" % i, [128, D], F32), Tk("ost%d" % i)) for i in range(2)])
        wst = Ring(wst.items + [(t_, Tk("stg")) for t_, _ in hc.items + hres.items + pre.items + ost.items])
        ident, masks, ones, ctk = consts(st, wst)
        wfg, _ = load_weight(st, "wfg", wffg_d, 8, DFF, wst)
        wfu, _ = load_weight(st, "wfu", wffu_d, 8, DFF, wst)
        wfd, _ = load_weight(st, "wfd", wffd_d, NFC, D, wst)
        ln_g, lgtk = bcast_load(st, "ln2g", ln2g_d)
        ln_b, lbtk = bcast_load(st, "ln2b", ln2b_d)
        T3 = 256
        hbf = Ring([(sb(st, "hbf%d" % i, [128, D], BF16), Tk("hbf%d" % i)) for i in range(2)])
        hT = [(sb(st, "hT%d" % i, [128, 8, T3], BF16), [Tk("hT%d_%d" % (i, b)) for b in range(2)]) for i in range(2)]
        sg = Ring([(sb(st, "sg%d" % i, [128, T3], F32), Tk("sg%d" % i)) for i in range(2)])
        guT = Ring([(sb(st, "guT%d" % i, [128, NFC, T3], BF16), Tk("guT%d" % i)) for i in range(1)])
        small = Ring([(sb(st, "sm3_%d" % i, [128, 16], F32), Tk("sm3_%d" % i)) for i in range(2)])
        small2 = Ring([(sb(st, "smb3_%d" % i, [128, 2], F32), Tk("smb3_%d" % i)) for i in range(2)])
        sc.barrier()
        NT3 = S // T3
        hrs = {}
        gus = {}

        def prep3(t):
            ht, httk = hT[t % 2]
            for b in range(2):
                h_t, htk = hc.next()
                r0 = t * T3 + b * 128
                sc.dma("sp", h_t[:], h1_d[r0:r0 + 128, :], htk, writes=[htk])
                hb, hbtk = hbf.next()
                sc.op("dve", lambda e, hb=hb, h_t=h_t: e.tensor_copy(out=hb[:], in_=h_t[:]), [htk], [hbtk])
                pt, pttk = pt_ring.next()
                sc.op("pe", [lambda e, kc=kc, pt=pt, hb=hb: e.transpose(out=pt[:, kc * 128:(kc + 1) * 128],
                                                                        in_=hb[:, kc * 128:(kc + 1) * 128],
                                                                        identity=ident[:]) for kc in range(8)],
                      [hbtk], [pttk])
                sc.op("act", lambda e, pt=pt, b=b: e.activation(
                    out=ht[:, :, b * 128:(b + 1) * 128], in_=pt[:].rearrange("p (k t) -> p k t", k=8), func=AF.Copy),
                    [pttk], [httk[b]])

        def compute3(t):
            ht, httk = hT[t % 2]
            gu, gutk = guT.next()
            for b in range(2):
                h_t, htk = hres.next()
                hrs[(t, b)] = (h_t, htk)
                r0 = t * T3 + b * 128
                sc.dma("sp", h_t[:], h1_d[r0:r0 + 128, :], htk, writes=[htk])
            for c in range(NFC):
                (pg, pgtk), (pu, putk) = ps_ring.next(), ps_ring.next()
                sc.op("pe", [lambda e, kc=kc, c=c, pg=pg: e.matmul(
                    pg[:, 0:T3], lhsT=wfg[:, kc, c * 128:(c + 1) * 128], rhs=ht[:, kc, :],
                    start=(kc == 0), stop=(kc == 7)) for kc in range(8)], httk, [pgtk])
                sc.op("pe", [lambda e, kc=kc, c=c, pu=pu: e.matmul(
                    pu[:, 0:T3], lhsT=wfu[:, kc, c * 128:(c + 1) * 128], rhs=ht[:, kc, :],
                    start=(kc == 0), stop=(kc == 7)) for kc in range(8)], httk, [putk])
                s_t, stk_ = sg.next()
                sc.op("act", lambda e, s_t=s_t, pg=pg: e.activation(out=s_t[:], in_=pg[:, 0:T3], func=AF.Silu),
                      [pgtk], [stk_])
                sc.op("dve", lambda e, s_t=s_t, pu=pu, c=c: e.tensor_tensor(out=gu[:, c, :], in0=pu[:, 0:T3],
                                                                            in1=s_t[:], op=ALU.mult),
                      [putk, stk_], [gutk])
            gus[t] = (gu, gutk)

        def compute3b(t):
            gu, gutk = gus.pop(t)
            for b in range(2):
                h_t, htk = hrs.pop((t, b))
                r0 = t * T3 + b * 128
                pr, prtk = pre.next()
                for n in range(2):
                    ps, pstk = ps_ring.next()
                    sc.op("pe", [lambda e, c=c, n=n, ps=ps, b=b: e.matmul(
                        ps[:, 0:512], lhsT=gu[:, c, b * 128:(b + 1) * 128], rhs=wfd[:, c, n * 512:(n + 1) * 512],
                        start=(c == 0), stop=(c == NFC - 1)) for c in range(NFC)], [gutk], [pstk])
                    sc.op("dve", lambda e, n=n, ps=ps, pr=pr, h_t=h_t: e.scalar_tensor_tensor(
                        out=pr[:, n * 512:(n + 1) * 512], in0=h_t[:, n * 512:(n + 1) * 512], scalar=ALPHA,
                        in1=ps[:, 0:512], op0=ALU.mult, op1=ALU.add), [pstk, htk], [prtk])
                o_t, otk = ost.next()
                sm, smtk = small.next()
                sm2, sm2tk = small2.next()
                layer_norm(pr[:], prtk, ln_g[:], ln_b[:], o_t[:], otk, sm, smtk, sm2, sm2tk)
                sc.dma(STQ, out_d[r0:r0 + 128, :], o_t[:], otk, reads=[otk])

        prep3(0)
        if NT3 > 1:
            prep3(1)
        for t in range(NT3):
            compute3(t)
            if t + 2 < NT3:
                prep3(t + 2)
            compute3b(t)
        sc.barrier()

    if 3 in phases:
        with contextlib.ExitStack() as st:
            phase3(st)

    sc.emit()
    return nc


def host_consts(S):
    half = 8
    inv = (500000.0 ** (-(np.arange(half, dtype=np.float32)) / np.float32(half))).astype(np.float32)
    ang = np.arange(S, dtype=np.float32)[:, None] * inv[None, :]
    cos = np.cos(ang).astype(np.float32)
    sin = np.sin(ang).astype(np.float32)
    csn = np.concatenate([cos, cos, -sin, sin], axis=1).astype(np.float32)
    kk = np.arange(128)[:, None]
    qq = np.arange(128)[None, :]
    m = np.zeros((128, 4, 128), np.float32)
    m[:, 0] = kk >= qq
    m[:, 1] = kk <= qq
    m[:, 2] = (kk >= qq) & (kk >= 64)
    m[:, 3] = (kk <= qq) & (kk < 64)
    ident = np.eye(128, dtype=np.float32)
    return csn, m, ident


def win_perm():
    cols = []
    for p in range(2):
        for m_ in range(4):
            for half in range(2):
                h = 4 * (2 * p + half) + m_
                cols.extend(range(h * 64, (h + 1) * 64))
    cols.extend(range(1024, 1280))
    cols.extend(range(1536, 2304))
    cols.extend(range(2304, 3072))
    cols.extend(range(1280, 1536))
    cols.extend(range(3072, 3840))
    return np.asarray(cols)


def make_in_maps(inputs, S, ncores):
    f = lambda a: np.ascontiguousarray(np.asarray(a, dtype=np.float32))
    csn, m, ident = host_consts(S)
    shared = {
        "w_in": f(np.asarray(inputs["w_in"])[0][:, win_perm()]),
        "w_gate": f(inputs["w_gate"][0]),
        "bg": f(np.asarray(inputs["b_gate"])[0].reshape(16, 128).T),
        "a_sink": f(np.asarray(inputs["a_sink"])[0].reshape(4, 2, 2).transpose(2, 0, 1)[:, None].repeat(64, axis=1).reshape(128, 8)),
        "w_br_a": f(inputs["w_br_a"][0]),
        "w_br_b": f(inputs["w_br_b"][0]),
        "w_out": f(inputs["w_out"][0]),
        "ln1_g": f(inputs["ln1_g"][0]),
        "ln1_b": f(inputs["ln1_b"][0]),
        "w_ff_gate": f(inputs["w_ff_gate"][0]),
        "w_ff_up": f(inputs["w_ff_up"][0]),
        "w_ff_down": f(inputs["w_ff_down"][0]),
        "ln2_g": f(inputs["ln2_g"][0]),
        "ln2_b": f(inputs["ln2_b"][0]),
        "csn": csn, "masks": m, "ident": ident,
    }
    x = np.asarray(inputs["x"])
    maps = []
    for c in range(ncores):
        d = dict(shared)
        d["x"] = f(x[c, :S])
        maps.append(d)
    return maps


_NC_CACHE = {}


def kernel(**inputs):
    S = 8192
    n = 8
    if S not in _NC_CACHE:
        _NC_CACHE[S] = build(S)
    nc = _NC_CACHE[S]
    in_maps = make_in_maps(inputs, S, n)
    res = run_bass_kernel_spmd(nc, in_maps, core_ids=list(range(n)))
    out = np.stack([np.asarray(r["out"], dtype=np.float32).reshape(S, D) for r in res.results], axis=0)
    return out
```

```python
import contextlib
import numpy as np
import concourse.bass as bass
import concourse.mybir as mybir
from concourse.bass_utils import run_bass_kernel_spmd

F32 = mybir.dt.float32
BF16 = mybir.dt.bfloat16
AF = mybir.ActivationFunctionType
ALU = mybir.AluOpType

D = 1024
HD = 64
IN_W = 3840
ROW = 4864
C_QA, C_KA, C_QB, C_KB, C_VA, C_VB = 0, 1024, 1280, 2048, 2816, 3328
DFF = 2816
NFC = DFF // 128
ALPHA = 2.0 ** 0.25
EPS = 1e-5
SB = 2048
ENG = ("pe", "act", "dve", "pool", "sp")
import os as _os0
STQ = _os0.environ.get("STQ", "pool")


class Tk:
    __slots__ = ("w", "r", "sem", "name")

    def __init__(self, name="", sem=None):
        self.w = None
        self.r = {}
        self.sem = sem
        self.name = name


class Sched:
    def __init__(self, nc):
        self.nc = nc
        self.ops = {e: [] for e in ENG}
        self.cnt = {e: 0 for e in ENG}
        self.waited = {e: {} for e in ENG}
        self.dma_tot = {}
        self.sems = {}
        for e in ("pe", "act", "dve", "pool"):
            self.sems[e] = nc.alloc_semaphore(name="sem_" + e)
        self.ndma = 0

    def dma_sem(self):
        k = "dma%d" % self.ndma
        self.ndma += 1
        self.sems[k] = self.nc.alloc_semaphore(name=k)
        self.dma_tot[k] = 0
        return k

    def _deps(self, eng, reads, writes, pe_accum, skip_same_w=False):
        deps = {}

        def add(ev):
            if ev is None:
                return
            s, v = ev
            if deps.get(s, 0) < v:
                deps[s] = v

        for t in reads:
            add(t.w)
        for t in writes:
            if not ((pe_accum and t.w is not None and t.w[0] == "pe") or
                    (skip_same_w and t.w is not None and t.w[0] == eng)):
                add(t.w)
            for s, v in t.r.items():
                add((s, v))
        waits = []
        wd = self.waited[eng]
        for s, v in deps.items():
            if s == eng and eng in ("pe", "sp"):
                continue
            if wd.get(s, 0) < v:
                wd[s] = v
                waits.append((s, v))
        return waits

    def _mark(self, ev, reads, writes):
        s, v = ev
        for t in reads:
            if t.r.get(s, 0) < v:
                t.r[s] = v
        for t in writes:
            t.w = ev
            t.r = {}

    def op(self, eng, fns, reads=(), writes=(), pe_accum=False, ssw=False):
        if not isinstance(fns, (list, tuple)):
            fns = [fns]
        waits = self._deps(eng, reads, writes, pe_accum, ssw)
        self.cnt[eng] += 1
        ev = (eng, self.cnt[eng])
        self.ops[eng].append((waits, list(fns), (eng, 1)))
        self._mark(ev, reads, writes)
        return ev

    def dma(self, q, out, in_, tile, reads=(), writes=()):
        if tile.sem is None:
            tile.sem = self.dma_sem()
        waits = self._deps(q, reads, writes, False)
        self.dma_tot[tile.sem] += 16
        ev = (tile.sem, self.dma_tot[tile.sem])
        self.ops[q].append((waits, [lambda e, o=out, i=in_: e.dma_start(out=o, in_=i)], (tile.sem, 16)))
        self._mark(ev, reads, writes)
        return ev

    def barrier(self):
        for e in ENG:
            waits = []
            wd = self.waited[e]
            for f in ("pe", "act", "dve", "pool"):
                if f != e and self.cnt[f] > wd.get(f, 0):
                    wd[f] = self.cnt[f]
                    waits.append((f, self.cnt[f]))
            for k, v in self.dma_tot.items():
                if v > wd.get(k, 0):
                    wd[k] = v
                    waits.append((k, v))
            self.ops[e].append((waits, [], None))

    def emit(self):
        nc = self.nc
        with nc.Block() as block:
            def run(name, eng):
                for waits, fns, inc in self.ops[name]:
                    for s, v in waits:
                        eng.wait_ge(self.sems[s], v)
                    ins = None
                    for f in fns:
                        ins = f(eng)
                    if inc is not None and ins is not None:
                        ins.then_inc(self.sems[inc[0]], inc[1])

            @block.tensor
            def _(e):
                run("pe", e)

            @block.scalar
            def _(e):
                run("act", e)

            @block.vector
            def _(e):
                run("dve", e)

            @block.gpsimd
            def _(e):
                run("pool", e)

            @block.sync
            def _(e):
                run("sp", e)


def sl(start, n, step):
    return slice(start, start + step * (n - 1) + 1, step)


class Ring:
    def __init__(self, items):
        self.items = items
        self.i = 0

    def next(self):
        it = self.items[self.i % len(self.items)]
        self.i += 1
        return it


def build(S=8192, dbg=False, phases=(1, 2, 3)):
    nc = bass.Bass("TRN2", target_bir_lowering=False)
    sc = Sched(nc)
    NB = S // 128

    def din(name, shape, dt=F32):
        return nc.dram_tensor(name, list(shape), dt, kind="ExternalInput").ap()

    x_d = din("x", [S, D])
    win_d = din("w_in", [D, IN_W])
    wgate_d = din("w_gate", [D, 2048])
    bg_d = din("bg", [128, 16])
    sink_d = din("a_sink", [128, 8])
    wbra_d = din("w_br_a", [1024, D])
    wbrb_d = din("w_br_b", [256, D])
    wout_d = din("w_out", [D, D])
    ln1g_d = din("ln1_g", [D])
    ln1b_d = din("ln1_b", [D])
    wffg_d = din("w_ff_gate", [D, DFF])
    wffu_d = din("w_ff_up", [D, DFF])
    wffd_d = din("w_ff_down", [DFF, D])
    ln2g_d = din("ln2_g", [D])
    ln2b_d = din("ln2_b", [D])
    csn_d = din("csn", [S, 32])
    masks_d = din("masks", [128, 4, 128])
    ident_d = din("ident", [128, 128])

    skind = "ExternalOutput" if dbg else "Internal"
    qkv_d = nc.dram_tensor("qkv_s", [S, ROW], BF16, kind=skind).ap()
    gT_d = nc.dram_tensor("gT_s", [16, 128, S], BF16, kind=skind).ap()
    h1_d = nc.dram_tensor("h1_s", [S, D], F32, kind=skind).ap()
    out_d = nc.dram_tensor("out", [S, D], F32, kind="ExternalOutput").ap()

    psb = [nc.alloc_psum_tensor("ps%d" % i, [128, 512], F32) for i in range(6)]
    ptb = [nc.alloc_psum_tensor("pt%d" % i, [128, 1024], BF16) for i in range(2)]
    ps_ring = Ring([(psb[i], Tk("ps%d" % i)) for i in range(6)])
    pt_ring = Ring([(ptb[i], Tk("pt%d" % i)) for i in range(2)])

    uid = [0]

    def sb(st, name, shape, dt):
        uid[0] += 1
        return st.enter_context(nc.sbuf_tensor("sb%d_%s" % (uid[0], name), list(shape), dt))

    rr = {"cast": 0}

    def cast_any(out, in_, reads, writes, engs=("dve", "act")):
        e = engs[rr["cast"] % len(engs)]
        rr["cast"] += 1
        if e == "act":
            sc.op("act", lambda g: g.activation(out=out, in_=in_, func=AF.Copy), reads, writes)
        else:
            sc.op(e, lambda g: g.tensor_copy(out=out, in_=in_), reads, writes)

    def load_weight(st, name, src, KC, N, wst):
        w = sb(st, name, [128, KC, N], BF16)
        tk = Tk(name)
        for kc in range(KC):
            for n0 in range(0, N, 1024):
                n1 = min(N, n0 + 1024)
                stg, stk = wst.next()
                sc.dma("sp", stg[:, 0:n1 - n0], src[kc * 128:(kc + 1) * 128, n0:n1], stk, writes=[stk])
                cast_any(w[:, kc, n0:n1], stg[:, 0:n1 - n0], [stk], [tk])
        return w, tk

    def consts(st, wst):
        ident = sb(st, "ident", [128, 128], BF16)
        masks = sb(st, "masks", [128, 4, 128], BF16)
        ones = sb(st, "ones", [128, 128], BF16)
        ctk = Tk("consts")
        stg, stk = wst.next()
        sc.dma("sp", stg[:, 0:128], ident_d, stk, writes=[stk])
        sc.op("dve", lambda g: g.tensor_copy(out=ident[:], in_=stg[:, 0:128]), [stk], [ctk])
        stg2, stk2 = wst.next()
        sc.dma("sp", stg2[:, 0:512], masks_d.rearrange("p a b -> p (a b)"), stk2, writes=[stk2])
        sc.op("dve", lambda g: g.tensor_copy(out=masks[:].rearrange("p a b -> p (a b)"), in_=stg2[:, 0:512]),
              [stk2], [ctk])
        sc.op("dve", lambda g: g.memset(ones[:], 1.0), [], [ctk])
        return ident, masks, ones, ctk

    def layer_norm_a(pre, ptk, small, smtk):
        stats = small[:, 0:12]
        mv = small[:, 12:14]
        rstd = small[:, 14:15]
        sc.op("dve", lambda g: g.bn_stats(out=small[:, 0:6], in_=pre[:, 0:512]), [ptk], [smtk])
        sc.op("dve", lambda g: g.bn_stats(out=small[:, 6:12], in_=pre[:, 512:1024]), [ptk], [smtk], ssw=True)
        sc.op("dve", lambda g: g.bn_aggr(out=mv, in_=stats), [smtk], [smtk])
        sc.op("dve", lambda g: g.tensor_scalar_add(out=rstd, in0=small[:, 13:14], scalar1=EPS), [smtk], [smtk])

    def layer_norm_b(pre, ptk, g_t, b_t, outt, otk, small, smtk, small2, sm2tk):
        rstd = small[:, 14:15]
        sc.op("dve", lambda g: g.scalar_tensor_tensor(out=pre, in0=pre, scalar=small[:, 12:13], in1=g_t,
                                                      op0=ALU.subtract, op1=ALU.mult), [ptk, smtk], [ptk])
        sc.op("act", lambda g: g.activation(out=small2[:, 0:1], in_=rstd, func=AF.Ln), [smtk], [sm2tk])
        sc.op("act", lambda g: g.activation(out=small2[:, 0:1], in_=small2[:, 0:1], func=AF.Exp, scale=-0.5),
              [sm2tk], [sm2tk])
        sc.op("dve", lambda g: g.scalar_tensor_tensor(out=outt, in0=pre, scalar=small2[:, 0:1], in1=b_t,
                                                      op0=ALU.mult, op1=ALU.add), [ptk, sm2tk], [otk])

    def layer_norm(pre, ptk, g_t, b_t, outt, otk, small, smtk, small2, sm2tk):
        layer_norm_a(pre, ptk, small, smtk)
        layer_norm_b(pre, ptk, g_t, b_t, outt, otk, small, smtk, small2, sm2tk)

    def bcast_load(st, name, src, wst_unused=None):
        t = sb(st, name, [128, D], F32)
        tk = Tk(name)
        sc.dma("sp", t[:], src.partition_broadcast(128), tk, writes=[tk])
        return t, tk

    def phase1(st):
      if True:
        wst = Ring([(sb(st, "wst%d" % i, [128, 1024], F32), Tk("wst%d" % i)) for i in range(3)])
        xin = Ring([(sb(st, "xin%d" % i, [128, D], F32), Tk("xin%d" % i)) for i in range(3)])
        wst = Ring(wst.items + [(t_, Tk("stg")) for t_, _ in xin.items])
        ident, masks, ones, ctk = consts(st, wst)
        w_bf, wtk = load_weight(st, "w_in_bf", win_d, 8, IN_W, wst)
        wg_bf, wgtk = load_weight(st, "w_g_bf", wgate_d, 8, 2048, wst)
        bg = sb(st, "bg", [128, 16], F32)
        bgtk = Tk("bg")
        sc.dma("sp", bg[:], bg_d, bgtk, writes=[bgtk])
        sc.barrier()

        xbf = Ring([(sb(st, "xbf%d" % i, [128, D], BF16), Tk("xbf%d" % i)) for i in range(2)])
        xT = [(sb(st, "xT%d" % i, [128, 8, 512], BF16), [Tk("xT%d_%d" % (i, b)) for b in range(4)]) for i in range(2)]
        cs = [(sb(st, "cs%d" % i, [128, 4, 32], F32), Tk("cs%d" % i)) for i in range(2)]
        stq = Ring([(sb(st, "stq%d" % i, [128, ROW], BF16), [Tk("stq%d_%d" % (i, n)) for n in range(8)])
                    for i in range(2)])
        rot = Ring([(sb(st, "rot%d" % i, [128, 44, 16], F32), Tk("rot%d" % i)) for i in range(2)])
        gst = Ring([(sb(st, "gst%d" % i, [128, 4, 512], BF16), Tk("gst%d" % i)) for i in range(2)])
        rA = Ring([(sb(st, "rA%d" % i, [128, 44, 16], F32), Tk("rA%d" % i)) for i in range(2)])
        rB = Ring([(sb(st, "rB%d" % i, [128, 44, 16], F32), Tk("rB%d" % i)) for i in range(2)])
        NT = S // 512

        def prep(t):
            xt, xtk = xT[t % 2]
            c_t, c_tk = cs[t % 2]
            sc.dma("sp", c_t[:], csn_d[t * 512:(t + 1) * 512, :].rearrange("(b p) c -> p b c", p=128), c_tk,
                   writes=[c_tk])
            for b in range(4):
                xi, xitk = xin.next()
                sc.dma("sp", xi[:], x_d[t * 512 + b * 128: t * 512 + (b + 1) * 128, :], xitk, writes=[xitk])
                xb, xbtk = xbf.next()
                sc.op("dve", lambda g, o=xb, i=xi: g.tensor_copy(out=o[:], in_=i[:]), [xitk], [xbtk])
                pt, pttk = pt_ring.next()
                sc.op("pe", [lambda g, o=pt, i=xb, kc=kc: g.transpose(out=o[:, kc * 128:(kc + 1) * 128],
                                                                       in_=i[:, kc * 128:(kc + 1) * 128],
                                                                       identity=ident[:])
                             for kc in range(8)], [xbtk], [pttk])
                sc.op("act", lambda g, o=xt, i=pt, b=b: g.activation(
                    out=o[:, :, b * 128:(b + 1) * 128], in_=i[:].rearrange("p (k t) -> p k t", k=8), func=AF.Copy),
                    [pttk], [xtk[b]])

        def v_chunk(ps, pstk, pcol0, nh, sq, sqtk, col0):
            pv = ps[:, pcol0:pcol0 + nh * 64].rearrange("p (h d) -> p h d", d=64).unsqueeze(2).broadcast_to(
                [128, nh, 2, 64])
            ov = sq[:, col0:col0 + nh * 128].rearrange("p (h r d) -> p h r d", r=2, d=64)
            sc.op("act", lambda g: g.activation(out=ov, in_=pv, func=AF.Copy), [pstk], [sqtk])

        def compute(t):
            xt, xtk = xT[t % 2]
            c_t, c_tk = cs[t % 2]
            for b in range(4):
                sq, sqtks = stq.next()
                rt_, rtk = rot.next()
                for n in range(8):
                    wdt = 512 if n < 7 else 256
                    ps, pstk = ps_ring.next()
                    sc.op("pe", [lambda g, o=ps, kc=kc, n=n, wdt=wdt, b=b: g.matmul(
                        o[:, 0:wdt], lhsT=xt[:, kc, b * 128:(b + 1) * 128], rhs=w_bf[:, kc, n * 512:n * 512 + wdt],
                        start=(kc == 0), stop=(kc == 7)) for kc in range(8)], [xtk[b]], [pstk])
                    nh = 8 if n <= 4 else (4 if n == 5 else 0)
                    if nh:
                        sc.op("act", lambda g, ps=ps, sq=sq, n=n, nh=nh: g.activation(
                            out=sq[:, n * 512:n * 512 + nh * 64], in_=ps[:, 0:nh * 64], func=AF.Copy),
                            [pstk], [sqtks[n]])
                        sc.op("act", lambda g, ps=ps, rt_=rt_, n=n, nh=nh: g.activation(
                            out=rt_[:, 8 * n:8 * n + nh, :],
                            in_=ps[:, 0:nh * 64].rearrange("p (h d) -> p h d", d=64)[:, :, 0:16], func=AF.Copy),
                            [pstk], [rtk])
                    if n == 5:
                        v_chunk(ps, pstk, 256, 4, sq, sqtks[n], C_VA)
                    elif n == 6:
                        v_chunk(ps, pstk, 0, 8, sq, sqtks[n], C_VB)
                    elif n == 7:
                        v_chunk(ps, pstk, 0, 4, sq, sqtks[n], C_VB + 1024)
                a_t, atk = rA.next()
                b_t, btk = rB.next()
                cc = c_t[:, b, 0:16].unsqueeze(1).broadcast_to([128, 44, 16])
                ns = c_t[:, b, 16:24].unsqueeze(1).broadcast_to([128, 44, 8])
                ps_ = c_t[:, b, 24:32].unsqueeze(1).broadcast_to([128, 44, 8])
                sc.op("dve", lambda g, a_t=a_t, rt_=rt_, cc=cc: g.tensor_tensor(
                    out=a_t[:], in0=rt_[:], in1=cc, op=ALU.mult), [rtk, c_tk], [atk])
                sc.op("pool", lambda g, b_t=b_t, rt_=rt_, ns=ns: g.tensor_tensor(
                    out=b_t[:, :, 0:8], in0=rt_[:, :, 8:16], in1=ns, op=ALU.mult), [rtk, c_tk], [btk])
                sc.op("pool", lambda g, b_t=b_t, rt_=rt_, ps_=ps_: g.tensor_tensor(
                    out=b_t[:, :, 8:16], in0=rt_[:, :, 0:8], in1=ps_, op=ALU.mult), [rtk, c_tk], [btk])
                ov = sq[:, 0:2816].rearrange("p (h d) -> p h d", d=64)[:, :, 0:16]
                sc.op("dve", lambda g, ov=ov, a_t=a_t, b_t=b_t: g.tensor_tensor(
                    out=ov, in0=a_t[:], in1=b_t[:], op=ALU.add), [atk, btk], sqtks[0:6])
                r0 = t * 512 + b * 128
                sc.dma(STQ, qkv_d[r0:r0 + 128, :], sq[:], sqtks[0], reads=sqtks)
            for gc in range(16):
                ps, pstk = ps_ring.next()
                sc.op("pe", [lambda g, o=ps, kc=kc, gc=gc: g.matmul(
                    o[:, 0:512], lhsT=wg_bf[:, kc, gc * 128:(gc + 1) * 128], rhs=xt[:, kc, :],
                    start=(kc == 0), stop=(kc == 7)) for kc in range(8)], xtk, [pstk])
                if gc % 4 == 0:
                    gs, gstk = gst.next()
                sc.op("act", lambda g, o=gs, i=ps, gc=gc: g.activation(
                    out=o[:, gc % 4, :], in_=i[:, 0:512], func=AF.Sigmoid, bias=bg[:, gc:gc + 1]), [pstk, bgtk], [gstk])
                if gc % 4 == 3:
                    c0 = gc - 3
                    sc.dma(STQ, gT_d[c0:c0 + 4, :, t * 512:(t + 1) * 512].rearrange("c p t -> p c t"), gs[:],
                           gstk, reads=[gstk])

        import os
        cut = int(os.environ.get("P1CUT", "9"))
        if cut >= 1:
            prep(0)
        for t in range(NT if cut >= 2 else 0):
            if t + 1 < NT:
                prep(t + 1)
            compute(t)
        sc.barrier()

    if 1 in phases:
        with contextlib.ExitStack() as st:
            phase1(st)

    def phase2(st):
      if True:
        wst = Ring([(sb(st, "wst%d" % i, [128, 1024], F32), Tk("wst%d" % i)) for i in range(2)])
        xr = Ring([(sb(st, "xr%d" % i, [128, D], F32), Tk("xr%d" % i)) for i in range(2)])
        pre = Ring([(sb(st, "pre%d" % i, [128, D], F32), Tk("pre%d" % i)) for i in range(2)])
        hst = Ring([(sb(st, "hst%d" % i, [128, D], F32), Tk("hst%d" % i)) for i in range(2)])
        wst = Ring(wst.items + [(t_, Tk("stg")) for t_, _ in xr.items + pre.items + hst.items])
        ident, masks, ones, ctk = consts(st, wst)
        wbra, _ = load_weight(st, "wbra", wbra_d, 8, D, wst)
        wbrb, _ = load_weight(st, "wbrb", wbrb_d, 2, D, wst)
        wout, _ = load_weight(st, "wout", wout_d, 8, D, wst)
        ln_g, lgtk = bcast_load(st, "ln1g", ln1g_d)
        ln_b, lbtk = bcast_load(st, "ln1b", ln1b_d)
        es = sb(st, "es", [128, 8], F32)
        estk = Tk("es")
        sc.dma("sp", es[:], sink_d, estk, writes=[estk])
        sc.op("act", lambda g: g.activation(out=es[:], in_=es[:], func=AF.Exp), [estk], [estk])
        mp = sb(st, "mp", [128, 4, 2, 128], BF16)
        for f in range(2):
            for l in range(2):
                i = f * 2 + l
                sc.op("dve", lambda g, i=i, f=f: g.tensor_copy(out=mp[:, i, 0, :], in_=masks[:, 2 if f else 0, :]),
                      [ctk], [ctk])
                sc.op("dve", lambda g, i=i, l=l: g.tensor_copy(out=mp[:, i, 1, :], in_=masks[:, 3 if l else 1, :]),
                      [ctk], [ctk])
        accU = sb(st, "accU", [128, 2, SB], F32)
        accL = sb(st, "accL", [128, 2, SB], F32)
        acctk = Tk("acc")
        obT = sb(st, "obT", [128, 2, SB], BF16)
        obtk = Tk("obT")
        NBS = 5
        qb = [(sb(st, "qb%d" % i, [128, 256], BF16), Tk("qb%d" % i)) for i in range(NBS)]
        kbt = [(sb(st, "kb%d" % i, [128, 2, 256], BF16), Tk("kb%d" % i)) for i in range(NBS)]
        vbt = [(sb(st, "vb%d" % i, [128, 2, 512], BF16), Tk("vb%d" % i)) for i in range(NBS)]
        for i in range(NBS):
            sc.op("pool", lambda g, i=i: g.memset(qb[i][0][:], 0.0), [], [qb[i][1]])
            sc.op("pool", lambda g, i=i: g.memset(kbt[i][0][:], 0.0), [], [kbt[i][1]])
            sc.op("pool", lambda g, i=i: g.memset(vbt[i][0][:], 0.0), [], [vbt[i][1]])
        qkT = Ring([(sb(st, "qkT%d" % i, [128, 6, 128], BF16), Tk("qkT%d" % i)) for i in range(3)])
        pTb = Ring([(sb(st, "pTb%d" % i, [128, 2, 2, 2, 128], BF16), Tk("pTb%d" % i)) for i in range(2)])
        qa = Ring([(sb(st, "qa%d" % i, [128, 1024], BF16), Tk("qa%d" % i)) for i in range(2)])
        ka = [(sb(st, "ka%d" % i, [128, 256], BF16), Tk("ka%d" % i)) for i in range(4)]
        va = [(sb(st, "va%d" % i, [128, 512], BF16), Tk("va%d" % i)) for i in range(4)]
        kaT = [(sb(st, "kaT%d" % i, [128, 2, 128], BF16), Tk("kaT%d" % i)) for i in range(4)]
        qaT = Ring([(sb(st, "qaT%d" % i, [128, 8, 128], BF16), Tk("qaT%d" % i)) for i in range(2)])
        pA = Ring([(sb(st, "pA%d" % i, [128, 512], BF16), Tk("pA%d" % i)) for i in range(6)])
        rtmp = Ring([(sb(st, "rtmp%d" % i, [128, 2, 128], F32), Tk("rtmp%d" % i)) for i in range(3)])
        T2 = 256
        oaT = Ring([(sb(st, "oaT%d" % i, [128, 8, T2], BF16), Tk("oaT%d" % i)) for i in range(2)])
        gt = Ring([(sb(st, "gt%d" % i, [128, 16, T2], BF16), Tk("gt%d" % i)) for i in range(2)])
        t1r = Ring([(sb(st, "t1_%d" % i, [128, T2], F32), Tk("t1_%d" % i)) for i in range(2)])
        t2r = Ring([(sb(st, "t2_%d" % i, [128, T2], F32), Tk("t2_%d" % i)) for i in range(2)])
        mixT = Ring([(sb(st, "mixT%d" % i, [128, 8, T2], BF16), Tk("mixT%d" % i)) for i in range(2)])
        small = Ring([(sb(st, "sm%d" % i, [128, 16], F32), Tk("sm%d" % i)) for i in range(2)])
        small2 = Ring([(sb(st, "smb%d" % i, [128, 2], F32), Tk("smb%d" % i)) for i in range(2)])
        sc.barrier()

        s_ring = Ring([ps_ring.items[i] for i in range(4)])
        ol_ring = Ring([ps_ring.items[i] for i in (4, 5)])
        bcount = [0]

        def b_T(u):
            sbi, g, d, r, J, j0 = u["args"]
            slot = bcount[0] % NBS
            bcount[0] += 1
            q_t, qtk = qb[slot]
            k_t, ktk = kbt[slot]
            v_t, vtk = vbt[slot]
            cq = C_QB + g * 256
            ck = C_KB + g * 256
            cv = C_VB + g * 512
            t0 = r + d * j0
            sc.dma("sp", q_t[:], qkv_d[sl(t0, 128, d), cq:cq + 256], qtk, writes=[qtk])
            first = last = 0
            for kb in range(2):
                jk0 = j0 - 64 + kb * 128
                lo = max(0, -jk0)
                hi = min(128, J - jk0)
                if kb == 0 and lo > 0:
                    first = 1
                if kb == 1 and hi < 128:
                    last = 1
                ts = r + d * (jk0 + lo)
                sc.dma("sp", k_t[lo:hi, kb, :], qkv_d[sl(ts, hi - lo, d), ck:ck + 256], ktk, writes=[ktk])
                sc.dma("sp", v_t[lo:hi, kb, :], qkv_d[sl(ts, hi - lo, d), cv:cv + 512], vtk, writes=[vtk])
            pt, pttk = pt_ring.next()
            fns = []
            for c in range(2):
                fns.append(lambda e, c=c: e.transpose(out=pt[:, c * 128:(c + 1) * 128],
                                                      in_=q_t[:, c * 128:(c + 1) * 128], identity=ident[:]))
            for kb in range(2):
                for c in range(2):
                    i = 2 + kb * 2 + c
                    fns.append(lambda e, c=c, kb=kb, i=i: e.transpose(out=pt[:, i * 128:(i + 1) * 128],
                                                                      in_=k_t[:, kb, c * 128:(c + 1) * 128],
                                                                      identity=ident[:]))
            sc.op("pe", fns, [qtk, ktk], [pttk])
            qk, qktk = qkT.next()
            sc.op("act", lambda e: e.activation(out=qk[:].rearrange("p a b -> p (a b)"), in_=pt[:, 0:768],
                                                func=AF.Copy), [pttk], [qktk])
            u.update(qk=qk, qktk=qktk, v_t=v_t, vtk=vtk, first=first, last=last, t0=t0)

        def b_S(u):
            qk, qktk = u["qk"], u["qktk"]
            sbank = [s_ring.next(), s_ring.next()]
            for half in range(2):
                ps, pstk = sbank[half]
                fns = []
                for c in range(2):
                    for kb in range(2):
                        fns.append(lambda e, c=c, kb=kb, half=half, ps=ps: e.matmul(
                            ps[:, (c * 2 + kb) * 128:(c * 2 + kb + 1) * 128],
                            lhsT=qk[half * 64:(half + 1) * 64, 2 + kb * 2 + c, :],
                            rhs=qk[half * 64:(half + 1) * 64, c, :], start=True, stop=True))
                sc.op("pe", fns, [qktk], [pstk])
            p_t, ptk_ = pTb.next()
            for half in range(2):
                ps, pstk = sbank[half]
                sc.op("act", lambda e, half=half, ps=ps: e.activation(
                    out=p_t[:, half].rearrange("p c k q -> p (c k q)"), in_=ps[:, 0:512], func=AF.Exp, scale=0.125),
                    [pstk], [ptk_], ssw=True)
            pv4 = p_t[:].rearrange("p h c k q -> p (h c) k q")
            mview = mp[:, u["first"] * 2 + u["last"]].unsqueeze(1).broadcast_to([128, 4, 2, 128])
            sc.op("dve", lambda e: e.tensor_tensor(out=pv4, in0=pv4, in1=mview, op=ALU.mult), [ptk_, ctk], [ptk_])
            u.update(p_t=p_t, ptk_=ptk_)

        def b_PV(u):
            sbi, g, d, r, J, j0 = u["args"]
            p_t, ptk_, v_t, vtk = u["p_t"], u["ptk_"], u["v_t"], u["vtk"]
            ol, oltk = ol_ring.next()
            fns = []
            for c in range(2):
                for typ in range(2):
                    for kb in range(2):
                        for half in range(2):
                            h = 2 * c + half
                            hs = slice(half * 64, (half + 1) * 64)
                            col = typ * 256 + c * 128
                            if typ == 0:
                                fns.append(lambda e, hs=hs, col=col, h=h, half=half, c=c, kb=kb: e.matmul(
                                    ol[hs, col:col + 128], lhsT=v_t[:, kb, h * 128:h * 128 + 64],
                                    rhs=p_t[:, half, c, kb, :], start=(kb == 0), stop=(kb == 1)))
                            else:
                                fns.append(lambda e, hs=hs, col=col, half=half, c=c, kb=kb: e.matmul(
                                    ol[hs, col:col + 128], lhsT=ones[:, 0:64],
                                    rhs=p_t[:, half, c, kb, :], start=(kb == 0), stop=(kb == 1)))
            sc.op("pe", fns, [ptk_, vtk, ctk], [oltk])
            a0 = u["t0"] - sbi * SB
            for typ, acc in ((0, accU), (1, accL)):
                av = acc[:, :, sl(a0, 128, d)]
                pv = ol[:, typ * 256:(typ + 1) * 256].rearrange("p (c q) -> p c q", c=2)
                sc.op("dve", lambda e, av=av, pv=pv: e.tensor_tensor(out=av, in0=av, in1=pv, op=ALU.add),
                      [acctk, oltk], [acctk], ssw=True)

        def mixer_b(sbi):
            sc.op("pool", lambda e: e.memset(accU[:], 0.0), [], [acctk])
            sc.op("pool", lambda e: e.memset(accL[:], 0.0), [], [acctk])
            units = []
            for g, d in enumerate((1, 4, 16)):
                J = S // d
                nj = SB // d // 128
                for r in range(d):
                    for jb in range(nj):
                        units.append({"args": (sbi, g, d, r, J, sbi * (SB // d) + jb * 128)})
            n = len(units)
            b_T(units[0])
            b_T(units[1])
            b_S(units[0])
            for i in range(n):
                if i + 2 < n:
                    b_T(units[i + 2])
                if i + 1 < n:
                    b_S(units[i + 1])
                b_PV(units[i])
            sc.op("act", lambda e: e.activation(out=accL[:], in_=accL[:], func=AF.Ln), [acctk], [acctk])
            sc.op("act", lambda e: e.activation(out=accL[:], in_=accL[:], func=AF.Exp, scale=-1.0), [acctk], [acctk])
            sc.op("dve", lambda e: e.tensor_tensor(out=obT[:], in0=accU[:], in1=accL[:], op=ALU.mult),
                  [acctk], [obtk])

        def a_load_kv(blk):
            k_t, ktk = ka[blk % 4]
            v_t, vtk = va[blk % 4]
            sc.dma("sp", k_t[:], qkv_d[blk * 128:(blk + 1) * 128, C_KA:C_KA + 256], ktk, writes=[ktk])
            sc.dma("sp", v_t[:], qkv_d[blk * 128:(blk + 1) * 128, C_VA:C_VA + 512], vtk, writes=[vtk])
            pt, pttk = pt_ring.next()
            sc.op("pe", [lambda e, c=c: e.transpose(out=pt[:, c * 128:(c + 1) * 128], in_=k_t[:, c * 128:(c + 1) * 128],
                                                    identity=ident[:]) for c in range(2)], [ktk], [pttk])
            kt_t, kttk = kaT[blk % 4]
            sc.op("act", lambda e: e.activation(out=kt_t[:].rearrange("p a b -> p (a b)"), in_=pt[:, 0:256],
                                                func=AF.Copy), [pttk], [kttk])

        blkctx = {}

        def a_prep(i):
            if i == 0:
                a_load_kv(0)
            if i + 1 < NB:
                a_load_kv(i + 1)
            q_t, qtk = qa.next()
            sc.dma("sp", q_t[:], qkv_d[i * 128:(i + 1) * 128, 0:1024], qtk, writes=[qtk])
            pt, pttk = pt_ring.next()
            sc.op("pe", [lambda e, c=c: e.transpose(out=pt[:, c * 128:(c + 1) * 128], in_=q_t[:, c * 128:(c + 1) * 128],
                                                    identity=ident[:]) for c in range(8)], [qtk], [pttk])
            qT, qTtk = qaT.next()
            sc.op("act", lambda e: e.activation(out=qT[:].rearrange("p a b -> p (a b)"), in_=pt[:, 0:1024],
                                                func=AF.Copy), [pttk], [qTtk])
            blkctx[i] = (qT, qTtk)

        def a_S(u):
            i, k = u["i"], u["k"]
            qT, qTtk = blkctx[i]
            kbs = [kb for kb in range(3) if 0 <= i - 1 + kb < NB]
            p, half = k // 2, k % 2
            hs = slice(half * 64, (half + 1) * 64)
            plist = []
            for kb in kbs:
                blk = i - 1 + kb
                ps, pstk = s_ring.next()
                kt_t, kttk = kaT[blk % 4]
                sc.op("pe", lambda e, ps=ps, kt_t=kt_t: e.matmul(
                    ps[:, 0:512], lhsT=kt_t[hs, p, :], rhs=qT[hs, p * 4:(p + 1) * 4, :], start=True, stop=True),
                    [kttk, qTtk], [pstk])
                pp, pptk = pA.next()
                sc.op("act", lambda e, ps=ps, pp=pp: e.activation(out=pp[:], in_=ps[:, 0:512], func=AF.Exp,
                                                                  scale=0.125), [pstk], [pptk])
                if kb != 1:
                    mv = masks[:, 0 if kb == 0 else 1, :].unsqueeze(1).broadcast_to([128, 4, 128])
                    ppv = pp[:].rearrange("p (m q) -> p m q", m=4)
                    sc.op("dve",
                          lambda e, ppv=ppv, mv=mv: e.tensor_tensor(out=ppv, in0=ppv, in1=mv, op=ALU.mult),
                          [pptk, ctk], [pptk])
                plist.append((pp, pptk, blk))
            u["plist"] = plist

        def a_PV(u):
            i, k, oa, oatk, bi = u["i"], u["k"], u["oa"], u["oatk"], u["bi"]
            plist = u["plist"]
            ol, oltk = ol_ring.next()
            n = len(plist)
            fns = []
            for typ in range(2):
                for j, (pp, pptk, blk) in enumerate(plist):
                    ppv = pp[:].rearrange("p (m q) -> p m q", m=4)
                    for hf in range(2):
                        hs = slice(hf * 64, (hf + 1) * 64)
                        if typ == 0:
                            fns.append(lambda e, j=j, ppv=ppv, blk=blk, hf=hf, hs=hs: e.matmul(
                                ol[hs, 0:256], lhsT=va[blk % 4][0][:, k * 128:k * 128 + 64], rhs=ppv[:, hf::2, :],
                                start=(j == 0), stop=(j == n - 1)))
                        else:
                            fns.append(lambda e, j=j, ppv=ppv, hf=hf, hs=hs: e.matmul(
                                ol[hs, 256:512], lhsT=ones[:, 0:64], rhs=ppv[:, hf::2, :],
                                start=(j == 0), stop=(j == n - 1)))
            sc.op("pe", fns, [x[1] for x in plist] + [va[x[2] % 4][1] for x in plist] + [ctk], [oltk])
            rt, rttk = rtmp.next()
            esv = es[:, 2 * k:2 * k + 2].unsqueeze(2).broadcast_to([128, 2, 128])
            lv = ol[:, 256:512].rearrange("p (m q) -> p m q", m=2)
            ov = ol[:, 0:256].rearrange("p (m q) -> p m q", m=2)
            sc.op("dve", lambda e: e.tensor_tensor(out=rt[:], in0=lv, in1=esv, op=ALU.add), [oltk, estk], [rttk])
            sc.op("act", lambda e: e.activation(out=rt[:], in_=rt[:], func=AF.Ln), [rttk], [rttk])
            sc.op("act", lambda e: e.activation(out=rt[:], in_=rt[:], func=AF.Exp, scale=-1.0), [rttk], [rttk])
            sc.op("dve", lambda e: e.tensor_tensor(
                out=oa[:, 2 * k:2 * k + 2, bi * 128:(bi + 1) * 128], in0=ov, in1=rt[:], op=ALU.mult),
                [oltk, rttk], [oatk], ssw=True)

        def dense2(tt, oa, oatk, sbi):
            tok0 = tt * T2
            g_t, gtk = gt.next()
            sc.dma("sp", g_t[:], gT_d[:, :, tok0:tok0 + T2].rearrange("c p t -> p c t"), gtk, writes=[gtk])
            mx, mxtk = mixT.next()
            so = tok0 - sbi * SB
            for c in range(8):
                (pa, patk), (pb, pbtk) = s_ring.next(), s_ring.next()
                sc.op("pe", [lambda e, kc=kc, c=c, pa=pa: e.matmul(
                    pa[:, 0:T2], lhsT=wbra[:, kc, c * 128:(c + 1) * 128], rhs=oa[:, kc, :],
                    start=(kc == 0), stop=(kc == 7)) for kc in range(8)], [oatk], [patk])
                sc.op("pe", [lambda e, kc=kc, c=c, pb=pb: e.matmul(
                    pb[:, 0:T2], lhsT=wbrb[:, kc, c * 128:(c + 1) * 128], rhs=obT[:, kc, so:so + T2],
                    start=(kc == 0), stop=(kc == 1)) for kc in range(2)], [obtk], [pbtk])
                t1, t1tk = t1r.next()
                t2, t2tk = t2r.next()
                sc.op("dve", lambda e, t1=t1, pa=pa, c=c: e.tensor_tensor(out=t1[:], in0=pa[:, 0:T2],
                                                                          in1=g_t[:, c, :], op=ALU.mult),
                      [patk, gtk], [t1tk])
                sc.op("dve", lambda e, t2=t2, pb=pb, c=c: e.tensor_tensor(out=t2[:], in0=pb[:, 0:T2],
                                                                          in1=g_t[:, 8 + c, :], op=ALU.mult),
                      [pbtk, gtk], [t2tk])
                sc.op("dve", lambda e, t1=t1, t2=t2, c=c: e.tensor_tensor(out=mx[:, c, :], in0=t1[:], in1=t2[:],
                                                                           op=ALU.add), [t1tk, t2tk], [mxtk], ssw=True)
            blks = []
            for bi in range(T2 // 128):
                r0 = tok0 + bi * 128
                x_t, xtk_ = xr.next()
                sc.dma("sp", x_t[:], x_d[r0:r0 + 128, :], xtk_, writes=[xtk_])
                pr, prtk = pre.next()
                for n in range(2):
                    ps, pstk = s_ring.next()
                    sc.op("pe", [lambda e, kc=kc, n=n, ps=ps, bi=bi: e.matmul(
                        ps[:, 0:512], lhsT=mx[:, kc, bi * 128:(bi + 1) * 128], rhs=wout[:, kc, n * 512:(n + 1) * 512],
                        start=(kc == 0), stop=(kc == 7)) for kc in range(8)], [mxtk], [pstk])
                    sc.op("dve", lambda e, n=n, ps=ps, pr=pr, x_t=x_t: e.scalar_tensor_tensor(
                        out=pr[:, n * 512:(n + 1) * 512], in0=x_t[:, n * 512:(n + 1) * 512], scalar=ALPHA,
                        in1=ps[:, 0:512], op0=ALU.mult, op1=ALU.add), [pstk, xtk_], [prtk], ssw=True)
                blks.append((r0, pr, prtk))
            for r0, pr, prtk in blks:
                sm, smtk = small.next()
                layer_norm_a(pr[:], prtk, sm, smtk)
                ln_pending.append((r0, pr, prtk, sm, smtk))

        ln_pending = []

        def flush_ln():
            while ln_pending:
                r0, pr, prtk, sm, smtk = ln_pending.pop(0)
                hs_, hstk = hst.next()
                sm2, sm2tk = small2.next()
                layer_norm_b(pr[:], prtk, ln_g[:], ln_b[:], hs_[:], hstk, sm, smtk, sm2, sm2tk)
                sc.dma(STQ, h1_d[r0:r0 + 128, :], hs_[:], hstk, reads=[hstk])

        NBT = T2 // 128
        for sbi in range(S // SB):
            mixer_b(sbi)
            units = []
            for tt in range(sbi * (SB // T2), (sbi + 1) * (SB // T2)):
                oa, oatk = oaT.next()
                for bi in range(NBT):
                    for k in range(4):
                        units.append({"i": tt * NBT + bi, "k": k, "bi": bi, "tt": tt, "oa": oa, "oatk": oatk})
            n = len(units)
            a_prep(units[0]["i"])
            a_S(units[0])
            pending = None
            for j in range(n):
                u = units[j]
                if u["k"] == 1 and u["i"] + 1 < (sbi + 1) * (SB // 128):
                    a_prep(u["i"] + 1)
                if j + 1 < n:
                    a_S(units[j + 1])
                a_PV(u)
                if ln_pending and u["k"] == 1:
                    flush_ln()
                if pending is not None:
                    dense2(*pending)
                    pending = None
                if u["k"] == 3 and u["bi"] == NBT - 1:
                    pending = (u["tt"], u["oa"], u["oatk"], sbi)
            if pending is not None:
                dense2(*pending)
            flush_ln()
        sc.barrier()

    if 2 in phases:
        with contextlib.ExitStack() as st:
            phase2(st)

    def phase3(st):
      if True:
        wst = Ring([(sb(st, "wst%d" % i, [128, 1024], F32), Tk("wst%d" % i)) for i in range(2)])
        hc = Ring([(sb(st, "hc%d" % i, [128, D], F32), Tk("hc%d" % i)) for i in range(2)])
        hres = Ring([(sb(st, "hres%d" % i, [128, D], F32), Tk("hres%d" % i)) for i in range(2)])
        pre = Ring([(sb(st, "pre3_%d" % i, [128, D], F32), Tk("pre3_%d" % i)) for i in range(2)])
        ost = Ring([(sb(st, "ost%d" % i, [128, D], F32), Tk("ost%d" % i)) for i in range(2)])
        wst = Ring(wst.items + [(t_, Tk("stg")) for t_, _ in hc.items + hres.items + pre.items + ost.items])
        ident, masks, ones, ctk = consts(st, wst)
        wfg, _ = load_weight(st, "wfg", wffg_d, 8, DFF, wst)
        wfu, _ = load_weight(st, "wfu", wffu_d, 8, DFF, wst)
        wfd, _ = load_weight(st, "wfd", wffd_d, NFC, D, wst)
        ln_g, lgtk = bcast_load(st, "ln2g", ln2g_d)
        ln_b, lbtk = bcast_load(st, "ln2b", ln2b_d)
        T3 = 256
        hbf = Ring([(sb(st, "hbf%d" % i, [128, D], BF16), Tk("hbf%d" % i)) for i in range(2)])
        hT = [(sb(st, "hT%d" % i, [128, 8, T3], BF16), [Tk("hT%d_%d" % (i, b)) for b in range(2)]) for i in range(2)]
        sg = Ring([(sb(st, "sg%d" % i, [128, T3], F32), Tk("sg%d" % i)) for i in range(2)])
        guT = Ring([(sb(st, "guT%d" % i, [128, NFC, T3], BF16), Tk("guT%d" % i)) for i in range(1)])
        small = Ring([(sb(st, "sm3_%d" % i, [128, 16], F32), Tk("sm3_%d" % i)) for i in range(2)])
        small2 = Ring([(sb(st, "smb3_%d" % i, [128, 2], F32), Tk("smb3_%d" % i)) for i in range(2)])
        sc.barrier()
        NT3 = S // T3
        hrs = {}
        gus = {}

        def prep3(t):
            ht, httk = hT[t % 2]
            for b in range(2):
                h_t, htk = hc.next()
                r0 = t * T3 + b * 128
                sc.dma("sp", h_t[:], h1_d[r0:r0 + 128, :], htk, writes=[htk])
                hb, hbtk = hbf.next()
                sc.op("dve", lambda e, hb=hb, h_t=h_t: e.tensor_copy(out=hb[:], in_=h_t[:]), [htk], [hbtk])
                pt, pttk = pt_ring.next()
                sc.op("pe", [lambda e, kc=kc, pt=pt, hb=hb: e.transpose(out=pt[:, kc * 128:(kc + 1) * 128],
                                                                        in_=hb[:, kc * 128:(kc + 1) * 128],
                                                                        identity=ident[:]) for kc in range(8)],
                      [hbtk], [pttk])
                sc.op("act", lambda e, pt=pt, b=b: e.activation(
                    out=ht[:, :, b * 128:(b + 1) * 128], in_=pt[:].rearrange("p (k t) -> p k t", k=8), func=AF.Copy),
                    [pttk], [httk[b]])

        def compute3(t):
            ht, httk = hT[t % 2]
            gu, gutk = guT.next()
            for b in range(2):
                h_t, htk = hres.next()
                hrs[(t, b)] = (h_t, htk)
                r0 = t * T3 + b * 128
                sc.dma("sp", h_t[:], h1_d[r0:r0 + 128, :], htk, writes=[htk])
            for c in range(NFC):
                (pg, pgtk), (pu, putk) = ps_ring.next(), ps_ring.next()
                sc.op("pe", [lambda e, kc=kc, c=c, pg=pg: e.matmul(
                    pg[:, 0:T3], lhsT=wfg[:, kc, c * 128:(c + 1) * 128], rhs=ht[:, kc, :],
                    start=(kc == 0), stop=(kc == 7)) for kc in range(8)], httk, [pgtk])
                sc.op("pe", [lambda e, kc=kc, c=c, pu=pu: e.matmul(
                    pu[:, 0:T3], lhsT=wfu[:, kc, c * 128:(c + 1) * 128], rhs=ht[:, kc, :],
                    start=(kc == 0), stop=(kc == 7)) for kc in range(8)], httk, [putk])
                s_t, stk_ = sg.next()
                sc.op("act", lambda e, s_t=s_t, pg=pg: e.activation(out=s_t[:], in_=pg[:, 0:T3], func=AF.Silu),
                      [pgtk], [stk_])
                sc.op("dve", lambda e, s_t=s_t, pu=pu, c=c: e.tensor_tensor(out=gu[:, c, :], in0=pu[:, 0:T3],
                                                                            in1=s_t[:], op=ALU.mult),
                      [putk, stk_], [gutk])
            gus[t] = (gu, gutk)

        def compute3b(t):
            gu, gutk = gus.pop(t)
            for b in range(2):
                h_t, htk = hrs.pop((t, b))
                r0 = t * T3 + b * 128
                pr, prtk = pre.next()
                for n in range(2):
                    ps, pstk = ps_ring.next()
                    sc.op("pe", [lambda e, c=c, n=n, ps=ps, b=b: e.matmul(
                        ps[:, 0:512], lhsT=gu[:, c, b * 128:(b + 1) * 128], rhs=wfd[:, c, n * 512:(n + 1) * 512],
                        start=(c == 0), stop=(c == NFC - 1)) for c in range(NFC)], [gutk], [pstk])
                    sc.op("dve", lambda e, n=n, ps=ps, pr=pr, h_t=h_t: e.scalar_tensor_tensor(
                        out=pr[:, n * 512:(n + 1) * 512], in0=h_t[:, n * 512:(n + 1) * 512], scalar=ALPHA,
                        in1=ps[:, 0:512], op0=ALU.mult, op1=ALU.add), [pstk, htk], [prtk])
                o_t, otk = ost.next()
                sm, smtk = small.next()
                sm2, sm2tk = small2.next()
                layer_norm(pr[:], prtk, ln_g[:], ln_b[:], o_t[:], otk, sm, smtk, sm2, sm2tk)
                sc.dma(STQ, out_d[r0:r0 + 128, :], o_t[:], otk, reads=[otk])

        prep3(0)
        if NT3 > 1:
            prep3(1)
        for t in range(NT3):
            compute3(t)
            if t + 2 < NT3:
                prep3(t + 2)
            compute3b(t)
        sc.barrier()

    if 3 in phases:
        with contextlib.ExitStack() as st:
            phase3(st)

    sc.emit()
    return nc


def host_consts(S):
    half = 8
    inv = (500000.0 ** (-(np.arange(half, dtype=np.float32)) / np.float32(half))).astype(np.float32)
    ang = np.arange(S, dtype=np.float32)[:, None] * inv[None, :]
    cos = np.cos(ang).astype(np.float32)
    sin = np.sin(ang).astype(np.float32)
    csn = np.concatenate([cos, cos, -sin, sin], axis=1).astype(np.float32)
    kk = np.arange(128)[:, None]
    qq = np.arange(128)[None, :]
    m = np.zeros((128, 4, 128), np.float32)
    m[:, 0] = kk >= qq
    m[:, 1] = kk <= qq
    m[:, 2] = (kk >= qq) & (kk >= 64)
    m[:, 3] = (kk <= qq) & (kk < 64)
    ident = np.eye(128, dtype=np.float32)
    return csn, m, ident


def win_perm():
    cols = []
    for p in range(2):
        for m_ in range(4):
            for half in range(2):
                h = 4 * (2 * p + half) + m_
                cols.extend(range(h * 64, (h + 1) * 64))
    cols.extend(range(1024, 1280))
    cols.extend(range(1536, 2304))
    cols.extend(range(2304, 3072))
    cols.extend(range(1280, 1536))
    cols.extend(range(3072, 3840))
    return np.asarray(cols)


def make_in_maps(inputs, S, ncores):
    f = lambda a: np.ascontiguousarray(np.asarray(a, dtype=np.float32))
    csn, m, ident = host_consts(S)
    shared = {
        "w_in": f(np.asarray(inputs["w_in"])[0][:, win_perm()]),
        "w_gate": f(inputs["w_gate"][0]),
        "bg": f(np.asarray(inputs["b_gate"])[0].reshape(16, 128).T),
        "a_sink": f(np.asarray(inputs["a_sink"])[0].reshape(4, 2, 2).transpose(2, 0, 1)[:, None].repeat(64, axis=1).reshape(128, 8)),
        "w_br_a": f(inputs["w_br_a"][0]),
        "w_br_b": f(inputs["w_br_b"][0]),
        "w_out": f(inputs["w_out"][0]),
        "ln1_g": f(inputs["ln1_g"][0]),
        "ln1_b": f(inputs["ln1_b"][0]),
        "w_ff_gate": f(inputs["w_ff_gate"][0]),
        "w_ff_up": f(inputs["w_ff_up"][0]),
        "w_ff_down": f(inputs["w_ff_down"][0]),
        "ln2_g": f(inputs["ln2_g"][0]),
        "ln2_b": f(inputs["ln2_b"][0]),
        "csn": csn, "masks": m, "ident": ident,
    }
    x = np.asarray(inputs["x"])
    maps = []
    for c in range(ncores):
        d = dict(shared)
        d["x"] = f(x[c, :S])
        maps.append(d)
    return maps


_NC_CACHE = {}


def kernel(**inputs):
    S = 8192
    n = 8
    if S not in _NC_CACHE:
        _NC_CACHE[S] = build(S)
    nc = _NC_CACHE[S]
    in_maps = make_in_maps(inputs, S, n)
    res = run_bass_kernel_spmd(nc, in_maps, core_ids=list(range(n)))
    out = np.stack([np.asarray(r["out"], dtype=np.float32).reshape(S, D) for r in res.results], axis=0)
    return out
```
